# Optimizing a Trainium2 kernel written in Bass

```python
import math
import jax, jax.numpy as jnp
from jax import lax
import numpy as np

D_MODEL = 2048
BATCH = 4
SEQ = 4096
DEPTH = 4

N_A_LAYERS = DEPTH // 2
N_B_LAYERS = DEPTH - N_A_LAYERS
N_META = 16
D_FF = 4 * D_MODEL
GDN_HEAD_DIM = 128
GDN_QK_HEADS = D_MODEL // 128
GDN_V_HEADS = 2 * GDN_QK_HEADS
GDN_QK_DIM = GDN_QK_HEADS * GDN_HEAD_DIM
GDN_V_DIM = GDN_V_HEADS * GDN_HEAD_DIM
GDN_CONV_DIM = 2 * GDN_QK_DIM + GDN_V_DIM
GDN_IN_DIM = GDN_CONV_DIM + GDN_V_DIM + 2 * GDN_V_HEADS
GDN_CONV_K = 4
CHUNK = 64
DIFF_HEADS = D_MODEL // 256
DIFF_QK_DIM = 128
DIFF_V_DIM = 2 * DIFF_QK_DIM
DIFF_Q_DIM = DIFF_HEADS * 2 * DIFF_QK_DIM
DIFF_KV_DIM = DIFF_Q_DIM + DIFF_HEADS * DIFF_V_DIM
ROT_DIM = DIFF_QK_DIM // 4
ROPE_THETA = 500000.0
Q_BLOCK = 128
EPS = 1e-6

kernel_name = 'yoco_gdn_diffattn_hybrid'


def rms_norm(x, gain):
    x32 = x.astype(jnp.float32)
    y = x32 * lax.rsqrt(jnp.mean(x32 * x32, axis=-1, keepdims=True) + EPS)
    return y.astype(x.dtype) * gain.astype(x.dtype)


def l2_normalize(x):
    x32 = x.astype(jnp.float32)
    return (x32 * lax.rsqrt(jnp.sum(x32 * x32, axis=-1, keepdims=True) + EPS)).astype(x.dtype)


def causal_depthwise_conv(x, w):
    return lax.conv_general_dilated(
        x, w[:, None, :].astype(x.dtype), window_strides=(1,), padding=[(GDN_CONV_K - 1, 0)],
        dimension_numbers=('NWC', 'WIO', 'NWC'), feature_group_count=x.shape[-1])


def partial_rope(x, cos, sin):
    half = ROT_DIM // 2
    shape = (1, x.shape[1]) + (1,) * (x.ndim - 3) + (half,)
    c = cos.reshape(shape).astype(x.dtype)
    s = sin.reshape(shape).astype(x.dtype)
    x1, x2 = x[..., :half], x[..., half:ROT_DIM]
    return jnp.concatenate([x1 * c - x2 * s, x2 * c + x1 * s, x[..., ROT_DIM:]], axis=-1)


def chunk_gated_delta_rule(q, k, v, beta, g):
    B, T, H, DK = q.shape
    DV = v.shape[-1]
    N = T // CHUNK

    def chunks(t):
        t = t.reshape((B, N, CHUNK, H) + t.shape[3:])
        return jnp.moveaxis(t, (1, 3), (0, 2))

    tri_incl = jnp.tril(jnp.ones((CHUNK, CHUNK), dtype=bool))
    tri_strict = jnp.tril(jnp.ones((CHUNK, CHUNK), dtype=bool), -1)

    def step(S, xs):
        qc, kc, vc, bc, gc = xs
        G = jnp.cumsum(gc, axis=-1)
        decay = jnp.exp(jnp.where(tri_incl, G[..., :, None] - G[..., None, :], -jnp.inf))
        kb = kc * bc[..., None]
        a_kk = jnp.where(tri_strict, jnp.einsum('bhik,bhjk->bhij', kb, kc) * decay, 0.0)
        rhs = jnp.concatenate([vc * bc[..., None], kb * jnp.exp(G)[..., None]], axis=-1)
        sol = lax.linalg.triangular_solve(a_kk, rhs, left_side=True, lower=True, unit_diagonal=True)
        u, w = sol[..., :DV], sol[..., DV:]
        v_new = u - jnp.einsum('bhik,bhkv->bhiv', w, S)
        a_qk = jnp.einsum('bhik,bhjk->bhij', qc, kc) * decay
        o = (jnp.einsum('bhik,bhkv->bhiv', qc * jnp.exp(G)[..., None], S)
             + jnp.einsum('bhij,bhjv->bhiv', a_qk, v_new))
        g_last = G[..., -1:]
        S = (S * jnp.exp(g_last)[..., None]
             + jnp.einsum('bhik,bhiv->bhkv', kc * jnp.exp(g_last - G)[..., None], v_new))
        return S, o

    S0 = jnp.zeros((B, H, DK, DV), jnp.float32)
    _, o = lax.scan(step, S0, (chunks(q), chunks(k), chunks(v), chunks(beta), chunks(g)))
    return jnp.moveaxis(o, (0, 2), (1, 3)).reshape(B, T, H, DV)


def gated_deltanet(h, w_in, conv_w, a_log, dt_bias, o_norm, w_out):
    B, L, _ = h.shape
    proj = h @ w_in
    qkv = proj[..., :GDN_CONV_DIM]
    z = proj[..., GDN_CONV_DIM:GDN_CONV_DIM + GDN_V_DIM]
    b_logit = proj[..., GDN_CONV_DIM + GDN_V_DIM:GDN_CONV_DIM + GDN_V_DIM + GDN_V_HEADS]
    a_logit = proj[..., GDN_CONV_DIM + GDN_V_DIM + GDN_V_HEADS:]
    qkv = jax.nn.silu(causal_depthwise_conv(qkv, conv_w))
    q = qkv[..., :GDN_QK_DIM].reshape(B, L, GDN_QK_HEADS, GDN_HEAD_DIM)
    k = qkv[..., GDN_QK_DIM:2 * GDN_QK_DIM].reshape(B, L, GDN_QK_HEADS, GDN_HEAD_DIM)
    v = qkv[..., 2 * GDN_QK_DIM:].reshape(B, L, GDN_V_HEADS, GDN_HEAD_DIM)
    rep = GDN_V_HEADS // GDN_QK_HEADS
    q = jnp.repeat(l2_normalize(q) * GDN_HEAD_DIM ** -0.5, rep, axis=2)
    k = jnp.repeat(l2_normalize(k), rep, axis=2)
    beta = jax.nn.sigmoid(b_logit.astype(jnp.float32))
    g = -jnp.exp(a_log.astype(jnp.float32)) * jax.nn.softplus(
        a_logit.astype(jnp.float32) + dt_bias.astype(jnp.float32))
    lead = CHUNK - N_META

    def pad(t):
        return jnp.pad(t.astype(jnp.float32), ((0, 0), (lead, 0)) + ((0, 0),) * (t.ndim - 2))

    o = chunk_gated_delta_rule(pad(q), pad(k), pad(v), pad(beta), pad(g))[:, lead:]
    o = rms_norm(o, o_norm) * jax.nn.silu(z.reshape(B, L, GDN_V_HEADS, GDN_HEAD_DIM).astype(jnp.float32))
    return o.reshape(B, L, GDN_V_DIM).astype(h.dtype) @ w_out


def shared_kv(h, kv_norm, w_kv, cos, sin):
    B, L, _ = h.shape
    kv = rms_norm(h, kv_norm) @ w_kv
    k = partial_rope(kv[..., :DIFF_Q_DIM].reshape(B, L, DIFF_HEADS, 2, DIFF_QK_DIM), cos, sin)
    v = kv[..., DIFF_Q_DIM:].reshape(B, L, DIFF_HEADS, DIFF_V_DIM)
    return k, v


def differential_attention(h, k, v, w_q, lam, subln, w_o, lambda_init, cos, sin):
    B, L, _ = h.shape
    q = partial_rope((h @ w_q).reshape(B, L, DIFF_HEADS, 2, DIFF_QK_DIM), cos, sin) * DIFF_QK_DIM ** -0.5
    lam32 = lam.astype(jnp.float32)
    lam_val = (jnp.exp(jnp.sum(lam32[0] * lam32[1])) - jnp.exp(jnp.sum(lam32[2] * lam32[3]))
               + lambda_init)
    n_blk = -(-L // Q_BLOCK)
    Lq = n_blk * Q_BLOCK
    q = jnp.pad(q, ((0, 0), (0, Lq - L), (0, 0), (0, 0), (0, 0)))
    q = jnp.moveaxis(q.reshape(B, n_blk, Q_BLOCK, DIFF_HEADS, 2, DIFF_QK_DIM), 1, 0)
    k_pos = jnp.arange(L)

    def block(args):
        q_blk, blk = args
        s = jnp.einsum('bqhmd,bkhmd->bhmqk', q_blk, k, preferred_element_type=jnp.float32)
        q_pos = blk * Q_BLOCK + jnp.arange(Q_BLOCK)
        s = jnp.where(k_pos[None, :] <= q_pos[:, None], s, -jnp.inf)
        p = jax.nn.softmax(s, axis=-1)
        a = p[:, :, 0] - lam_val * p[:, :, 1]
        return jnp.einsum('bhqk,bkhv->bqhv', a.astype(v.dtype), v,
                          preferred_element_type=jnp.float32).astype(v.dtype)

    o = lax.map(block, (q, jnp.arange(n_blk)))
    o = jnp.moveaxis(o, 0, 1).reshape(B, Lq, DIFF_HEADS, DIFF_V_DIM)[:, :L]
    o = rms_norm(o, subln) * (1.0 - lambda_init)
    return o.reshape(B, L, DIFF_HEADS * DIFF_V_DIM) @ w_o


def squared_relu_mlp(h, w_up, w_down):
    return jnp.square(jax.nn.relu(h @ w_up)) @ w_down


def setup_inputs(seed: int = 0) -> dict:
    key = jax.random.key(seed)
    ks = jax.random.split(key, 17)
    f32 = jnp.float32

    def dense(k, shape, fan_in):
        return jax.random.normal(k, shape, f32) * fan_in ** -0.5

    def gain(k, shape):
        return 1.0 + 0.05 * jax.random.normal(k, shape, f32)

    x = jax.random.normal(ks[0], (BATCH, SEQ, D_MODEL), f32)
    meta_tokens = jax.random.normal(ks[1], (N_META, D_MODEL), f32)
    norm_gains = gain(ks[2], (DEPTH, 4, D_MODEL))
    mlp_w_up = dense(ks[3], (DEPTH, D_MODEL, D_FF), D_MODEL)
    mlp_w_down = dense(ks[4], (DEPTH, D_FF, D_MODEL), D_FF)
    gdn_w_in = dense(ks[5], (N_A_LAYERS, D_MODEL, GDN_IN_DIM), D_MODEL)
    gdn_conv_w = dense(ks[6], (N_A_LAYERS, GDN_CONV_K, GDN_CONV_DIM), GDN_CONV_K)
    gdn_a_log = jnp.log(jax.random.uniform(ks[7], (N_A_LAYERS, GDN_V_HEADS), f32, 1.0, 16.0))
    dt = jnp.exp(jax.random.uniform(ks[8], (N_A_LAYERS, GDN_V_HEADS), f32,
                                    math.log(1e-3), math.log(1e-1)))
    gdn_dt_bias = dt + jnp.log(-jnp.expm1(-dt))
    gdn_o_norm = gain(ks[9], (N_A_LAYERS, GDN_HEAD_DIM))
    gdn_w_out = dense(ks[10], (N_A_LAYERS, GDN_V_DIM, D_MODEL), GDN_V_DIM)
    kv_norm = gain(ks[11], (D_MODEL,))
    w_kv = dense(ks[12], (D_MODEL, DIFF_KV_DIM), D_MODEL)
    diff_w_q = dense(ks[13], (N_B_LAYERS, D_MODEL, DIFF_Q_DIM), D_MODEL)
    diff_lambda = 0.1 * jax.random.normal(ks[14], (N_B_LAYERS, 4, DIFF_QK_DIM), f32)
    diff_subln = gain(ks[15], (N_B_LAYERS, DIFF_V_DIM))
    diff_w_o = dense(ks[16], (N_B_LAYERS, DIFF_HEADS * DIFF_V_DIM, D_MODEL), DIFF_HEADS * DIFF_V_DIM)
    return {'x': x, 'meta_tokens': meta_tokens, 'norm_gains': norm_gains,
            'mlp_w_up': mlp_w_up, 'mlp_w_down': mlp_w_down,
            'gdn_w_in': gdn_w_in, 'gdn_conv_w': gdn_conv_w, 'gdn_a_log': gdn_a_log,
            'gdn_dt_bias': gdn_dt_bias, 'gdn_o_norm': gdn_o_norm, 'gdn_w_out': gdn_w_out,
            'kv_norm': kv_norm, 'w_kv': w_kv, 'diff_w_q': diff_w_q, 'diff_lambda': diff_lambda,
            'diff_subln': diff_subln, 'diff_w_o': diff_w_o}


def reference(x, meta_tokens, norm_gains, mlp_w_up, mlp_w_down, gdn_w_in, gdn_conv_w, gdn_a_log,
              gdn_dt_bias, gdn_o_norm, gdn_w_out, kv_norm, w_kv, diff_w_q, diff_lambda, diff_subln,
              diff_w_o):
    B = x.shape[0]
    meta = jnp.broadcast_to(meta_tokens.astype(x.dtype)[None], (B, N_META, D_MODEL))
    h = jnp.concatenate([meta, x], axis=1)
    L = h.shape[1]
    pos = jnp.arange(L, dtype=jnp.float32)
    inv_freq = ROPE_THETA ** (-jnp.arange(0, ROT_DIM, 2, dtype=jnp.float32) / ROT_DIM)
    ang = pos[:, None] * inv_freq[None, :]
    cos, sin = jnp.cos(ang), jnp.sin(ang)
    kv_k = kv_v = None
    for layer in range(DEPTH):
        hn = rms_norm(h, norm_gains[layer, 0])
        if layer < N_A_LAYERS:
            mix = gated_deltanet(hn, gdn_w_in[layer], gdn_conv_w[layer], gdn_a_log[layer],
                                 gdn_dt_bias[layer], gdn_o_norm[layer], gdn_w_out[layer])
        else:
            if layer == N_A_LAYERS:
                kv_k, kv_v = shared_kv(h, kv_norm, w_kv, cos, sin)
            j = layer - N_A_LAYERS
            lambda_init = 0.8 - 0.6 * math.exp(-0.3 * layer)
            mix = differential_attention(hn, kv_k, kv_v, diff_w_q[j], diff_lambda[j], diff_subln[j],
                                         diff_w_o[j], lambda_init, cos, sin)
        h = h + rms_norm(mix, norm_gains[layer, 1])
        ff = squared_relu_mlp(rms_norm(h, norm_gains[layer, 2]), mlp_w_up[layer], mlp_w_down[layer])
        h = h + rms_norm(ff, norm_gains[layer, 3])
    return h[:, N_META:]
```

```python
import math, os
from contextlib import ExitStack
import numpy as np
import ml_dtypes
import concourse.bass as bass
import concourse.mybir as mybir
from concourse.bass_utils import run_bass_kernel_spmd

F32, BF16 = mybir.dt.float32, mybir.dt.bfloat16
AF = mybir.ActivationFunctionType
ALU = mybir.AluOpType
AX = mybir.AxisListType

D = 2048
DFF = 8192
NMETA = 16
GIN = 12352
EPS = 1e-6
NEG = -30000.0


class Tl:
    __slots__ = ("ap", "w", "r", "dsem", "name", "psum")

    def __init__(self, ap, name=""):
        self.ap = ap
        self.w = {}
        self.r = {}
        self.dsem = {}
        self.name = name
        self.psum = False

    def __getitem__(self, k):
        return self.ap[k]


class Pool_:
    def __init__(self, tiles):
        self.tiles = tiles
        self.i = 0

    def next(self):
        t = self.tiles[self.i % len(self.tiles)]
        self.i += 1
        return t


class Ctx:
    def __init__(self, nc):
        self.nc = nc
        self.es = ExitStack()
        self.eng = {"pe": nc.tensor, "act": nc.scalar, "dve": nc.vector, "pool": nc.gpsimd, "sp": nc.sync}
        self.sem = {k: self.es.enter_context(nc.semaphore("s_" + k)) for k in ("pe", "act", "dve", "pool")}
        self.cnt = {k: 0 for k in self.sem}
        self.waited = {}
        self.dsems = []
        self.dq = {"sp": [], "pool": []}
        self.dnext = {"sp": 0, "pool": 0}
        self.uid = 0
        self.phase_es = None

    def begin_phase(self):
        self.phase_es = ExitStack()
        self.dnext = {"sp": 0, "pool": 0}

    def end_phase(self):
        self.barrier()
        self.phase_es.close()
        self.phase_es = None

    def sb(self, shape, dtype, name="t"):
        self.uid += 1
        h = self.phase_es.enter_context(self.nc.sbuf_tensor(f"{name}_{self.uid}", list(shape), dtype))
        return Tl(h[tuple(slice(None) for _ in shape)], name)

    def ps(self, shape, dtype, name="p"):
        self.uid += 1
        nb = 512 if dtype == F32 else 1024
        h = self.phase_es.enter_context(self.nc.psum_tensor(f"{name}_{self.uid}", [shape[0], nb], dtype))
        n = 1
        for d in shape[1:]:
            n *= d
        assert n <= nb
        v = h[:, 0:n]
        if len(shape) == 3:
            v = v.rearrange("p (a b) -> p a b", a=shape[1])
        t = Tl(v, name)
        t.psum = True
        return t

    def sbpool(self, n, shape, dtype, name="t"):
        return Pool_([self.sb(shape, dtype, name) for _ in range(n)])

    def pspool(self, n, shape, dtype, name="p"):
        return Pool_([self.ps(shape, dtype, name) for _ in range(n)])

    def _dsem(self, t, q):
        if q not in t.dsem:
            lst = self.dq[q]
            if self.dnext[q] >= len(lst):
                s = self.es.enter_context(self.nc.semaphore(f"d{q}_{len(lst)}"))
                ent = [s, 0]
                lst.append(ent)
                self.dsems.append(ent)
            t.dsem[q] = lst[self.dnext[q]]
            self.dnext[q] += 1
        return t.dsem[q]

    def _wait(self, e, ev):
        key, sem, val = ev
        if e == "pe" and key == "pe":
            return
        k = (e, key)
        if self.waited.get(k, 0) >= val:
            return
        self.eng[e].wait_ge(sem, val)
        self.waited[k] = val

    def _deps(self, e, outs, ins):
        for t in ins:
            if t is None:
                continue
            for ev in t.w.values():
                self._wait(e, ev)
            if t.psum:
                for ev in list(t.r.values()):
                    if ev[0] != e:
                        self._wait(e, ev)
        for t in outs:
            if t is None:
                continue
            for ev in t.w.values():
                self._wait(e, ev)
            for ev in t.r.values():
                self._wait(e, ev)

    def _mark(self, ev, outs, ins):
        for t in ins:
            if t is not None:
                t.r[ev[0]] = ev
        for t in outs:
            if t is not None:
                t.w = {ev[0]: ev}
                t.r = {}

    def op(self, e, inst_fn, outs, ins):
        self._deps(e, outs, ins)
        inst = inst_fn(self.eng[e])
        self.cnt[e] += 1
        inst.then_inc(self.sem[e], 1)
        self._mark((e, self.sem[e], self.cnt[e]), outs, ins)

    def dma(self, q, out_ap, in_ap, out_t=None, in_t=None):
        self._deps(q, [out_t], [in_t])
        owner = out_t if out_t is not None else in_t
        ds = self._dsem(owner, q)
        inst = self.eng[q].dma_start(out=out_ap, in_=in_ap)
        ds[1] += 16
        inst.then_inc(ds[0], 16)
        self._mark((id(ds), ds[0], ds[1]), [out_t], [in_t])

    def barrier(self):
        for e in ("pe", "act", "dve", "pool", "sp"):
            for k in self.sem:
                if k != e and self.cnt[k] > 0:
                    self._wait(e, (k, self.sem[k], self.cnt[k]))
            for ds in self.dsems:
                if ds[1] > 0:
                    self._wait(e, (id(ds), ds[0], ds[1]))

    def mm(self, out_t, out_ap, lhsT_t, lhsT_ap, rhs_t, rhs_ap, start=True, stop=True):
        self.op("pe", lambda en: en.matmul(out_ap, lhsT=lhsT_ap, rhs=rhs_ap, start=start, stop=stop),
                [out_t], [lhsT_t, rhs_t])

    def tr(self, out_t, out_ap, in_t, in_ap, id_t, id_ap):
        self.op("pe", lambda en: en.transpose(out_ap, in_ap, id_ap), [out_t], [in_t, id_t])

    def act(self, out_t, out_ap, in_t, in_ap, func, bias=0.0, scale=1.0, extra_in=(), accum=None, accum_t=None):
        kw = {}
        if accum is not None:
            kw["accum_out"] = accum
        self.op("act", lambda en: en.activation(out=out_ap, in_=in_ap, func=func, bias=bias, scale=scale, **kw),
                [out_t] + ([accum_t] if accum_t is not None else []), [in_t] + list(extra_in))

    def tt(self, e, out_t, out_ap, a_t, a_ap, b_t, b_ap, op):
        self.op(e, lambda en: en.tensor_tensor(out=out_ap, in0=a_ap, in1=b_ap, op=op), [out_t], [a_t, b_t])

    def ts(self, e, out_t, out_ap, a_t, a_ap, s1, op0, s2=None, op1=None, s_t=()):
        if op1 is None:
            f = lambda en: en.tensor_scalar(out=out_ap, in0=a_ap, scalar1=s1, scalar2=None, op0=op0)
        else:
            f = lambda en: en.tensor_scalar(out=out_ap, in0=a_ap, scalar1=s1, scalar2=s2, op0=op0, op1=op1)
        self.op(e, f, [out_t], [a_t] + list(s_t))

    def stt(self, e, out_t, out_ap, a_t, a_ap, sc, b_t, b_ap, op0, op1, s_t=()):
        e = "dve"
        self.op(e, lambda en: en.scalar_tensor_tensor(out=out_ap, in0=a_ap, scalar=sc, in1=b_ap, op0=op0, op1=op1),
                [out_t], [a_t, b_t] + list(s_t))

    def cp(self, e, out_t, out_ap, in_t, in_ap):
        if e == "act":
            self.op(e, lambda en: en.copy(out=out_ap, in_=in_ap), [out_t], [in_t])
        else:
            self.op(e, lambda en: en.tensor_copy(out=out_ap, in_=in_ap), [out_t], [in_t])

    def memset(self, e, t, ap, val):
        self.op(e, lambda en: en.memset(ap, val), [t], [])

    def recip(self, out_t, out_ap, in_t, in_ap):
        self.op("dve", lambda en: en.reciprocal(out=out_ap, in_=in_ap), [out_t], [in_t])


def token_tiles(Tp):
    res = []
    t = 0
    while t < Tp:
        w = min(512, Tp - t)
        res.append((t, w))
        t += w
    return res


class Builder:
    def __init__(self, T, n_layers=4, debug=False):
        self.T = T
        self.Tp = ((T + 127) // 128) * 128
        self.NC = self.Tp // 128
        self.n_layers = n_layers
        self.nc = bass.Bass("TRN2", target_bir_lowering=False)
        self.c = Ctx(self.nc)
        self.debug = debug

    def declare(self):
        nc, Tp = self.nc, self.Tp

        def inp(name, shape, dt=F32):
            return nc.dram_tensor(name, list(shape), dt, kind="ExternalInput").ap()

        def scr(name, shape, dt):
            kind = "ExternalOutput" if (self.debug and name.startswith("s_")) else "Internal"
            return nc.dram_tensor(name, list(shape), dt, kind=kind).ap()

        self.h0 = inp("h0", [D, Tp])
        self.w32 = {
            "up": inp("mlp_w_up", [4, D, DFF]), "down": inp("mlp_w_down", [4, DFF, D]),
            "gin": inp("gdn_w_in", [2, D, GIN]), "gout": inp("gdn_w_out", [2, 4096, D]),
            "kv": inp("w_kv", [D, 4096]), "q": inp("diff_w_q", [2, D, D]), "o": inp("diff_w_o", [2, D, D]),
        }
        self.wbf = {k: scr(k + "_bf", v.shape, BF16) for k, v in self.w32.items()}
        self.gains = inp("gains", [128, 16, 16])
        self.kvg = inp("kvg", [128, 16])
        self.convw = inp("convw", [128, 2, 4, 64])
        self.alog = inp("alog", [128, 2, 32])
        self.dtb = inp("dtb", [128, 2, 32])
        self.onorm = inp("onorm", [128, 2])
        self.lam = inp("lam", [128, 2, 4, 128])
        self.subln = inp("subln", [128, 2, 256])
        self.k_identb = inp("identb", [128, 128], BF16)
        self.k_onesb = inp("onesb", [128, 128], BF16)
        self.k_onesf = inp("onesf", [128, 128])
        self.k_U = inp("Umat", [128, 128])
        self.k_sel = inp("sel", [32, 32, 128])
        self.k_identf = inp("identf", [128, 128])
        self.k_m4 = inp("m4", [128, 8, 4, 128], BF16)
        self.k_negU = inp("negU", [128, 128])
        self.k_M2 = inp("M2", [128, 128])
        self.k_RT = inp("RT", [128, 128])
        self.k_cos = inp("cosT", [128, Tp])
        self.k_sin = inp("sinT", [128, Tp])
        self.k_md = inp("maskd", [128, 4, 512])
        self.hT = nc.dram_tensor("hT", [D, Tp], F32, kind="ExternalOutput").ap()
        self.s_q = scr("s_q", [D, Tp], BF16)
        self.s_k = scr("s_k", [D, Tp], BF16)
        self.s_v = scr("s_v", [4096, Tp], BF16)
        self.s_z = scr("s_z", [4096, Tp], BF16)
        self.s_bg = scr("s_bg", [Tp, 64], F32)
        self.s_og = scr("s_og", [4096, Tp], BF16)
        self.a_k = scr("a_k", [D, Tp], BF16)
        self.a_v = scr("a_v", [Tp, D], BF16)
        self.a_q = scr("a_q", [D, Tp], BF16)
        self.a_on = scr("a_on", [D, Tp], BF16)

    def phase_init(self):
        c = self.c
        c.begin_phase()
        dummy = c.sb([128, 1], F32, "dummy")
        order = []
        for l in range(2):
            order += [("gin", l), ("gout", l), ("up", l), ("down", l)]
        order += [("kv", None)]
        for l in range(2):
            order += [("q", l), ("o", l), ("up", 2 + l), ("down", 2 + l)]
        for name, l in order:
            src = self.w32[name] if l is None else self.w32[name][l]
            dst = self.wbf[name] if l is None else self.wbf[name][l]
            K = src.shape[0]
            for r0 in range(0, K, 128):
                c.dma("pool", dst[r0:r0 + 128, :], src[r0:r0 + 128, :], in_t=dummy)
        for r0 in range(0, D, 128):
            c.dma("sp", self.hT[r0:r0 + 128, :], self.h0[r0:r0 + 128, :], in_t=dummy)
        c.end_phase()

    def load_const(self, ap_dram, shape, dt, name):
        c = self.c
        t = c.sb(shape, dt, name)
        c.dma("sp", t.ap, ap_dram, out_t=t)
        return t

    def rmsnorm_fm(self, x, nkc, w, gain_t, gain_ap_fn, out_hn, sq, ps_pool, rs_pool, onesb, nfeat):
        c = self.c
        ps = ps_pool.next()
        for kc in range(nkc):
            s1 = sq.next()
            c.act(s1, s1[:, :w], x, x[:, kc, :w], AF.Square)
            c.mm(ps, ps[:, :w], onesb, onesb[:, :], s1, s1[:, :w], start=(kc == 0), stop=(kc == nkc - 1))
        rs = rs_pool.next()
        c.act(rs, rs[:, :w], ps, ps[:, :w], AF.Sqrt, bias=EPS, scale=1.0 / nfeat)
        c.recip(rs, rs[:, :w], rs, rs[:, :w])
        if out_hn is not None:
            for kc in range(nkc):
                e = "dve" if kc % 2 == 0 else "pool"
                c.stt(e, out_hn, out_hn[:, kc, :w], x, x[:, kc, :w], gain_ap_fn(kc), rs, rs[:, :w],
                      ALU.mult, ALU.mult, s_t=[gain_t])
        return rs

    def linear(self, W, k0, KC, n0, n1, bc, x_t, x_fn, w, wpool, pspool, epi):
        c = self.c
        for nb in range(n0, n1, bc):
            cols = min(bc, n1 - nb)
            wt = wpool.next()
            for k8 in range(0, KC, 8):
                c.dma("sp", wt[:, k8:k8 + 8, 0:cols],
                      W[k0 + k8 * 128:k0 + (k8 + 8) * 128, nb:nb + cols].rearrange("(kc p) n -> p kc n", p=128), out_t=wt)
            for j in range(cols // 128):
                ps = pspool.next()
                for kc in range(KC):
                    c.mm(ps, ps[:, :w], wt, wt[:, kc, j * 128:(j + 1) * 128], x_t, x_fn(kc),
                         start=(kc == 0), stop=(kc == KC - 1))
                epi((nb - n0) // 128 + j, ps)

    def postnorm_residual(self, y, w, t0, gains, gidx, h32, sq, ps1, rsp, onesb):
        c = self.c
        rs = self.rmsnorm_fm(y, 16, w, None, None, None, sq, ps1, rsp, onesb, D)
        for kc in range(16):
            e = "dve" if kc % 2 == 0 else "pool"
            c.stt(e, y, y[:, kc, :w], y, y[:, kc, :w], gains[:, gidx, kc:kc + 1], rs, rs[:, :w],
                  ALU.mult, ALU.mult, s_t=[gains])
            c.tt(e, h32, h32[:, kc, :w], y, y[:, kc, :w], h32, h32[:, kc, :w], ALU.add)
        c.dma("pool", self.hT[:, t0:t0 + w].rearrange("(kc p) t -> p kc t", p=128), h32[:, :, :w], in_t=h32)

    def phase_g1(self, l):
        c = self.c
        Tp = self.Tp
        c.begin_phase()
        gains = self.load_const(self.gains, [128, 16, 16], F32, "gains")
        onesb = self.load_const(self.k_onesb, [128, 128], BF16, "onesb")
        cw = self.load_const(self.convw[:, l], [128, 4, 64], F32, "cw")
        alog = self.load_const(self.alog[:, l], [128, 32], F32, "alog")
        dtb = self.load_const(self.dtb[:, l], [128, 32], F32, "dtb")
        negea = c.sb([128, 32], F32, "negea")
        c.act(negea, negea[:, :], alog, alog[:, :], AF.Exp)
        c.ts("dve", negea, negea[:, :], negea, negea[:, :], -1.0, ALU.mult)
        W = self.wbf["gin"][l]
        wlast = c.sb([128, 16, 64], BF16, "wlast")
        c.dma("sp", wlast[:, :, :], W[:, 12288:12352].rearrange("(kc p) n -> p kc n", p=128), out_t=wlast)
        halo = c.sb([128, 64, 3], F32, "halo")
        c.memset("pool", halo, halo[:, :, :], 0.0)
        h32p = c.sbpool(1, [128, 16, 512], F32, "h32")
        hn = c.sb([128, 16, 512], BF16, "hn")
        sq = c.sbpool(4, [128, 512], BF16, "sq")
        wpool = c.sbpool(3, [128, 16, 512], BF16, "wblk")
        psp = c.pspool(4, [128, 512], F32, "psl")
        ps1 = c.pspool(2, [128, 512], F32, "ps1")
        psba = c.pspool(1, [128, 64], F32, "psba")
        rsp = c.sbpool(2, [128, 512], F32, "rs")
        xpp = c.sbpool(3, [128, 515], F32, "xp")
        yp = c.sbpool(3, [128, 512], F32, "y")
        sp_ = c.sbpool(3, [128, 512], F32, "s")
        sqp = c.sbpool(2, [128, 512], BF16, "sq1")
        r1p = c.sbpool(2, [128, 512], F32, "r1")
        obp = c.sbpool(4, [128, 512], BF16, "ob")
        bap = c.sbpool(2, [128, 64], F32, "ba")
        tmp32 = c.sbpool(4, [128, 32], F32, "tmp32")

        for (t0, w) in token_tiles(Tp):
            h32 = h32p.next()
            c.dma("sp", h32[:, :, :w], self.hT[:, t0:t0 + w].rearrange("(kc p) t -> p kc t", p=128), out_t=h32)
            self.rmsnorm_fm(h32, 16, w, gains, lambda kc: gains[:, l * 4 + 0, kc:kc + 1], hn, sq, ps1, rsp, onesb, D)

            def epi(oc, ps, t0=t0, w=w):
                if oc < 64:
                    xp = xpp.next()
                    c.cp("pool", xp, xp[:, 0:3], halo, halo[:, oc, :])
                    c.cp("act", xp, xp[:, 3:3 + w], ps, ps[:, :w])
                    c.cp("pool", halo, halo[:, oc, :], xp, xp[:, w:w + 3])
                    y = yp.next()
                    c.ts("dve", y, y[:, :w], xp, xp[:, 0:w], cw[:, 0, oc:oc + 1], ALU.mult, s_t=[cw])
                    for j in range(1, 4):
                        c.stt("dve", y, y[:, :w], xp, xp[:, j:j + w], cw[:, j, oc:oc + 1], y, y[:, :w],
                              ALU.mult, ALU.add, s_t=[cw])
                    if oc >= 32:
                        ob = obp.next()
                        c.act(ob, ob[:, :w], y, y[:, :w], AF.Silu)
                        r = (oc - 32) * 128
                        c.dma("pool", self.s_v[r:r + 128, t0:t0 + w], ob[:, :w], in_t=ob)
                    else:
                        s = sp_.next()
                        c.act(s, s[:, :w], y, y[:, :w], AF.Silu)
                        sq1 = sqp.next()
                        c.tt("pool", sq1, sq1[:, :w], s, s[:, :w], s, s[:, :w], ALU.mult)
                        p1 = ps1.next()
                        c.mm(p1, p1[:, :w], onesb, onesb[:, :], sq1, sq1[:, :w])
                        r1 = r1p.next()
                        c.act(r1, r1[:, :w], p1, p1[:, :w], AF.Sqrt, bias=EPS, scale=1.0)
                        c.recip(r1, r1[:, :w], r1, r1[:, :w])
                        ob = obp.next()
                        if oc < 16:
                            c.stt("dve", ob, ob[:, :w], s, s[:, :w], 128.0 ** -0.5, r1, r1[:, :w], ALU.mult, ALU.mult)
                            c.dma("pool", self.s_q[oc * 128:(oc + 1) * 128, t0:t0 + w], ob[:, :w], in_t=ob)
                        else:
                            c.tt("dve", ob, ob[:, :w], s, s[:, :w], r1, r1[:, :w], ALU.mult)
                            r = (oc - 16) * 128
                            c.dma("pool", self.s_k[r:r + 128, t0:t0 + w], ob[:, :w], in_t=ob)
                else:
                    ob = obp.next()
                    c.act(ob, ob[:, :w], ps, ps[:, :w], AF.Silu)
                    r = (oc - 64) * 128
                    c.dma("pool", self.s_z[r:r + 128, t0:t0 + w], ob[:, :w], in_t=ob)

            self.linear(W, 0, 16, 0, 12288, 512, hn, lambda kc, w=w: hn[:, kc, :w], w, wpool, psp, epi)
            for sbk in range(w // 128):
                pb = psba.next()
                for kc in range(16):
                    c.mm(pb, pb[:, :], hn, hn[:, kc, sbk * 128:(sbk + 1) * 128], wlast, wlast[:, kc, :],
                         start=(kc == 0), stop=(kc == 15))
                ba = bap.next()
                c.act(ba, ba[:, 0:32], pb, pb[:, 0:32], AF.Sigmoid)
                a = tmp32.next()
                c.tt("dve", a, a[:, :], pb, pb[:, 32:64], dtb, dtb[:, :], ALU.add)
                ab = tmp32.next()
                c.act(ab, ab[:, :], a, a[:, :], AF.Abs)
                c.act(ab, ab[:, :], ab, ab[:, :], AF.Exp, scale=-1.0)
                c.act(ab, ab[:, :], ab, ab[:, :], AF.Ln, bias=1.0)
                c.stt("dve", a, a[:, :], a, a[:, :], 0.0, ab, ab[:, :], ALU.max, ALU.add)
                c.tt("dve", ba, ba[:, 32:64], a, a[:, :], negea, negea[:, :], ALU.mult)
                tt0 = t0 + sbk * 128
                c.dma("pool", self.s_bg[tt0:tt0 + 128, :], ba[:, :], in_t=ba)
        c.end_phase()

    def phase_g2(self, l):
        c = self.c
        c.begin_phase()
        identb = self.load_const(self.k_identb, [128, 128], BF16, "identb")
        onesb = self.load_const(self.k_onesb, [128, 128], BF16, "onesb")
        onesf = self.load_const(self.k_onesf, [128, 128], F32, "onesf")
        Um = self.load_const(self.k_U, [128, 128], F32, "Um")
        negU = self.load_const(self.k_negU, [128, 128], F32, "negU")
        M2 = self.load_const(self.k_M2, [128, 128], F32, "M2")
        onorm = self.load_const(self.onorm, [128, 2], F32, "onorm")
        S32 = c.sb([128, 32, 128], F32, "S32")
        Sbf = c.sb([128, 32, 128], BF16, "Sbf")
        c.memset("pool", S32, S32[:, :, :], 0.0)
        c.memset("pool", Sbf, Sbf[:, :, :], 0.0)
        qp = c.sbpool(2, [128, 16, 128], BF16, "qc")
        kp = c.sbpool(2, [128, 16, 128], BF16, "kc")
        vp = c.sbpool(2, [128, 32, 128], BF16, "vc")
        zp = c.sbpool(2, [128, 32, 128], BF16, "zc")
        bgp = c.sbpool(2, [128, 64], F32, "bg")
        gtp = c.sbpool(2, [32, 128], F32, "GT")
        sel = self.load_const(self.k_sel, [32, 32, 128], F32, "sel")
        identf = self.load_const(self.k_identf, [128, 128], F32, "identf")
        m4 = self.load_const(self.k_m4, [128, 8, 4, 128], BF16, "m4")
        amp = c.sbpool(2, [128, 6, 4, 128], BF16, "amn")
        xtp = c.sbpool(2, [128, 4, 128], BF16, "xt")
        tsp = c.sbpool(2, [128, 4, 128], BF16, "ts")
        small = {n: c.sbpool(2, [128, 32], F32, n) for n in ("G", "eG", "nbeG", "negb", "kdc", "eGl")}
        f4 = {n: c.sbpool(2, [128, 4, 128], F32, n) for n in ("e1", "DmT", "e2", "Dms", "eGbc", "o32", "rs4")}
        b4 = {n: c.sbpool(2, [128, 4, 128], BF16, n) for n in ("aqk", "qg", "TT", "kdec", "bv", "rb", "vn", "sq4", "og")}
        pabp = c.sbpool(4, [128, 4, 128], BF16, "pab")
        psf = c.pspool(5, [128, 4, 128], F32, "psf")
        psb = c.pspool(2, [128, 8, 128], BF16, "psb")
        pss = c.pspool(1, [128, 512], F32, "pss")

        for ci in range(self.NC):
            t0 = ci * 128
            qc, kc_, vc, zc, bg = qp.next(), kp.next(), vp.next(), zp.next(), bgp.next()
            for hh in range(0, 16, 8):
                c.dma("sp", qc[:, hh:hh + 8, :],
                      self.s_q[hh * 128:(hh + 8) * 128, t0:t0 + 128].rearrange("(h p) t -> p h t", p=128), out_t=qc)
                c.dma("sp", kc_[:, hh:hh + 8, :],
                      self.s_k[hh * 128:(hh + 8) * 128, t0:t0 + 128].rearrange("(h p) t -> p h t", p=128), out_t=kc_)
            for hh in range(0, 32, 8):
                c.dma("sp", vc[:, hh:hh + 8, :],
                      self.s_v[hh * 128:(hh + 8) * 128, t0:t0 + 128].rearrange("(h p) t -> p h t", p=128), out_t=vc)
                c.dma("sp", zc[:, hh:hh + 8, :],
                      self.s_z[hh * 128:(hh + 8) * 128, t0:t0 + 128].rearrange("(h p) t -> p h t", p=128), out_t=zc)
            c.dma("sp", bg[:, :], self.s_bg[t0:t0 + 128, :], out_t=bg)
            p = pss.next()
            c.mm(p, p[:, 0:32], Um, Um[:, :], bg, bg[:, 32:64])
            c.mm(p, p[:, 32:64], onesf, onesf[:, :], bg, bg[:, 32:64])
            G = small["G"].next(); eG = small["eG"].next(); nbeG = small["nbeG"].next()
            negb = small["negb"].next(); kdc = small["kdc"].next(); eGl = small["eGl"].next()
            c.cp("dve", G, G[:, :], p, p[:, 0:32])
            c.act(eG, eG[:, :], p, p[:, 0:32], AF.Exp)
            c.stt("dve", nbeG, nbeG[:, :], eG, eG[:, :], -1.0, bg, bg[:, 0:32], ALU.mult, ALU.mult)
            c.ts("dve", negb, negb[:, :], bg, bg[:, 0:32], -1.0, ALU.mult)
            c.tt("dve", kdc, kdc[:, :], p, p[:, 32:64], G, G[:, :], ALU.subtract)
            c.act(kdc, kdc[:, :], kdc, kdc[:, :], AF.Exp)
            c.act(eGl, eGl[:, :], p, p[:, 32:64], AF.Exp)
            CUT = 99
            if CUT < 1:
                continue
            c.tr(p, p[0:32, 128:256], G, G[:, :], identf, identf[:, :])
            GT = gtp.next()
            c.cp("act", GT, GT[:, :], p, p[0:32, 128:256])
            if CUT < 2:
                continue
            for hg in range(8):
                hs = [4 * hg + j for j in range(4)]
                qhs = [2 * hg, 2 * hg + 1]
                pG = psf.next()
                for j, h in enumerate(hs):
                    c.mm(pG, pG[:, j, :], sel, sel[:, h, :], GT, GT[:, :])
                e1 = f4["e1"].next(); DmT = f4["DmT"].next(); e2 = f4["e2"].next(); Dms = f4["Dms"].next()
                eGbc = f4["eGbc"].next()
                for j, h in enumerate(hs):
                    c.stt("dve", e1, e1[:, j, :], pG, pG[:, j, :], G[:, h:h + 1], negU, negU[:, :],
                          ALU.subtract, ALU.add, s_t=[G])
                    c.stt("dve", e2, e2[:, j, :], pG, pG[:, j, :], G[:, h:h + 1], M2, M2[:, :],
                          ALU.subtract, ALU.subtract, s_t=[G])
                c.act(DmT, DmT[:, :, :], e1, e1[:, :, :], AF.Exp)
                c.act(Dms, Dms[:, :, :], e2, e2[:, :, :], AF.Exp, scale=-1.0)
                c.act(eGbc, eGbc[:, :, :], pG, pG[:, :, :], AF.Exp)
                if CUT < 3:
                    continue
                pK = psf.next()
                for j, qh in enumerate(qhs):
                    c.mm(pK, pK[:, j, :], kc_, kc_[:, qh, :], kc_, kc_[:, qh, :])
                    c.mm(pK, pK[:, 2 + j, :], kc_, kc_[:, qh, :], qc, qc[:, qh, :])
                Pa = pabp.next()
                aqk = b4["aqk"].next(); qg = b4["qg"].next()
                for j, h in enumerate(hs):
                    c.stt("dve", Pa, Pa[:, j, :], pK, pK[:, j // 2, :], negb[:, h:h + 1], Dms, Dms[:, j, :],
                          ALU.mult, ALU.mult, s_t=[negb])
                    c.tt("dve", aqk, aqk[:, j, :], pK, pK[:, 2 + j // 2, :], DmT, DmT[:, j, :], ALU.mult)
                    c.tt("pool", qg, qg[:, j, :], qc, qc[:, qhs[j // 2], :], eGbc, eGbc[:, j, :], ALU.mult)
                if CUT < 4:
                    continue
                pT = psb.next()
                for j in range(4):
                    c.tr(pT, pT[:, j, :], Pa, Pa[:, j, :], identb, identb[:, :])
                for j, qh in enumerate(qhs):
                    c.tr(pT, pT[:, 4 + j, :], kc_, kc_[:, qh, :], identb, identb[:, :])
                pV = psb.next()
                for j, h in enumerate(hs):
                    c.tr(pV, pV[:, j, :], vc, vc[:, h, :], identb, identb[:, :])
                TT = b4["TT"].next(); kdec = b4["kdec"].next(); bv = b4["bv"].next()
                c.tt("dve", TT, TT[:, :, :], pT, pT[:, 0:4, :], m4, m4[:, 1, :, :], ALU.mult)
                c.tt("dve", TT, TT[:, :, :], TT, TT[:, :, :], m4, m4[:, 0, :, :], ALU.add)
                for j, h in enumerate(hs):
                    c.ts("dve", kdec, kdec[:, j, :], pT, pT[:, 4 + j // 2, :], kdc[:, h:h + 1], ALU.mult, s_t=[kdc])
                    c.ts("dve", bv, bv[:, j, :], pV, pV[:, j, :], bg[:, h:h + 1], ALU.mult, s_t=[bg])
                amn = amp.next()
                for sv in range(6):
                    c.tt("pool", amn, amn[:, sv, :, :], Pa, Pa[:, :, :], m4, m4[:, 2 + sv, :, :], ALU.mult)
                if CUT < 5:
                    continue
                for sv in range(6):
                    pX = psf.next()
                    for j in range(4):
                        c.mm(pX, pX[:, j, :], amn, amn[:, sv, j, :], TT, TT[:, j, :])
                    X = xtp.next()
                    c.cp("act", X, X[:, :, :], pX, pX[:, :, :])
                    pTr = psb.next()
                    for j in range(4):
                        c.tr(pTr, pTr[:, j, :], TT, TT[:, j, :], identb, identb[:, :])
                    Ts = tsp.next()
                    c.cp("dve", Ts, Ts[:, :, :], pTr, pTr[:, 0:4, :])
                    pD = psf.next()
                    for j in range(4):
                        c.mm(pD, pD[:, j, :], Ts, Ts[:, j, :], X, X[:, j, :])
                    c.tt("dve", TT, TT[:, :, :], pD, pD[:, :, :], TT, TT[:, :, :], ALU.add)
                if CUT < 6:
                    continue
                pkS = psf.next()
                for j, h in enumerate(hs):
                    c.mm(pkS, pkS[:, j, :], kc_, kc_[:, qhs[j // 2], :], Sbf, Sbf[:, h, :])
                rb = b4["rb"].next()
                for j, h in enumerate(hs):
                    c.stt("dve", rb, rb[:, j, :], pkS, pkS[:, j, :], nbeG[:, h:h + 1], bv, bv[:, j, :],
                          ALU.mult, ALU.add, s_t=[nbeG])
                pvn = psf.next()
                for j in range(4):
                    c.mm(pvn, pvn[:, j, :], TT, TT[:, j, :], rb, rb[:, j, :])
                vn = b4["vn"].next()
                c.cp("act", vn, vn[:, :, :], pvn, pvn[:, :, :])
                po = psf.next()
                for j, h in enumerate(hs):
                    c.mm(po, po[:, j, :], Sbf, Sbf[:, h, :], qg, qg[:, j, :], start=True, stop=False)
                    c.mm(po, po[:, j, :], vn, vn[:, j, :], aqk, aqk[:, j, :], start=False, stop=True)
                pS = psf.next()
                for j in range(4):
                    c.mm(pS, pS[:, j, :], kdec, kdec[:, j, :], vn, vn[:, j, :])
                for j, h in enumerate(hs):
                    c.stt("dve", S32, S32[:, h, :], S32, S32[:, h, :], eGl[:, h:h + 1], pS, pS[:, j, :],
                          ALU.mult, ALU.add, s_t=[eGl])
                c.cp("pool", Sbf, Sbf[:, 4 * hg:4 * hg + 4, :], S32, S32[:, 4 * hg:4 * hg + 4, :])
                if CUT < 7:
                    continue
                o32 = f4["o32"].next(); sq4 = b4["sq4"].next(); rs4 = f4["rs4"].next(); og = b4["og"].next()
                c.cp("act", o32, o32[:, :, :], po, po[:, :, :])
                c.tt("pool", sq4, sq4[:, :, :], o32, o32[:, :, :], o32, o32[:, :, :], ALU.mult)
                pq = psf.next()
                c.mm(pq, pq[:, :, :], onesb, onesb[:, :], sq4, sq4[:, :, :])
                c.act(rs4, rs4[:, :, :], pq, pq[:, :, :], AF.Sqrt, bias=EPS, scale=1.0 / 128)
                c.recip(rs4, rs4[:, :, :], rs4, rs4[:, :, :])
                c.stt("dve", o32, o32[:, :, :], o32, o32[:, :, :], onorm[:, l:l + 1], rs4, rs4[:, :, :],
                      ALU.mult, ALU.mult, s_t=[onorm])
                c.tt("pool", og, og[:, :, :], o32, o32[:, :, :], zc, zc[:, 4 * hg:4 * hg + 4, :], ALU.mult)
                c.dma("pool", self.s_og[hg * 512:(hg + 1) * 512, t0:t0 + 128].rearrange("(h p) t -> p h t", p=128),
                      og[:, :, :], in_t=og)
        c.end_phase()

    def phase_outproj(self, l, src, KC, W):
        c = self.c
        c.begin_phase()
        gains = self.load_const(self.gains, [128, 16, 16], F32, "gains")
        onesb = self.load_const(self.k_onesb, [128, 128], BF16, "onesb")
        xin = c.sbpool(1, [128, KC, 512], BF16, "xin")
        h32p = c.sbpool(1, [128, 16, 512], F32, "h32")
        mix = c.sb([128, 16, 512], F32, "mix")
        sq = c.sbpool(4, [128, 512], BF16, "sq")
        bc = 512 if KC == 16 else 256
        wpool = c.sbpool(3, [128, KC, bc], BF16, "wblk")
        psp = c.pspool(4, [128, 512], F32, "psl")
        ps1 = c.pspool(2, [128, 512], F32, "ps1")
        rsp = c.sbpool(2, [128, 512], F32, "rs")
        for (t0, w) in token_tiles(self.Tp):
            x = xin.next()
            for k8 in range(0, KC, 8):
                c.dma("sp", x[:, k8:k8 + 8, :w],
                      src[k8 * 128:(k8 + 8) * 128, t0:t0 + w].rearrange("(kc p) t -> p kc t", p=128), out_t=x)
            h32 = h32p.next()
            c.dma("sp", h32[:, :, :w], self.hT[:, t0:t0 + w].rearrange("(kc p) t -> p kc t", p=128), out_t=h32)

            def epi(oc, ps, w=w):
                c.cp("act", mix, mix[:, oc, :w], ps, ps[:, :w])

            self.linear(W, 0, KC, 0, D, bc, x, lambda kc, w=w, x=x: x[:, kc, :w], w, wpool, psp, epi)
            self.postnorm_residual(mix, w, t0, gains, l * 4 + 1, h32, sq, ps1, rsp, onesb)
        c.end_phase()

    def phase_mlp(self, l):
        c = self.c
        c.begin_phase()
        gains = self.load_const(self.gains, [128, 16, 16], F32, "gains")
        onesb = self.load_const(self.k_onesb, [128, 128], BF16, "onesb")
        h32p = c.sbpool(1, [128, 16, 512], F32, "h32")
        hn = c.sb([128, 16, 512], BF16, "hn")
        sq = c.sbpool(4, [128, 512], BF16, "sq")
        actb = c.sb([128, 16, 512], BF16, "actb")
        ff = c.sb([128, 16, 512], F32, "ff")
        wpl = c.sbpool(3, [128, 16, 512], BF16, "wblk")
        psp = c.pspool(4, [128, 512], F32, "psl")
        ps1 = c.pspool(2, [128, 512], F32, "ps1")
        rsp = c.sbpool(2, [128, 512], F32, "rs")
        rl = c.sbpool(3, [128, 512], F32, "rl")
        Wu, Wd = self.wbf["up"][l], self.wbf["down"][l]
        for (t0, w) in token_tiles(self.Tp):
            h32 = h32p.next()
            c.dma("sp", h32[:, :, :w], self.hT[:, t0:t0 + w].rearrange("(kc p) t -> p kc t", p=128), out_t=h32)
            self.rmsnorm_fm(h32, 16, w, gains, lambda kc: gains[:, l * 4 + 2, kc:kc + 1], hn, sq, ps1, rsp, onesb, D)
            for qd in range(4):
                def epi_up(oc, ps, w=w):
                    r = rl.next()
                    c.act(r, r[:, :w], ps, ps[:, :w], AF.Relu)
                    c.tt("pool", actb, actb[:, oc, :w], r, r[:, :w], r, r[:, :w], ALU.mult)

                self.linear(Wu, 0, 16, qd * 2048, qd * 2048 + 2048, 512, hn, lambda kc, w=w: hn[:, kc, :w],
                            w, wpl, psp, epi_up)

                def epi_dn(oc, ps, w=w, qd=qd):
                    if qd == 0:
                        c.cp("act", ff, ff[:, oc, :w], ps, ps[:, :w])
                    else:
                        c.tt("dve", ff, ff[:, oc, :w], ps, ps[:, :w], ff, ff[:, oc, :w], ALU.add)

                self.linear(Wd, qd * 2048, 16, 0, D, 512, actb, lambda kc, w=w: actb[:, kc, :w],
                            w, wpl, psp, epi_dn)
            self.postnorm_residual(ff, w, t0, gains, l * 4 + 3, h32, sq, ps1, rsp, onesb)
        c.end_phase()

    def rope_epi(self, ps, w, t0, cosT, sinT, RT, xsp, t1p, obp, ps1, dst_rows, scale):
        c = self.c
        xs = xsp.next()
        c.cp("act", xs, xs[:, :w], ps, ps[:, :w])
        pr = ps1.next()
        c.mm(pr, pr[:, :w], RT, RT[:, :], xs, xs[:, :w])
        t1 = t1p.next()
        c.tt("dve", t1, t1[:, :w], pr, pr[:, :w], sinT, sinT[:, t0:t0 + w], ALU.mult)
        c.tt("pool", xs, xs[:, :w], xs, xs[:, :w], cosT, cosT[:, t0:t0 + w], ALU.mult)
        ob = obp.next()
        c.tt("dve", t1, t1[:, :w], t1, t1[:, :w], xs, xs[:, :w], ALU.add)
        c.ts("dve", ob, ob[:, :w], t1, t1[:, :w], scale, ALU.mult)
        c.dma("pool", dst_rows[:, t0:t0 + w], ob[:, :w], in_t=ob)

    def phase_qkproj(self, l, mode):
        c = self.c
        c.begin_phase()
        gains = self.load_const(self.gains, [128, 16, 16], F32, "gains")
        kvg = self.load_const(self.kvg, [128, 16], F32, "kvg")
        onesb = self.load_const(self.k_onesb, [128, 128], BF16, "onesb")
        RT = self.load_const(self.k_RT, [128, 128], F32, "RT")
        cosT = self.load_const(self.k_cos, [128, self.Tp], F32, "cosT")
        sinT = self.load_const(self.k_sin, [128, self.Tp], F32, "sinT")
        h32p = c.sbpool(1, [128, 16, 512], F32, "h32")
        hn = c.sb([128, 16, 512], BF16, "hn")
        sq = c.sbpool(4, [128, 512], BF16, "sq")
        wpool = c.sbpool(3, [128, 16, 512], BF16, "wblk")
        psp = c.pspool(4, [128, 512], F32, "psl")
        ps1 = c.pspool(2, [128, 512], F32, "ps1")
        rsp = c.sbpool(2, [128, 512], F32, "rs")
        xsp = c.sbpool(3, [128, 512], F32, "xs")
        t1p = c.sbpool(3, [128, 512], F32, "t1")
        obp = c.sbpool(4, [128, 512], BF16, "ob")
        if mode == "kv":
            W, dst, gt, gfn, scale = self.wbf["kv"], self.a_k, kvg, (lambda kc: kvg[:, kc:kc + 1]), 1.0
        else:
            W, dst, gt, gfn, scale = self.wbf["q"][l - 2], self.a_q, gains, (lambda kc: gains[:, l * 4, kc:kc + 1]), 128.0 ** -0.5
        for (t0, w) in token_tiles(self.Tp):
            h32 = h32p.next()
            c.dma("sp", h32[:, :, :w], self.hT[:, t0:t0 + w].rearrange("(kc p) t -> p kc t", p=128), out_t=h32)
            self.rmsnorm_fm(h32, 16, w, gt, gfn, hn, sq, ps1, rsp, onesb, D)

            def epi(oc, ps, w=w, t0=t0):
                self.rope_epi(ps, w, t0, cosT, sinT, RT, xsp, t1p, obp, ps1, dst[oc * 128:(oc + 1) * 128, :], scale)

            self.linear(W, 0, 16, 0, D, 512, hn, lambda kc, w=w: hn[:, kc, :w], w, wpool, psp, epi)
            if mode == "kv":
                for nb in range(4):
                    wt = wpool.next()
                    c.dma("sp", wt[:, :, :], W[:, D + nb * 512:D + (nb + 1) * 512].rearrange("(kc p) n -> p kc n", p=128),
                          out_t=wt)
                    for sbk in range(w // 128):
                        ps = psp.next()
                        for kc in range(16):
                            c.mm(ps, ps[:, :], hn, hn[:, kc, sbk * 128:(sbk + 1) * 128], wt, wt[:, kc, :],
                                 start=(kc == 0), stop=(kc == 15))
                        ob = obp.next()
                        c.cp("act", ob, ob[:, :], ps, ps[:, :])
                        tt0 = t0 + sbk * 128
                        c.dma("pool", self.a_v[tt0:tt0 + 128, nb * 512:(nb + 1) * 512], ob[:, :], in_t=ob)
        c.end_phase()

    def phase_attn(self, l):
        c = self.c
        Tp, NB, T = self.Tp, self.NC, self.T
        j_ = l - 2
        lambda_init = 0.8 - 0.6 * math.exp(-0.3 * l)
        c.begin_phase()
        identb = self.load_const(self.k_identb, [128, 128], BF16, "identb")
        onesb = self.load_const(self.k_onesb, [128, 128], BF16, "onesb")
        md = self.load_const(self.k_md, [128, 4, 512], F32, "md")
        lamt = self.load_const(self.lam[:, j_], [128, 4, 128], F32, "lam")
        subg = self.load_const(self.subln[:, j_], [128, 256], F32, "subg")
        c.ts("dve", subg, subg[:, :], subg, subg[:, :], 1.0 - lambda_init, ALU.mult)
        lp = c.sb([128, 2, 128], F32, "lp")
        c.tt("dve", lp, lp[:, 0, :], lamt, lamt[:, 0, :], lamt, lamt[:, 1, :], ALU.mult)
        c.tt("dve", lp, lp[:, 1, :], lamt, lamt[:, 2, :], lamt, lamt[:, 3, :], ALU.mult)
        ls = c.sb([128, 2], F32, "ls")
        c.op("dve", lambda en: en.tensor_reduce(out=ls[:, :], in_=lp[:, :, :], axis=AX.X, op=ALU.add), [ls], [lp])
        c.act(ls, ls[:, :], ls, ls[:, :], AF.Exp)
        neglam = c.sb([128, 1], F32, "neglam")
        c.tt("dve", neglam, neglam[:, :], ls, ls[:, 1:2], ls, ls[:, 0:1], ALU.subtract)
        c.ts("dve", neglam, neglam[:, :], neglam, neglam[:, :], -lambda_init, ALU.add)

        ktp = c.sbpool(2, [128, 2, Tp], BF16, "kt")
        vxp = c.sbpool(2, [128, NB, 257], BF16, "vx")
        for t in vxp.tiles:
            c.memset("pool", t, t[:, :, 256:257], 1.0)
        qtp = c.sbpool(2, [128, 2, 512], BF16, "qt")
        sqp = c.sbpool(2, [128, 512], BF16, "sqq")
        km2 = c.sb([128, 2], F32, "km2")
        kmt = c.sbpool(2, [128, 1], F32, "kmt")
        negBp = c.sbpool(3, [128, 512], F32, "negB")
        tmpp = c.sbpool(3, [128, 512], F32, "tmp")
        ptp = c.sbpool(3, [128, 512], BF16, "pt")
        on0p = c.sbpool(2, [128, 4, 256], F32, "on0")
        on1p = c.sbpool(2, [128, 256], F32, "on1")
        junk = c.sbpool(2, [128, 256], F32, "junk")
        colp = c.sbpool(4, [128, 1], F32, "col")
        osp = c.sbpool(2, [128, 256], BF16, "osn")
        ontp = c.sbpool(2, [128, 2, 128], BF16, "ont")
        pss = c.pspool(2, [128, 512], F32, "pss")
        pso = c.pspool(4, [128, 257], F32, "pso")
        psq = c.pspool(1, [128, 512], F32, "psq")
        pst = c.pspool(1, [128, 2, 128], BF16, "pst")

        for h in range(8):
            kt = ktp.next(); vx = vxp.next()
            c.dma("sp", kt[:, :, :], self.a_k[h * 256:(h + 1) * 256, :].rearrange("(m p) t -> p m t", p=128), out_t=kt)
            for n8 in range(0, NB, 8):
                n9 = min(NB, n8 + 8)
                c.dma("sp", vx[:, n8:n9, 0:256],
                      self.a_v[n8 * 128:n9 * 128, h * 256:(h + 1) * 256].rearrange("(nb p) f -> p nb f", p=128), out_t=vx)
            for m in range(2):
                first = True
                for (t0, w) in token_tiles(Tp):
                    s2 = sqp.next()
                    c.tt("pool", s2, s2[:, :w], kt, kt[:, m, t0:t0 + w], kt, kt[:, m, t0:t0 + w], ALU.mult)
                    pq = psq.next()
                    c.mm(pq, pq[:, :w], onesb, onesb[:, :], s2, s2[:, :w])
                    if first:
                        c.op("dve", lambda en, pq=pq, w=w, m=m: en.tensor_reduce(out=km2[:, m:m + 1], in_=pq[:, :w], axis=AX.X, op=ALU.max),
                             [km2], [pq])
                        first = False
                    else:
                        k1 = kmt.next()
                        c.op("dve", lambda en, pq=pq, w=w, k1=k1: en.tensor_reduce(out=k1[:, :], in_=pq[:, :w], axis=AX.X, op=ALU.max),
                             [k1], [pq])
                        c.tt("dve", km2, km2[:, m:m + 1], km2, km2[:, m:m + 1], k1, k1[:, :], ALU.max)
            c.ts("dve", km2, km2[:, :], km2, km2[:, :], 1.05, ALU.mult)
            for qi, (t0, w) in enumerate(token_tiles(Tp)):
                if t0 >= T:
                    continue
                nqs = w // 128
                qt = qtp.next()
                c.dma("sp", qt[:, :, :w], self.a_q[h * 256:(h + 1) * 256, t0:t0 + w].rearrange("(m p) t -> p m t", p=128),
                      out_t=qt)
                on0 = on0p.next()
                for m in range(2):
                    s2 = sqp.next()
                    c.tt("pool", s2, s2[:, :w], qt, qt[:, m, :w], qt, qt[:, m, :w], ALU.mult)
                    pq = psq.next()
                    c.mm(pq, pq[:, :w], onesb, onesb[:, :], s2, s2[:, :w])
                    negB = negBp.next()
                    c.act(negB, negB[:, :w], pq, pq[:, :w], AF.Sqrt, scale=km2[:, m:m + 1], extra_in=[km2])
                    c.ts("dve", negB, negB[:, :w], negB, negB[:, :w], -1.0, ALU.mult)
                    accs = [pso.next() for _ in range(nqs)]
                    kb_last = t0 // 128 + nqs - 1
                    for kb in range(kb_last + 1):
                        d = kb - t0 // 128
                        ps = pss.next()
                        c.mm(ps, ps[:, :w], kt, kt[:, m, kb * 128:(kb + 1) * 128], qt, qt[:, m, :w])
                        tmp = tmpp.next()
                        c.tt("dve", tmp, tmp[:, :w], ps, ps[:, :w], negB, negB[:, :w], ALU.add)
                        if d >= 0:
                            c.tt("pool", tmp, tmp[:, :w], tmp, tmp[:, :w], md, md[:, d, :w], ALU.add)
                        pt = ptp.next()
                        c.act(pt, pt[:, :w], tmp, tmp[:, :w], AF.Exp)
                        for qs in range(nqs):
                            if d > qs:
                                continue
                            c.mm(accs[qs], accs[qs][:, :], pt, pt[:, qs * 128:(qs + 1) * 128], vx, vx[:, kb, :],
                                 start=(kb == 0), stop=(d == qs))
                    for qs in range(nqs):
                        a = accs[qs]
                        rl_ = colp.next()
                        c.recip(rl_, rl_[:, :], a, a[:, 256:257])
                        if m == 0:
                            c.ts("dve", on0, on0[:, qs, :], a, a[:, 0:256], rl_[:, 0:1], ALU.mult, s_t=[rl_])
                        else:
                            on1 = on1p.next()
                            c.ts("dve", on1, on1[:, :], a, a[:, 0:256], rl_[:, 0:1], ALU.mult, s_t=[rl_])
                            c.stt("dve", on1, on1[:, :], on1, on1[:, :], neglam[:, 0:1], on0, on0[:, qs, :],
                                  ALU.mult, ALU.add, s_t=[neglam])
                            jk = junk.next(); ssq = colp.next()
                            c.memset("dve", ssq, ssq[:, :], 0.0)
                            c.act(jk, jk[:, :], on1, on1[:, :], AF.Square, accum=ssq[:, :], accum_t=ssq)
                            c.act(ssq, ssq[:, :], ssq, ssq[:, :], AF.Sqrt, bias=EPS, scale=1.0 / 256)
                            c.recip(ssq, ssq[:, :], ssq, ssq[:, :])
                            osn = osp.next()
                            c.stt("dve", osn, osn[:, :], on1, on1[:, :], ssq[:, 0:1], subg, subg[:, :],
                                  ALU.mult, ALU.mult, s_t=[ssq])
                            ptr = pst.next()
                            for e2 in range(2):
                                c.tr(ptr, ptr[:, e2, :], osn, osn[:, e2 * 128:(e2 + 1) * 128], identb, identb[:, :])
                            ont = ontp.next()
                            c.cp("act", ont, ont[:, :, :], ptr, ptr[:, :, :])
                            tq = t0 + qs * 128
                            c.dma("pool", self.a_on[h * 256:(h + 1) * 256, tq:tq + 128].rearrange("(e p) t -> p e t", p=128),
                                  ont[:, :, :], in_t=ont)
        c.end_phase()

    def phase_final(self):
        c = self.c
        c.barrier()

    def build(self, stop=None):
        self.declare()
        self.phase_init()
        for l in range(self.n_layers):
            if l < 2:
                self.phase_g1(l)
                if stop == "g1":
                    break
                self.phase_g2(l)
                if stop == "g2":
                    break
                self.phase_outproj(l, self.s_og, 32, self.wbf["gout"][l])
                if stop == "g3":
                    break
            else:
                if l == 2:
                    self.phase_qkproj(l, "kv")
                self.phase_qkproj(l, "q")
                self.phase_attn(l)
                self.phase_outproj(l, self.a_on, 16, self.wbf["o"][l - 2])
            self.phase_mlp(l)
        self.phase_final()
        return self.nc


def host_consts(Tp):
    bf = ml_dtypes.bfloat16
    p = np.arange(128)[:, None]
    f = np.arange(128)[None, :]
    cst = {}
    cst["identb"] = np.eye(128, dtype=np.float32).astype(bf)
    cst["onesb"] = np.ones((128, 128), np.float32).astype(bf)
    cst["onesf"] = np.ones((128, 128), np.float32)
    cst["identf"] = np.eye(128, dtype=np.float32)
    sel = np.zeros((32, 32, 128), np.float32)
    for hh in range(32):
        sel[hh, hh, :] = 1.0
    cst["sel"] = sel
    ii = np.arange(128)[:, None]; jj = np.arange(128)[None, :]
    m4 = np.zeros((128, 8, 4, 128), np.float32)
    m4[:, 0] = np.eye(128, dtype=np.float32)[:, None, :]
    def M(sv):
        n = 2 ** sv
        return ((ii // (2 * n) == jj // (2 * n)) & (ii % (2 * n) >= n) & (jj % (2 * n) < n)).astype(np.float32)
    m4[:, 1] = M(0).T[:, None, :]
    for sv in range(1, 7):
        m4[:, 1 + sv] = M(sv)[:, None, :]
    cst["m4"] = m4.astype(bf)
    cst["Umat"] = (p <= f).astype(np.float32)
    cst["negU"] = np.where(f >= p, 0.0, NEG).astype(np.float32)
    cst["M2"] = np.where(p > f, 0.0, NEG).astype(np.float32)
    R = np.zeros((128, 128), np.float32)
    for m in range(16):
        R[m, m + 16] = -1.0
        R[m + 16, m] = 1.0
    cst["RT"] = np.ascontiguousarray(R.T)
    pos = np.arange(Tp, dtype=np.float32)
    inv = (500000.0 ** (-np.arange(0, 32, 2, dtype=np.float32) / 32)).astype(np.float32)
    ang = pos[None, :] * inv[:, None]
    cosT = np.ones((128, Tp), np.float32)
    sinT = np.zeros((128, Tp), np.float32)
    cosT[0:16] = np.cos(ang); cosT[16:32] = np.cos(ang)
    sinT[0:16] = np.sin(ang); sinT[16:32] = np.sin(ang)
    cst["cosT"], cst["sinT"] = cosT, sinT
    md = np.zeros((128, 4, 512), np.float32)
    ff = np.arange(512)[None, :]
    for d in range(4):
        md[:, d, :] = np.where(d * 128 + p <= ff, 0.0, NEG)
    cst["maskd"] = md
    return cst


def rep(a, n=128):
    return np.ascontiguousarray(np.broadcast_to(a[None], (n,) + a.shape)).astype(np.float32)


def host_params(inp):
    ng = np.asarray(inp["norm_gains"], np.float32)
    d = {}
    d["gains"] = np.ascontiguousarray(ng.reshape(16, 16, 128).transpose(2, 0, 1))
    d["kvg"] = np.ascontiguousarray(np.asarray(inp["kv_norm"], np.float32).reshape(16, 128).T)
    cw = np.asarray(inp["gdn_conv_w"], np.float32)
    d["convw"] = np.ascontiguousarray(cw.reshape(2, 4, 64, 128).transpose(3, 0, 1, 2))
    d["alog"] = rep(np.asarray(inp["gdn_a_log"], np.float32))
    d["dtb"] = rep(np.asarray(inp["gdn_dt_bias"], np.float32))
    d["onorm"] = np.ascontiguousarray(np.asarray(inp["gdn_o_norm"], np.float32).T)
    d["lam"] = rep(np.asarray(inp["diff_lambda"], np.float32))
    d["subln"] = rep(np.asarray(inp["diff_subln"], np.float32))
    return d


_CACHE = {}


def kernel(**inputs):
    x = np.asarray(inputs["x"], np.float32)
    B, S, _ = x.shape
    T = NMETA + S
    Tp = ((T + 127) // 128) * 128
    key = (T,)
    if key not in _CACHE:
        _CACHE[key] = Builder(T).build()
    nc = _CACHE[key]
    cst = host_consts(Tp)
    prm = host_params(inputs)
    meta = np.asarray(inputs["meta_tokens"], np.float32)
    shared = dict(cst)
    shared.update(prm)
    for k in ("mlp_w_up", "mlp_w_down", "gdn_w_in", "gdn_w_out", "w_kv", "diff_w_q", "diff_w_o"):
        shared[k] = np.ascontiguousarray(np.asarray(inputs[k], np.float32))
    in_maps = []
    for core in range(8):
        b = core % B
        h0 = np.zeros((D, Tp), np.float32)
        h0[:, :NMETA] = meta.T
        h0[:, NMETA:T] = x[b].T
        m = dict(shared)
        m["h0"] = h0
        in_maps.append(m)
    res = run_bass_kernel_spmd(nc, in_maps, core_ids=list(range(8)))
    out = np.empty((B, S, D), np.float32)
    for b in range(B):
        hT = res.results[b]["hT"]
        out[b] = hT[:, NMETA:T].T
    return out
```

```python
import math, os
from contextlib import ExitStack
import numpy as np
import ml_dtypes
import concourse.bass as bass
import concourse.mybir as mybir
from concourse.bass_utils import run_bass_kernel_spmd

F32, BF16 = mybir.dt.float32, mybir.dt.bfloat16
AF = mybir.ActivationFunctionType
ALU = mybir.AluOpType
AX = mybir.AxisListType

D = 2048
DFF = 8192
NMETA = 16
GIN = 12352
EPS = 1e-6
NEG = -30000.0


class Tl:
    __slots__ = ("ap", "w", "r", "dsem", "name", "psum")

    def __init__(self, ap, name=""):
        self.ap = ap
        self.w = {}
        self.r = {}
        self.dsem = {}
        self.name = name
        self.psum = False

    def __getitem__(self, k):
        return self.ap[k]


class Pool_:
    def __init__(self, tiles):
        self.tiles = tiles
        self.i = 0

    def next(self):
        t = self.tiles[self.i % len(self.tiles)]
        self.i += 1
        return t


class Ctx:
    def __init__(self, nc):
        self.nc = nc
        self.es = ExitStack()
        self.eng = {"pe": nc.tensor, "act": nc.scalar, "dve": nc.vector, "pool": nc.gpsimd, "sp": nc.sync}
        self.sem = {k: self.es.enter_context(nc.semaphore("s_" + k)) for k in ("pe", "act", "dve", "pool")}
        self.cnt = {k: 0 for k in self.sem}
        self.waited = {}
        self.dsems = []
        self.dq = {"sp": [], "pool": []}
        self.dnext = {"sp": 0, "pool": 0}
        self.uid = 0
        self.phase_es = None

    def begin_phase(self):
        self.phase_es = ExitStack()
        self.dnext = {"sp": 0, "pool": 0}

    def end_phase(self):
        self.barrier()
        self.phase_es.close()
        self.phase_es = None

    def sb(self, shape, dtype, name="t"):
        self.uid += 1
        h = self.phase_es.enter_context(self.nc.sbuf_tensor(f"{name}_{self.uid}", list(shape), dtype))
        return Tl(h[tuple(slice(None) for _ in shape)], name)

    def ps(self, shape, dtype, name="p"):
        self.uid += 1
        nb = 512 if dtype == F32 else 1024
        h = self.phase_es.enter_context(self.nc.psum_tensor(f"{name}_{self.uid}", [shape[0], nb], dtype))
        n = 1
        for d in shape[1:]:
            n *= d
        assert n <= nb
        v = h[:, 0:n]
        if len(shape) == 3:
            v = v.rearrange("p (a b) -> p a b", a=shape[1])
        t = Tl(v, name)
        t.psum = True
        return t

    def sbpool(self, n, shape, dtype, name="t"):
        return Pool_([self.sb(shape, dtype, name) for _ in range(n)])

    def pspool(self, n, shape, dtype, name="p"):
        return Pool_([self.ps(shape, dtype, name) for _ in range(n)])

    def _dsem(self, t, q):
        if q not in t.dsem:
            lst = self.dq[q]
            if self.dnext[q] >= len(lst):
                s = self.es.enter_context(self.nc.semaphore(f"d{q}_{len(lst)}"))
                ent = [s, 0]
                lst.append(ent)
                self.dsems.append(ent)
            t.dsem[q] = lst[self.dnext[q]]
            self.dnext[q] += 1
        return t.dsem[q]

    def _wait(self, e, ev):
        key, sem, val = ev
        if e == "pe" and key == "pe":
            return
        k = (e, key)
        if self.waited.get(k, 0) >= val:
            return
        self.eng[e].wait_ge(sem, val)
        self.waited[k] = val

    def _deps(self, e, outs, ins):
        for t in ins:
            if t is None:
                continue
            for ev in t.w.values():
                self._wait(e, ev)
            if t.psum:
                for ev in list(t.r.values()):
                    if ev[0] != e:
                        self._wait(e, ev)
        for t in outs:
            if t is None:
                continue
            for ev in t.w.values():
                self._wait(e, ev)
            for ev in t.r.values():
                self._wait(e, ev)

    def _mark(self, ev, outs, ins):
        for t in ins:
            if t is not None:
                t.r[ev[0]] = ev
        for t in outs:
            if t is not None:
                t.w = {ev[0]: ev}
                t.r = {}

    def op(self, e, inst_fn, outs, ins):
        self._deps(e, outs, ins)
        inst = inst_fn(self.eng[e])
        self.cnt[e] += 1
        inst.then_inc(self.sem[e], 1)
        self._mark((e, self.sem[e], self.cnt[e]), outs, ins)

    def dma(self, q, out_ap, in_ap, out_t=None, in_t=None):
        self._deps(q, [out_t], [in_t])
        owner = out_t if out_t is not None else in_t
        ds = self._dsem(owner, q)
        inst = self.eng[q].dma_start(out=out_ap, in_=in_ap)
        ds[1] += 16
        inst.then_inc(ds[0], 16)
        self._mark((id(ds), ds[0], ds[1]), [out_t], [in_t])

    def barrier(self):
        for e in ("pe", "act", "dve", "pool", "sp"):
            for k in self.sem:
                if k != e and self.cnt[k] > 0:
                    self._wait(e, (k, self.sem[k], self.cnt[k]))
            for ds in self.dsems:
                if ds[1] > 0:
                    self._wait(e, (id(ds), ds[0], ds[1]))

    def mm(self, out_t, out_ap, lhsT_t, lhsT_ap, rhs_t, rhs_ap, start=True, stop=True):
        self.op("pe", lambda en: en.matmul(out_ap, lhsT=lhsT_ap, rhs=rhs_ap, start=start, stop=stop),
                [out_t], [lhsT_t, rhs_t])

    def tr(self, out_t, out_ap, in_t, in_ap, id_t, id_ap):
        self.op("pe", lambda en: en.transpose(out_ap, in_ap, id_ap), [out_t], [in_t, id_t])

    def act(self, out_t, out_ap, in_t, in_ap, func, bias=0.0, scale=1.0, extra_in=(), accum=None, accum_t=None):
        kw = {}
        if accum is not None:
            kw["accum_out"] = accum
        self.op("act", lambda en: en.activation(out=out_ap, in_=in_ap, func=func, bias=bias, scale=scale, **kw),
                [out_t] + ([accum_t] if accum_t is not None else []), [in_t] + list(extra_in))

    def tt(self, e, out_t, out_ap, a_t, a_ap, b_t, b_ap, op):
        self.op(e, lambda en: en.tensor_tensor(out=out_ap, in0=a_ap, in1=b_ap, op=op), [out_t], [a_t, b_t])

    def ts(self, e, out_t, out_ap, a_t, a_ap, s1, op0, s2=None, op1=None, s_t=()):
        if op1 is None:
            f = lambda en: en.tensor_scalar(out=out_ap, in0=a_ap, scalar1=s1, scalar2=None, op0=op0)
        else:
            f = lambda en: en.tensor_scalar(out=out_ap, in0=a_ap, scalar1=s1, scalar2=s2, op0=op0, op1=op1)
        self.op(e, f, [out_t], [a_t] + list(s_t))

    def stt(self, e, out_t, out_ap, a_t, a_ap, sc, b_t, b_ap, op0, op1, s_t=()):
        e = "dve"
        self.op(e, lambda en: en.scalar_tensor_tensor(out=out_ap, in0=a_ap, scalar=sc, in1=b_ap, op0=op0, op1=op1),
                [out_t], [a_t, b_t] + list(s_t))

    def cp(self, e, out_t, out_ap, in_t, in_ap):
        if e == "act":
            self.op(e, lambda en: en.copy(out=out_ap, in_=in_ap), [out_t], [in_t])
        else:
            self.op(e, lambda en: en.tensor_copy(out=out_ap, in_=in_ap), [out_t], [in_t])

    def memset(self, e, t, ap, val):
        self.op(e, lambda en: en.memset(ap, val), [t], [])

    def recip(self, out_t, out_ap, in_t, in_ap):
        self.op("dve", lambda en: en.reciprocal(out=out_ap, in_=in_ap), [out_t], [in_t])


def token_tiles(Tp):
    res = []
    t = 0
    while t < Tp:
        w = min(512, Tp - t)
        res.append((t, w))
        t += w
    return res


class Builder:
    def __init__(self, T, n_layers=4, debug=False):
        self.T = T
        self.Tp = ((T + 127) // 128) * 128
        self.NC = self.Tp // 128
        self.n_layers = n_layers
        self.nc = bass.Bass("TRN2", target_bir_lowering=False)
        self.c = Ctx(self.nc)
        self.debug = debug

    def declare(self):
        nc, Tp = self.nc, self.Tp

        def inp(name, shape, dt=F32):
            return nc.dram_tensor(name, list(shape), dt, kind="ExternalInput").ap()

        def scr(name, shape, dt):
            kind = "ExternalOutput" if (self.debug and name.startswith("s_")) else "Internal"
            return nc.dram_tensor(name, list(shape), dt, kind=kind).ap()

        self.h0 = inp("h0", [D, Tp])
        self.w32 = {
            "up": inp("mlp_w_up", [4, D, DFF]), "down": inp("mlp_w_down", [4, DFF, D]),
            "gin": inp("gdn_w_in", [2, D, GIN]), "gout": inp("gdn_w_out", [2, 4096, D]),
            "kv": inp("w_kv", [D, 4096]), "q": inp("diff_w_q", [2, D, D]), "o": inp("diff_w_o", [2, D, D]),
        }
        self.wbf = {k: scr(k + "_bf", v.shape, BF16) for k, v in self.w32.items()}
        self.gains = inp("gains", [128, 16, 16])
        self.kvg = inp("kvg", [128, 16])
        self.convw = inp("convw", [128, 2, 4, 64])
        self.alog = inp("alog", [128, 2, 32])
        self.dtb = inp("dtb", [128, 2, 32])
        self.onorm = inp("onorm", [128, 2])
        self.lam = inp("lam", [128, 2, 4, 128])
        self.subln = inp("subln", [128, 2, 256])
        self.k_identb = inp("identb", [128, 128], BF16)
        self.k_onesb = inp("onesb", [128, 128], BF16)
        self.k_onesf = inp("onesf", [128, 128])
        self.k_U = inp("Umat", [128, 128])
        self.k_sel = inp("sel", [32, 32, 128])
        self.k_identf = inp("identf", [128, 128])
        self.k_m4 = inp("m4", [128, 8, 4, 128], BF16)
        self.k_negU = inp("negU", [128, 128])
        self.k_M2 = inp("M2", [128, 128])
        self.k_RT = inp("RT", [128, 128])
        self.k_cos = inp("cosT", [128, Tp])
        self.k_sin = inp("sinT", [128, Tp])
        self.k_md = inp("maskd", [128, 4, 512])
        self.hT = nc.dram_tensor("hT", [D, Tp], F32, kind="ExternalOutput").ap()
        self.s_q = scr("s_q", [D, Tp], BF16)
        self.s_k = scr("s_k", [D, Tp], BF16)
        self.s_v = scr("s_v", [4096, Tp], BF16)
        self.s_z = scr("s_z", [4096, Tp], BF16)
        self.s_bg = scr("s_bg", [Tp, 64], F32)
        self.s_og = scr("s_og", [4096, Tp], BF16)
        self.a_k = scr("a_k", [D, Tp], BF16)
        self.a_v = scr("a_v", [Tp, D], BF16)
        self.a_q = scr("a_q", [D, Tp], BF16)
        self.a_on = scr("a_on", [D, Tp], BF16)

    def phase_init(self):
        c = self.c
        c.begin_phase()
        dummy = c.sb([128, 1], F32, "dummy")
        order = []
        for l in range(2):
            order += [("gin", l), ("gout", l), ("up", l), ("down", l)]
        order += [("kv", None)]
        for l in range(2):
            order += [("q", l), ("o", l), ("up", 2 + l), ("down", 2 + l)]
        for name, l in order:
            src = self.w32[name] if l is None else self.w32[name][l]
            dst = self.wbf[name] if l is None else self.wbf[name][l]
            K = src.shape[0]
            for r0 in range(0, K, 128):
                c.dma("pool", dst[r0:r0 + 128, :], src[r0:r0 + 128, :], in_t=dummy)
        for r0 in range(0, D, 128):
            c.dma("sp", self.hT[r0:r0 + 128, :], self.h0[r0:r0 + 128, :], in_t=dummy)
        c.end_phase()

    def load_const(self, ap_dram, shape, dt, name):
        c = self.c
        t = c.sb(shape, dt, name)
        c.dma("sp", t.ap, ap_dram, out_t=t)
        return t

    def rmsnorm_fm(self, x, nkc, w, gain_t, gain_ap_fn, out_hn, sq, ps_pool, rs_pool, onesb, nfeat):
        c = self.c
        ps = ps_pool.next()
        for kc in range(nkc):
            s1 = sq.next()
            c.act(s1, s1[:, :w], x, x[:, kc, :w], AF.Square)
            c.mm(ps, ps[:, :w], onesb, onesb[:, :], s1, s1[:, :w], start=(kc == 0), stop=(kc == nkc - 1))
        rs = rs_pool.next()
        c.act(rs, rs[:, :w], ps, ps[:, :w], AF.Sqrt, bias=EPS, scale=1.0 / nfeat)
        c.recip(rs, rs[:, :w], rs, rs[:, :w])
        if out_hn is not None:
            for kc in range(nkc):
                e = "dve" if kc % 2 == 0 else "pool"
                c.stt(e, out_hn, out_hn[:, kc, :w], x, x[:, kc, :w], gain_ap_fn(kc), rs, rs[:, :w],
                      ALU.mult, ALU.mult, s_t=[gain_t])
        return rs

    def linear(self, W, k0, KC, n0, n1, bc, x_t, x_fn, w, wpool, pspool, epi):
        c = self.c
        for nb in range(n0, n1, bc):
            cols = min(bc, n1 - nb)
            wt = wpool.next()
            for k8 in range(0, KC, 8):
                c.dma("sp", wt[:, k8:k8 + 8, 0:cols],
                      W[k0 + k8 * 128:k0 + (k8 + 8) * 128, nb:nb + cols].rearrange("(kc p) n -> p kc n", p=128), out_t=wt)
            for j in range(cols // 128):
                ps = pspool.next()
                for kc in range(KC):
                    c.mm(ps, ps[:, :w], wt, wt[:, kc, j * 128:(j + 1) * 128], x_t, x_fn(kc),
                         start=(kc == 0), stop=(kc == KC - 1))
                epi((nb - n0) // 128 + j, ps)

    def postnorm_residual(self, y, w, t0, gains, gidx, h32, sq, ps1, rsp, onesb):
        c = self.c
        rs = self.rmsnorm_fm(y, 16, w, None, None, None, sq, ps1, rsp, onesb, D)
        for kc in range(16):
            e = "dve" if kc % 2 == 0 else "pool"
            c.stt(e, y, y[:, kc, :w], y, y[:, kc, :w], gains[:, gidx, kc:kc + 1], rs, rs[:, :w],
                  ALU.mult, ALU.mult, s_t=[gains])
            c.tt(e, h32, h32[:, kc, :w], y, y[:, kc, :w], h32, h32[:, kc, :w], ALU.add)
        c.dma("pool", self.hT[:, t0:t0 + w].rearrange("(kc p) t -> p kc t", p=128), h32[:, :, :w], in_t=h32)

    def phase_g1(self, l):
        c = self.c
        Tp = self.Tp
        c.begin_phase()
        gains = self.load_const(self.gains, [128, 16, 16], F32, "gains")
        onesb = self.load_const(self.k_onesb, [128, 128], BF16, "onesb")
        cw = self.load_const(self.convw[:, l], [128, 4, 64], F32, "cw")
        alog = self.load_const(self.alog[:, l], [128, 32], F32, "alog")
        dtb = self.load_const(self.dtb[:, l], [128, 32], F32, "dtb")
        negea = c.sb([128, 32], F32, "negea")
        c.act(negea, negea[:, :], alog, alog[:, :], AF.Exp)
        c.ts("dve", negea, negea[:, :], negea, negea[:, :], -1.0, ALU.mult)
        W = self.wbf["gin"][l]
        wlast = c.sb([128, 16, 64], BF16, "wlast")
        c.dma("sp", wlast[:, :, :], W[:, 12288:12352].rearrange("(kc p) n -> p kc n", p=128), out_t=wlast)
        halo = c.sb([128, 64, 3], F32, "halo")
        c.memset("pool", halo, halo[:, :, :], 0.0)
        h32p = c.sbpool(1, [128, 16, 512], F32, "h32")
        hn = c.sb([128, 16, 512], BF16, "hn")
        sq = c.sbpool(4, [128, 512], BF16, "sq")
        wpool = c.sbpool(3, [128, 16, 512], BF16, "wblk")
        psp = c.pspool(4, [128, 512], F32, "psl")
        ps1 = c.pspool(2, [128, 512], F32, "ps1")
        psba = c.pspool(1, [128, 64], F32, "psba")
        rsp = c.sbpool(2, [128, 512], F32, "rs")
        xpp = c.sbpool(3, [128, 515], F32, "xp")
        yp = c.sbpool(3, [128, 512], F32, "y")
        sp_ = c.sbpool(3, [128, 512], F32, "s")
        sqp = c.sbpool(2, [128, 512], BF16, "sq1")
        r1p = c.sbpool(2, [128, 512], F32, "r1")
        obp = c.sbpool(4, [128, 512], BF16, "ob")
        bap = c.sbpool(2, [128, 64], F32, "ba")
        tmp32 = c.sbpool(4, [128, 32], F32, "tmp32")

        for (t0, w) in token_tiles(Tp):
            h32 = h32p.next()
            c.dma("sp", h32[:, :, :w], self.hT[:, t0:t0 + w].rearrange("(kc p) t -> p kc t", p=128), out_t=h32)
            self.rmsnorm_fm(h32, 16, w, gains, lambda kc: gains[:, l * 4 + 0, kc:kc + 1], hn, sq, ps1, rsp, onesb, D)

            def epi(oc, ps, t0=t0, w=w):
                if oc < 64:
                    xp = xpp.next()
                    c.cp("pool", xp, xp[:, 0:3], halo, halo[:, oc, :])
                    c.cp("act", xp, xp[:, 3:3 + w], ps, ps[:, :w])
                    c.cp("pool", halo, halo[:, oc, :], xp, xp[:, w:w + 3])
                    y = yp.next()
                    c.ts("dve", y, y[:, :w], xp, xp[:, 0:w], cw[:, 0, oc:oc + 1], ALU.mult, s_t=[cw])
                    for j in range(1, 4):
                        c.stt("dve", y, y[:, :w], xp, xp[:, j:j + w], cw[:, j, oc:oc + 1], y, y[:, :w],
                              ALU.mult, ALU.add, s_t=[cw])
                    if oc >= 32:
                        ob = obp.next()
                        c.act(ob, ob[:, :w], y, y[:, :w], AF.Silu)
                        r = (oc - 32) * 128
                        c.dma("pool", self.s_v[r:r + 128, t0:t0 + w], ob[:, :w], in_t=ob)
                    else:
                        s = sp_.next()
                        c.act(s, s[:, :w], y, y[:, :w], AF.Silu)
                        sq1 = sqp.next()
                        c.tt("pool", sq1, sq1[:, :w], s, s[:, :w], s, s[:, :w], ALU.mult)
                        p1 = ps1.next()
                        c.mm(p1, p1[:, :w], onesb, onesb[:, :], sq1, sq1[:, :w])
                        r1 = r1p.next()
                        c.act(r1, r1[:, :w], p1, p1[:, :w], AF.Sqrt, bias=EPS, scale=1.0)
                        c.recip(r1, r1[:, :w], r1, r1[:, :w])
                        ob = obp.next()
                        if oc < 16:
                            c.stt("dve", ob, ob[:, :w], s, s[:, :w], 128.0 ** -0.5, r1, r1[:, :w], ALU.mult, ALU.mult)
                            c.dma("pool", self.s_q[oc * 128:(oc + 1) * 128, t0:t0 + w], ob[:, :w], in_t=ob)
                        else:
                            c.tt("dve", ob, ob[:, :w], s, s[:, :w], r1, r1[:, :w], ALU.mult)
                            r = (oc - 16) * 128
                            c.dma("pool", self.s_k[r:r + 128, t0:t0 + w], ob[:, :w], in_t=ob)
                else:
                    ob = obp.next()
                    c.act(ob, ob[:, :w], ps, ps[:, :w], AF.Silu)
                    r = (oc - 64) * 128
                    c.dma("pool", self.s_z[r:r + 128, t0:t0 + w], ob[:, :w], in_t=ob)

            self.linear(W, 0, 16, 0, 12288, 512, hn, lambda kc, w=w: hn[:, kc, :w], w, wpool, psp, epi)
            for sbk in range(w // 128):
                pb = psba.next()
                for kc in range(16):
                    c.mm(pb, pb[:, :], hn, hn[:, kc, sbk * 128:(sbk + 1) * 128], wlast, wlast[:, kc, :],
                         start=(kc == 0), stop=(kc == 15))
                ba = bap.next()
                c.act(ba, ba[:, 0:32], pb, pb[:, 0:32], AF.Sigmoid)
                a = tmp32.next()
                c.tt("dve", a, a[:, :], pb, pb[:, 32:64], dtb, dtb[:, :], ALU.add)
                ab = tmp32.next()
                c.act(ab, ab[:, :], a, a[:, :], AF.Abs)
                c.act(ab, ab[:, :], ab, ab[:, :], AF.Exp, scale=-1.0)
                c.act(ab, ab[:, :], ab, ab[:, :], AF.Ln, bias=1.0)
                c.stt("dve", a, a[:, :], a, a[:, :], 0.0, ab, ab[:, :], ALU.max, ALU.add)
                c.tt("dve", ba, ba[:, 32:64], a, a[:, :], negea, negea[:, :], ALU.mult)
                tt0 = t0 + sbk * 128
                c.dma("pool", self.s_bg[tt0:tt0 + 128, :], ba[:, :], in_t=ba)
        c.end_phase()

    def phase_g2(self, l):
        c = self.c
        c.begin_phase()
        identb = self.load_const(self.k_identb, [128, 128], BF16, "identb")
        onesb = self.load_const(self.k_onesb, [128, 128], BF16, "onesb")
        onesf = self.load_const(self.k_onesf, [128, 128], F32, "onesf")
        Um = self.load_const(self.k_U, [128, 128], F32, "Um")
        negU = self.load_const(self.k_negU, [128, 128], F32, "negU")
        M2 = self.load_const(self.k_M2, [128, 128], F32, "M2")
        onorm = self.load_const(self.onorm, [128, 2], F32, "onorm")
        S32h = [c.sb([128, 4, 128], F32, "S32") for _ in range(8)]
        Sbfh = [c.sb([128, 4, 128], BF16, "Sbf") for _ in range(8)]
        for t in S32h + Sbfh:
            c.memset("pool", t, t[:, :, :], 0.0)
        qp = c.sbpool(2, [128, 16, 128], BF16, "qc")
        kp = c.sbpool(2, [128, 16, 128], BF16, "kc")
        vp = c.sbpool(2, [128, 32, 128], BF16, "vc")
        zp = c.sbpool(2, [128, 32, 128], BF16, "zc")
        bgp = c.sbpool(2, [128, 64], F32, "bg")
        gtp = c.sbpool(2, [32, 128], F32, "GT")
        sel = self.load_const(self.k_sel, [32, 32, 128], F32, "sel")
        identf = self.load_const(self.k_identf, [128, 128], F32, "identf")
        m4 = self.load_const(self.k_m4, [128, 8, 4, 128], BF16, "m4")
        small = {n: c.sbpool(2, [128, 32], F32, n) for n in ("G", "eG", "nbeG", "negb", "kdc", "eGl")}
        NSLOT = 3
        slots = []
        for _ in range(NSLOT):
            P = {n: c.sb([128, 4, 128], F32, n) for n in ("e1", "e2", "eGbc", "o32", "rs4")}
            P.update({n: c.sb([128, 4, 128], BF16, n) for n in
                      ("aqk", "qg", "TT", "kdec", "bv", "rb", "vn", "sq4", "og", "Pa", "X", "Ts")})
            P["amn"] = c.sb([128, 6, 4, 128], BF16, "amn")
            slots.append(P)
        psf = c.pspool(5, [128, 4, 128], F32, "psf")
        psb = c.pspool(2, [128, 8, 128], BF16, "psb")
        pss = c.pspool(1, [128, 512], F32, "pss")

        def group_gen(hg, P, t0, qc, kc_, vc, zc, bg, G, GT, nbeG, negb, kdc, eGl):
            hs = [4 * hg + j for j in range(4)]
            qhs = [2 * hg, 2 * hg + 1]
            e1, e2, eGbc = P["e1"], P["e2"], P["eGbc"]
            pG = psf.next()
            for j, h in enumerate(hs):
                c.mm(pG, pG[:, j, :], sel, sel[:, h, :], GT, GT[:, :])
            for j, h in enumerate(hs):
                c.stt("dve", e1, e1[:, j, :], pG, pG[:, j, :], G[:, h:h + 1], negU, negU[:, :],
                      ALU.subtract, ALU.add, s_t=[G])
                c.stt("dve", e2, e2[:, j, :], pG, pG[:, j, :], G[:, h:h + 1], M2, M2[:, :],
                      ALU.subtract, ALU.subtract, s_t=[G])
            c.act(eGbc, eGbc[:, :, :], pG, pG[:, :, :], AF.Exp)
            c.act(e1, e1[:, :, :], e1, e1[:, :, :], AF.Exp)
            c.act(e2, e2[:, :, :], e2, e2[:, :, :], AF.Exp, scale=-1.0)
            DmT, Dms = e1, e2
            yield
            pK = psf.next()
            for j, qh in enumerate(qhs):
                c.mm(pK, pK[:, j, :], kc_, kc_[:, qh, :], kc_, kc_[:, qh, :])
                c.mm(pK, pK[:, 2 + j, :], kc_, kc_[:, qh, :], qc, qc[:, qh, :])
            Pa, aqk, qg = P["Pa"], P["aqk"], P["qg"]
            for j, h in enumerate(hs):
                c.stt("dve", Pa, Pa[:, j, :], pK, pK[:, j // 2, :], negb[:, h:h + 1], Dms, Dms[:, j, :],
                      ALU.mult, ALU.mult, s_t=[negb])
                c.tt("dve", aqk, aqk[:, j, :], pK, pK[:, 2 + j // 2, :], DmT, DmT[:, j, :], ALU.mult)
                c.tt("pool", qg, qg[:, j, :], qc, qc[:, qhs[j // 2], :], eGbc, eGbc[:, j, :], ALU.mult)
            yield
            pT = psb.next()
            for j in range(4):
                c.tr(pT, pT[:, j, :], Pa, Pa[:, j, :], identb, identb[:, :])
            for j, qh in enumerate(qhs):
                c.tr(pT, pT[:, 4 + j, :], kc_, kc_[:, qh, :], identb, identb[:, :])
            pV = psb.next()
            for j, h in enumerate(hs):
                c.tr(pV, pV[:, j, :], vc, vc[:, h, :], identb, identb[:, :])
            TT, kdec, bv, amn = P["TT"], P["kdec"], P["bv"], P["amn"]
            c.tt("dve", TT, TT[:, :, :], pT, pT[:, 0:4, :], m4, m4[:, 1, :, :], ALU.mult)
            c.tt("dve", TT, TT[:, :, :], TT, TT[:, :, :], m4, m4[:, 0, :, :], ALU.add)
            for j, h in enumerate(hs):
                c.ts("dve", kdec, kdec[:, j, :], pT, pT[:, 4 + j // 2, :], kdc[:, h:h + 1], ALU.mult, s_t=[kdc])
                c.ts("dve", bv, bv[:, j, :], pV, pV[:, j, :], bg[:, h:h + 1], ALU.mult, s_t=[bg])
            for sv in range(6):
                c.tt("pool", amn, amn[:, sv, :, :], Pa, Pa[:, :, :], m4, m4[:, 2 + sv, :, :], ALU.mult)
            yield
            X, Ts = P["X"], P["Ts"]
            for sv in range(6):
                pX = psf.next()
                for j in range(4):
                    c.mm(pX, pX[:, j, :], amn, amn[:, sv, j, :], TT, TT[:, j, :])
                c.cp("act", X, X[:, :, :], pX, pX[:, :, :])
                pTr = psb.next()
                for j in range(4):
                    c.tr(pTr, pTr[:, j, :], TT, TT[:, j, :], identb, identb[:, :])
                c.cp("dve", Ts, Ts[:, :, :], pTr, pTr[:, 0:4, :])
                yield
                pD = psf.next()
                for j in range(4):
                    c.mm(pD, pD[:, j, :], Ts, Ts[:, j, :], X, X[:, j, :])
                c.tt("dve", TT, TT[:, :, :], pD, pD[:, :, :], TT, TT[:, :, :], ALU.add)
                yield
            rb, vn = P["rb"], P["vn"]
            pkS = psf.next()
            for j, h in enumerate(hs):
                c.mm(pkS, pkS[:, j, :], kc_, kc_[:, qhs[j // 2], :], Sbfh[hg], Sbfh[hg][:, j, :])
            for j, h in enumerate(hs):
                c.stt("dve", rb, rb[:, j, :], pkS, pkS[:, j, :], nbeG[:, h:h + 1], bv, bv[:, j, :],
                      ALU.mult, ALU.add, s_t=[nbeG])
            yield
            pvn = psf.next()
            for j in range(4):
                c.mm(pvn, pvn[:, j, :], TT, TT[:, j, :], rb, rb[:, j, :])
            c.cp("act", vn, vn[:, :, :], pvn, pvn[:, :, :])
            yield
            o32, sq4, rs4, og = P["o32"], P["sq4"], P["rs4"], P["og"]
            po = psf.next()
            for j, h in enumerate(hs):
                c.mm(po, po[:, j, :], Sbfh[hg], Sbfh[hg][:, j, :], qg, qg[:, j, :], start=True, stop=False)
                c.mm(po, po[:, j, :], vn, vn[:, j, :], aqk, aqk[:, j, :], start=False, stop=True)
            pS = psf.next()
            for j in range(4):
                c.mm(pS, pS[:, j, :], kdec, kdec[:, j, :], vn, vn[:, j, :])
            c.cp("act", o32, o32[:, :, :], po, po[:, :, :])
            for j, h in enumerate(hs):
                c.stt("dve", S32h[hg], S32h[hg][:, j, :], S32h[hg], S32h[hg][:, j, :], eGl[:, h:h + 1], pS, pS[:, j, :],
                      ALU.mult, ALU.add, s_t=[eGl])
            c.cp("pool", Sbfh[hg], Sbfh[hg][:, :, :], S32h[hg], S32h[hg][:, :, :])
            c.tt("pool", sq4, sq4[:, :, :], o32, o32[:, :, :], o32, o32[:, :, :], ALU.mult)
            yield
            pq = psf.next()
            c.mm(pq, pq[:, :, :], onesb, onesb[:, :], sq4, sq4[:, :, :])
            c.act(rs4, rs4[:, :, :], pq, pq[:, :, :], AF.Sqrt, bias=EPS, scale=1.0 / 128)
            c.recip(rs4, rs4[:, :, :], rs4, rs4[:, :, :])
            c.stt("dve", o32, o32[:, :, :], o32, o32[:, :, :], onorm[:, l:l + 1], rs4, rs4[:, :, :],
                  ALU.mult, ALU.mult, s_t=[onorm])
            c.tt("pool", og, og[:, :, :], o32, o32[:, :, :], zc, zc[:, 4 * hg:4 * hg + 4, :], ALU.mult)
            c.dma("pool", self.s_og[hg * 512:(hg + 1) * 512, t0:t0 + 128].rearrange("(h p) t -> p h t", p=128),
                  og[:, :, :], in_t=og)

        for ci in range(self.NC):
            t0 = ci * 128
            qc, kc_, vc, zc, bg = qp.next(), kp.next(), vp.next(), zp.next(), bgp.next()
            for hh in range(0, 16, 8):
                c.dma("sp", qc[:, hh:hh + 8, :],
                      self.s_q[hh * 128:(hh + 8) * 128, t0:t0 + 128].rearrange("(h p) t -> p h t", p=128), out_t=qc)
                c.dma("sp", kc_[:, hh:hh + 8, :],
                      self.s_k[hh * 128:(hh + 8) * 128, t0:t0 + 128].rearrange("(h p) t -> p h t", p=128), out_t=kc_)
            for hh in range(0, 32, 8):
                c.dma("sp", vc[:, hh:hh + 8, :],
                      self.s_v[hh * 128:(hh + 8) * 128, t0:t0 + 128].rearrange("(h p) t -> p h t", p=128), out_t=vc)
                c.dma("sp", zc[:, hh:hh + 8, :],
                      self.s_z[hh * 128:(hh + 8) * 128, t0:t0 + 128].rearrange("(h p) t -> p h t", p=128), out_t=zc)
            c.dma("sp", bg[:, :], self.s_bg[t0:t0 + 128, :], out_t=bg)
            p = pss.next()
            c.mm(p, p[:, 0:32], Um, Um[:, :], bg, bg[:, 32:64])
            c.mm(p, p[:, 32:64], onesf, onesf[:, :], bg, bg[:, 32:64])
            G = small["G"].next(); eG = small["eG"].next(); nbeG = small["nbeG"].next()
            negb = small["negb"].next(); kdc = small["kdc"].next(); eGl = small["eGl"].next()
            c.cp("dve", G, G[:, :], p, p[:, 0:32])
            c.act(eG, eG[:, :], p, p[:, 0:32], AF.Exp)
            c.stt("dve", nbeG, nbeG[:, :], eG, eG[:, :], -1.0, bg, bg[:, 0:32], ALU.mult, ALU.mult)
            c.ts("dve", negb, negb[:, :], bg, bg[:, 0:32], -1.0, ALU.mult)
            c.tt("dve", kdc, kdc[:, :], p, p[:, 32:64], G, G[:, :], ALU.subtract)
            c.act(kdc, kdc[:, :], kdc, kdc[:, :], AF.Exp)
            c.act(eGl, eGl[:, :], p, p[:, 32:64], AF.Exp)
            c.tr(p, p[0:32, 128:256], G, G[:, :], identf, identf[:, :])
            GT = gtp.next()
            c.cp("act", GT, GT[:, :], p, p[0:32, 128:256])
            pending = list(range(8))
            active = []
            for sl in range(NSLOT):
                hg = pending.pop(0)
                active.append(group_gen(hg, slots[sl], t0, qc, kc_, vc, zc, bg, G, GT, nbeG, negb, kdc, eGl))
            while any(a is not None for a in active):
                for sl in range(NSLOT):
                    g = active[sl]
                    if g is None:
                        continue
                    try:
                        next(g)
                    except StopIteration:
                        if pending:
                            hg = pending.pop(0)
                            active[sl] = group_gen(hg, slots[sl], t0, qc, kc_, vc, zc, bg, G, GT, nbeG, negb, kdc, eGl)
                        else:
                            active[sl] = None
        c.end_phase()

    def phase_outproj(self, l, src, KC, W):
        c = self.c
        c.begin_phase()
        gains = self.load_const(self.gains, [128, 16, 16], F32, "gains")
        onesb = self.load_const(self.k_onesb, [128, 128], BF16, "onesb")
        xin = c.sbpool(1, [128, KC, 512], BF16, "xin")
        h32p = c.sbpool(1, [128, 16, 512], F32, "h32")
        mix = c.sb([128, 16, 512], F32, "mix")
        sq = c.sbpool(4, [128, 512], BF16, "sq")
        bc = 512 if KC == 16 else 256
        wpool = c.sbpool(3, [128, KC, bc], BF16, "wblk")
        psp = c.pspool(4, [128, 512], F32, "psl")
        ps1 = c.pspool(2, [128, 512], F32, "ps1")
        rsp = c.sbpool(2, [128, 512], F32, "rs")
        for (t0, w) in token_tiles(self.Tp):
            x = xin.next()
            for k8 in range(0, KC, 8):
                c.dma("sp", x[:, k8:k8 + 8, :w],
                      src[k8 * 128:(k8 + 8) * 128, t0:t0 + w].rearrange("(kc p) t -> p kc t", p=128), out_t=x)
            h32 = h32p.next()
            c.dma("sp", h32[:, :, :w], self.hT[:, t0:t0 + w].rearrange("(kc p) t -> p kc t", p=128), out_t=h32)

            def epi(oc, ps, w=w):
                c.cp("act", mix, mix[:, oc, :w], ps, ps[:, :w])

            self.linear(W, 0, KC, 0, D, bc, x, lambda kc, w=w, x=x: x[:, kc, :w], w, wpool, psp, epi)
            self.postnorm_residual(mix, w, t0, gains, l * 4 + 1, h32, sq, ps1, rsp, onesb)
        c.end_phase()

    def phase_mlp(self, l):
        c = self.c
        c.begin_phase()
        gains = self.load_const(self.gains, [128, 16, 16], F32, "gains")
        onesb = self.load_const(self.k_onesb, [128, 128], BF16, "onesb")
        h32p = c.sbpool(1, [128, 16, 512], F32, "h32")
        hn = c.sb([128, 16, 512], BF16, "hn")
        sq = c.sbpool(4, [128, 512], BF16, "sq")
        actb = c.sb([128, 16, 512], BF16, "actb")
        ff = c.sb([128, 16, 512], F32, "ff")
        wpl = c.sbpool(3, [128, 16, 512], BF16, "wblk")
        psp = c.pspool(4, [128, 512], F32, "psl")
        ps1 = c.pspool(2, [128, 512], F32, "ps1")
        rsp = c.sbpool(2, [128, 512], F32, "rs")
        rl = c.sbpool(3, [128, 512], F32, "rl")
        Wu, Wd = self.wbf["up"][l], self.wbf["down"][l]
        for (t0, w) in token_tiles(self.Tp):
            h32 = h32p.next()
            c.dma("sp", h32[:, :, :w], self.hT[:, t0:t0 + w].rearrange("(kc p) t -> p kc t", p=128), out_t=h32)
            self.rmsnorm_fm(h32, 16, w, gains, lambda kc: gains[:, l * 4 + 2, kc:kc + 1], hn, sq, ps1, rsp, onesb, D)
            for qd in range(4):
                def epi_up(oc, ps, w=w):
                    r = rl.next()
                    c.act(r, r[:, :w], ps, ps[:, :w], AF.Relu)
                    c.tt("pool", actb, actb[:, oc, :w], r, r[:, :w], r, r[:, :w], ALU.mult)

                self.linear(Wu, 0, 16, qd * 2048, qd * 2048 + 2048, 512, hn, lambda kc, w=w: hn[:, kc, :w],
                            w, wpl, psp, epi_up)

                def epi_dn(oc, ps, w=w, qd=qd):
                    if qd == 0:
                        c.cp("act", ff, ff[:, oc, :w], ps, ps[:, :w])
                    else:
                        c.tt("dve", ff, ff[:, oc, :w], ps, ps[:, :w], ff, ff[:, oc, :w], ALU.add)

                self.linear(Wd, qd * 2048, 16, 0, D, 512, actb, lambda kc, w=w: actb[:, kc, :w],
                            w, wpl, psp, epi_dn)
            self.postnorm_residual(ff, w, t0, gains, l * 4 + 3, h32, sq, ps1, rsp, onesb)
        c.end_phase()

    def rope_epi(self, ps, w, t0, cosT, sinT, RT, xsp, t1p, obp, ps1, dst_rows, scale):
        c = self.c
        xs = xsp.next()
        c.cp("act", xs, xs[:, :w], ps, ps[:, :w])
        pr = ps1.next()
        c.mm(pr, pr[:, :w], RT, RT[:, :], xs, xs[:, :w])
        t1 = t1p.next()
        c.tt("dve", t1, t1[:, :w], pr, pr[:, :w], sinT, sinT[:, t0:t0 + w], ALU.mult)
        c.tt("pool", xs, xs[:, :w], xs, xs[:, :w], cosT, cosT[:, t0:t0 + w], ALU.mult)
        ob = obp.next()
        c.tt("dve", t1, t1[:, :w], t1, t1[:, :w], xs, xs[:, :w], ALU.add)
        c.ts("dve", ob, ob[:, :w], t1, t1[:, :w], scale, ALU.mult)
        c.dma("pool", dst_rows[:, t0:t0 + w], ob[:, :w], in_t=ob)

    def phase_qkproj(self, l, mode):
        c = self.c
        c.begin_phase()
        gains = self.load_const(self.gains, [128, 16, 16], F32, "gains")
        kvg = self.load_const(self.kvg, [128, 16], F32, "kvg")
        onesb = self.load_const(self.k_onesb, [128, 128], BF16, "onesb")
        RT = self.load_const(self.k_RT, [128, 128], F32, "RT")
        cosT = self.load_const(self.k_cos, [128, self.Tp], F32, "cosT")
        sinT = self.load_const(self.k_sin, [128, self.Tp], F32, "sinT")
        h32p = c.sbpool(1, [128, 16, 512], F32, "h32")
        hn = c.sb([128, 16, 512], BF16, "hn")
        sq = c.sbpool(4, [128, 512], BF16, "sq")
        wpool = c.sbpool(3, [128, 16, 512], BF16, "wblk")
        psp = c.pspool(4, [128, 512], F32, "psl")
        ps1 = c.pspool(2, [128, 512], F32, "ps1")
        rsp = c.sbpool(2, [128, 512], F32, "rs")
        xsp = c.sbpool(3, [128, 512], F32, "xs")
        t1p = c.sbpool(3, [128, 512], F32, "t1")
        obp = c.sbpool(4, [128, 512], BF16, "ob")
        if mode == "kv":
            W, dst, gt, gfn, scale = self.wbf["kv"], self.a_k, kvg, (lambda kc: kvg[:, kc:kc + 1]), 1.0
        else:
            W, dst, gt, gfn, scale = self.wbf["q"][l - 2], self.a_q, gains, (lambda kc: gains[:, l * 4, kc:kc + 1]), 128.0 ** -0.5
        for (t0, w) in token_tiles(self.Tp):
            h32 = h32p.next()
            c.dma("sp", h32[:, :, :w], self.hT[:, t0:t0 + w].rearrange("(kc p) t -> p kc t", p=128), out_t=h32)
            self.rmsnorm_fm(h32, 16, w, gt, gfn, hn, sq, ps1, rsp, onesb, D)

            def epi(oc, ps, w=w, t0=t0):
                self.rope_epi(ps, w, t0, cosT, sinT, RT, xsp, t1p, obp, ps1, dst[oc * 128:(oc + 1) * 128, :], scale)

            self.linear(W, 0, 16, 0, D, 512, hn, lambda kc, w=w: hn[:, kc, :w], w, wpool, psp, epi)
            if mode == "kv":
                for nb in range(4):
                    wt = wpool.next()
                    c.dma("sp", wt[:, :, :], W[:, D + nb * 512:D + (nb + 1) * 512].rearrange("(kc p) n -> p kc n", p=128),
                          out_t=wt)
                    for sbk in range(w // 128):
                        ps = psp.next()
                        for kc in range(16):
                            c.mm(ps, ps[:, :], hn, hn[:, kc, sbk * 128:(sbk + 1) * 128], wt, wt[:, kc, :],
                                 start=(kc == 0), stop=(kc == 15))
                        ob = obp.next()
                        c.cp("act", ob, ob[:, :], ps, ps[:, :])
                        tt0 = t0 + sbk * 128
                        c.dma("pool", self.a_v[tt0:tt0 + 128, nb * 512:(nb + 1) * 512], ob[:, :], in_t=ob)
        c.end_phase()

    def phase_attn(self, l):
        c = self.c
        Tp, NB, T = self.Tp, self.NC, self.T
        j_ = l - 2
        lambda_init = 0.8 - 0.6 * math.exp(-0.3 * l)
        c.begin_phase()
        identb = self.load_const(self.k_identb, [128, 128], BF16, "identb")
        onesb = self.load_const(self.k_onesb, [128, 128], BF16, "onesb")
        md = self.load_const(self.k_md, [128, 4, 512], F32, "md")
        lamt = self.load_const(self.lam[:, j_], [128, 4, 128], F32, "lam")
        subg = self.load_const(self.subln[:, j_], [128, 256], F32, "subg")
        c.ts("dve", subg, subg[:, :], subg, subg[:, :], 1.0 - lambda_init, ALU.mult)
        lp = c.sb([128, 2, 128], F32, "lp")
        c.tt("dve", lp, lp[:, 0, :], lamt, lamt[:, 0, :], lamt, lamt[:, 1, :], ALU.mult)
        c.tt("dve", lp, lp[:, 1, :], lamt, lamt[:, 2, :], lamt, lamt[:, 3, :], ALU.mult)
        ls = c.sb([128, 2], F32, "ls")
        c.op("dve", lambda en: en.tensor_reduce(out=ls[:, :], in_=lp[:, :, :], axis=AX.X, op=ALU.add), [ls], [lp])
        c.act(ls, ls[:, :], ls, ls[:, :], AF.Exp)
        neglam = c.sb([128, 1], F32, "neglam")
        c.tt("dve", neglam, neglam[:, :], ls, ls[:, 1:2], ls, ls[:, 0:1], ALU.subtract)
        c.ts("dve", neglam, neglam[:, :], neglam, neglam[:, :], -lambda_init, ALU.add)

        ktp = c.sbpool(2, [128, 2, Tp], BF16, "kt")
        vxp = c.sbpool(2, [128, NB, 257], BF16, "vx")
        for t in vxp.tiles:
            c.memset("pool", t, t[:, :, 256:257], 1.0)
        qtp = c.sbpool(2, [128, 2, 512], BF16, "qt")
        sqp = c.sbpool(2, [128, 512], BF16, "sqq")
        km2 = c.sb([128, 2], F32, "km2")
        kmt = c.sbpool(2, [128, 1], F32, "kmt")
        negBp = c.sbpool(3, [128, 512], F32, "negB")
        tmpp = c.sbpool(3, [128, 512], F32, "tmp")
        ptp = c.sbpool(3, [128, 512], BF16, "pt")
        on0p = c.sbpool(2, [128, 4, 256], F32, "on0")
        on1p = c.sbpool(2, [128, 256], F32, "on1")
        junk = c.sbpool(2, [128, 256], F32, "junk")
        colp = c.sbpool(4, [128, 1], F32, "col")
        osp = c.sbpool(2, [128, 256], BF16, "osn")
        ontp = c.sbpool(2, [128, 2, 128], BF16, "ont")
        pss = c.pspool(2, [128, 512], F32, "pss")
        pso = c.pspool(4, [128, 257], F32, "pso")
        psq = c.pspool(1, [128, 512], F32, "psq")
        pst = c.pspool(1, [128, 2, 128], BF16, "pst")

        for h in range(8):
            kt = ktp.next(); vx = vxp.next()
            c.dma("sp", kt[:, :, :], self.a_k[h * 256:(h + 1) * 256, :].rearrange("(m p) t -> p m t", p=128), out_t=kt)
            for n8 in range(0, NB, 8):
                n9 = min(NB, n8 + 8)
                c.dma("sp", vx[:, n8:n9, 0:256],
                      self.a_v[n8 * 128:n9 * 128, h * 256:(h + 1) * 256].rearrange("(nb p) f -> p nb f", p=128), out_t=vx)
            for m in range(2):
                first = True
                for (t0, w) in token_tiles(Tp):
                    s2 = sqp.next()
                    c.tt("pool", s2, s2[:, :w], kt, kt[:, m, t0:t0 + w], kt, kt[:, m, t0:t0 + w], ALU.mult)
                    pq = psq.next()
                    c.mm(pq, pq[:, :w], onesb, onesb[:, :], s2, s2[:, :w])
                    if first:
                        c.op("dve", lambda en, pq=pq, w=w, m=m: en.tensor_reduce(out=km2[:, m:m + 1], in_=pq[:, :w], axis=AX.X, op=ALU.max),
                             [km2], [pq])
                        first = False
                    else:
                        k1 = kmt.next()
                        c.op("dve", lambda en, pq=pq, w=w, k1=k1: en.tensor_reduce(out=k1[:, :], in_=pq[:, :w], axis=AX.X, op=ALU.max),
                             [k1], [pq])
                        c.tt("dve", km2, km2[:, m:m + 1], km2, km2[:, m:m + 1], k1, k1[:, :], ALU.max)
            c.ts("dve", km2, km2[:, :], km2, km2[:, :], 1.05, ALU.mult)
            for qi, (t0, w) in enumerate(token_tiles(Tp)):
                if t0 >= T:
                    continue
                nqs = w // 128
                qt = qtp.next()
                c.dma("sp", qt[:, :, :w], self.a_q[h * 256:(h + 1) * 256, t0:t0 + w].rearrange("(m p) t -> p m t", p=128),
                      out_t=qt)
                on0 = on0p.next()
                for m in range(2):
                    s2 = sqp.next()
                    c.tt("pool", s2, s2[:, :w], qt, qt[:, m, :w], qt, qt[:, m, :w], ALU.mult)
                    pq = psq.next()
                    c.mm(pq, pq[:, :w], onesb, onesb[:, :], s2, s2[:, :w])
                    negB = negBp.next()
                    c.act(negB, negB[:, :w], pq, pq[:, :w], AF.Sqrt, scale=km2[:, m:m + 1], extra_in=[km2])
                    c.ts("dve", negB, negB[:, :w], negB, negB[:, :w], -1.0, ALU.mult)
                    accs = [pso.next() for _ in range(nqs)]
                    kb_last = t0 // 128 + nqs - 1
                    for kb in range(kb_last + 1):
                        d = kb - t0 // 128
                        ps = pss.next()
                        c.mm(ps, ps[:, :w], kt, kt[:, m, kb * 128:(kb + 1) * 128], qt, qt[:, m, :w])
                        tmp = tmpp.next()
                        c.tt("dve", tmp, tmp[:, :w], ps, ps[:, :w], negB, negB[:, :w], ALU.add)
                        if d >= 0:
                            c.tt("pool", tmp, tmp[:, :w], tmp, tmp[:, :w], md, md[:, d, :w], ALU.add)
                        pt = ptp.next()
                        c.act(pt, pt[:, :w], tmp, tmp[:, :w], AF.Exp)
                        for qs in range(nqs):
                            if d > qs:
                                continue
                            c.mm(accs[qs], accs[qs][:, :], pt, pt[:, qs * 128:(qs + 1) * 128], vx, vx[:, kb, :],
                                 start=(kb == 0), stop=(d == qs))
                    for qs in range(nqs):
                        a = accs[qs]
                        rl_ = colp.next()
                        c.recip(rl_, rl_[:, :], a, a[:, 256:257])
                        if m == 0:
                            c.ts("dve", on0, on0[:, qs, :], a, a[:, 0:256], rl_[:, 0:1], ALU.mult, s_t=[rl_])
                        else:
                            on1 = on1p.next()
                            c.ts("dve", on1, on1[:, :], a, a[:, 0:256], rl_[:, 0:1], ALU.mult, s_t=[rl_])
                            c.stt("dve", on1, on1[:, :], on1, on1[:, :], neglam[:, 0:1], on0, on0[:, qs, :],
                                  ALU.mult, ALU.add, s_t=[neglam])
                            jk = junk.next(); ssq = colp.next()
                            c.memset("dve", ssq, ssq[:, :], 0.0)
                            c.act(jk, jk[:, :], on1, on1[:, :], AF.Square, accum=ssq[:, :], accum_t=ssq)
                            c.act(ssq, ssq[:, :], ssq, ssq[:, :], AF.Sqrt, bias=EPS, scale=1.0 / 256)
                            c.recip(ssq, ssq[:, :], ssq, ssq[:, :])
                            osn = osp.next()
                            c.stt("dve", osn, osn[:, :], on1, on1[:, :], ssq[:, 0:1], subg, subg[:, :],
                                  ALU.mult, ALU.mult, s_t=[ssq])
                            ptr = pst.next()
                            for e2 in range(2):
                                c.tr(ptr, ptr[:, e2, :], osn, osn[:, e2 * 128:(e2 + 1) * 128], identb, identb[:, :])
                            ont = ontp.next()
                            c.cp("act", ont, ont[:, :, :], ptr, ptr[:, :, :])
                            tq = t0 + qs * 128
                            c.dma("pool", self.a_on[h * 256:(h + 1) * 256, tq:tq + 128].rearrange("(e p) t -> p e t", p=128),
                                  ont[:, :, :], in_t=ont)
        c.end_phase()

    def phase_final(self):
        c = self.c
        c.barrier()

    def build(self, stop=None):
        self.declare()
        self.phase_init()
        for l in range(self.n_layers):
            if l < 2:
                self.phase_g1(l)
                if stop == "g1":
                    break
                self.phase_g2(l)
                if stop == "g2":
                    break
                self.phase_outproj(l, self.s_og, 32, self.wbf["gout"][l])
                if stop == "g3":
                    break
            else:
                if l == 2:
                    self.phase_qkproj(l, "kv")
                self.phase_qkproj(l, "q")
                self.phase_attn(l)
                self.phase_outproj(l, self.a_on, 16, self.wbf["o"][l - 2])
            self.phase_mlp(l)
        self.phase_final()
        return self.nc


def host_consts(Tp):
    bf = ml_dtypes.bfloat16
    p = np.arange(128)[:, None]
    f = np.arange(128)[None, :]
    cst = {}
    cst["identb"] = np.eye(128, dtype=np.float32).astype(bf)
    cst["onesb"] = np.ones((128, 128), np.float32).astype(bf)
    cst["onesf"] = np.ones((128, 128), np.float32)
    cst["identf"] = np.eye(128, dtype=np.float32)
    sel = np.zeros((32, 32, 128), np.float32)
    for hh in range(32):
        sel[hh, hh, :] = 1.0
    cst["sel"] = sel
    ii = np.arange(128)[:, None]; jj = np.arange(128)[None, :]
    m4 = np.zeros((128, 8, 4, 128), np.float32)
    m4[:, 0] = np.eye(128, dtype=np.float32)[:, None, :]
    def M(sv):
        n = 2 ** sv
        return ((ii // (2 * n) == jj // (2 * n)) & (ii % (2 * n) >= n) & (jj % (2 * n) < n)).astype(np.float32)
    m4[:, 1] = M(0).T[:, None, :]
    for sv in range(1, 7):
        m4[:, 1 + sv] = M(sv)[:, None, :]
    cst["m4"] = m4.astype(bf)
    cst["Umat"] = (p <= f).astype(np.float32)
    cst["negU"] = np.where(f >= p, 0.0, NEG).astype(np.float32)
    cst["M2"] = np.where(p > f, 0.0, NEG).astype(np.float32)
    R = np.zeros((128, 128), np.float32)
    for m in range(16):
        R[m, m + 16] = -1.0
        R[m + 16, m] = 1.0
    cst["RT"] = np.ascontiguousarray(R.T)
    pos = np.arange(Tp, dtype=np.float32)
    inv = (500000.0 ** (-np.arange(0, 32, 2, dtype=np.float32) / 32)).astype(np.float32)
    ang = pos[None, :] * inv[:, None]
    cosT = np.ones((128, Tp), np.float32)
    sinT = np.zeros((128, Tp), np.float32)
    cosT[0:16] = np.cos(ang); cosT[16:32] = np.cos(ang)
    sinT[0:16] = np.sin(ang); sinT[16:32] = np.sin(ang)
    cst["cosT"], cst["sinT"] = cosT, sinT
    md = np.zeros((128, 4, 512), np.float32)
    ff = np.arange(512)[None, :]
    for d in range(4):
        md[:, d, :] = np.where(d * 128 + p <= ff, 0.0, NEG)
    cst["maskd"] = md
    return cst


def rep(a, n=128):
    return np.ascontiguousarray(np.broadcast_to(a[None], (n,) + a.shape)).astype(np.float32)


def host_params(inp):
    ng = np.asarray(inp["norm_gains"], np.float32)
    d = {}
    d["gains"] = np.ascontiguousarray(ng.reshape(16, 16, 128).transpose(2, 0, 1))
    d["kvg"] = np.ascontiguousarray(np.asarray(inp["kv_norm"], np.float32).reshape(16, 128).T)
    cw = np.asarray(inp["gdn_conv_w"], np.float32)
    d["convw"] = np.ascontiguousarray(cw.reshape(2, 4, 64, 128).transpose(3, 0, 1, 2))
    d["alog"] = rep(np.asarray(inp["gdn_a_log"], np.float32))
    d["dtb"] = rep(np.asarray(inp["gdn_dt_bias"], np.float32))
    d["onorm"] = np.ascontiguousarray(np.asarray(inp["gdn_o_norm"], np.float32).T)
    d["lam"] = rep(np.asarray(inp["diff_lambda"], np.float32))
    d["subln"] = rep(np.asarray(inp["diff_subln"], np.float32))
    return d


_CACHE = {}


def kernel(**inputs):
    x = np.asarray(inputs["x"], np.float32)
    B, S, _ = x.shape
    T = NMETA + S
    Tp = ((T + 127) // 128) * 128
    key = (T,)
    if key not in _CACHE:
        _CACHE[key] = Builder(T).build()
    nc = _CACHE[key]
    cst = host_consts(Tp)
    prm = host_params(inputs)
    meta = np.asarray(inputs["meta_tokens"], np.float32)
    shared = dict(cst)
    shared.update(prm)
    for k in ("mlp_w_up", "mlp_w_down", "gdn_w_in", "gdn_w_out", "w_kv", "diff_w_q", "diff_w_o"):
        shared[k] = np.ascontiguousarray(np.asarray(inputs[k], np.float32))
    in_maps = []
    for core in range(8):
        b = core % B
        h0 = np.zeros((D, Tp), np.float32)
        h0[:, :NMETA] = meta.T
        h0[:, NMETA:T] = x[b].T
        m = dict(shared)
        m["h0"] = h0
        in_maps.append(m)
    res = run_bass_kernel_spmd(nc, in_maps, core_ids=list(range(8)))
    out = np.empty((B, S, D), np.float32)
    for b in range(B):
        hT = res.results[b]["hT"]
        out[b] = hT[:, NMETA:T].T
    return out
```

```python
import math, os
from contextlib import ExitStack
import numpy as np
import ml_dtypes
import concourse.bass as bass
import concourse.mybir as mybir
from concourse.bass_utils import run_bass_kernel_spmd

F32, BF16 = mybir.dt.float32, mybir.dt.bfloat16
AF = mybir.ActivationFunctionType
ALU = mybir.AluOpType
AX = mybir.AxisListType

D = 2048
DFF = 8192
NMETA = 16
GIN = 12352
EPS = 1e-6
NEG = -30000.0


class Tl:
    __slots__ = ("ap", "w", "r", "dsem", "name", "psum")

    def __init__(self, ap, name=""):
        self.ap = ap
        self.w = {}
        self.r = {}
        self.dsem = {}
        self.name = name
        self.psum = False

    def __getitem__(self, k):
        return self.ap[k]


class Pool_:
    def __init__(self, tiles):
        self.tiles = tiles
        self.i = 0

    def next(self):
        t = self.tiles[self.i % len(self.tiles)]
        self.i += 1
        return t


class Ctx:
    def __init__(self, nc):
        self.nc = nc
        self.es = ExitStack()
        self.eng = {"pe": nc.tensor, "act": nc.scalar, "dve": nc.vector, "pool": nc.gpsimd, "sp": nc.sync}
        self.sem = {k: self.es.enter_context(nc.semaphore("s_" + k)) for k in ("pe", "act", "dve", "pool")}
        self.cnt = {k: 0 for k in self.sem}
        self.waited = {}
        self.dsems = []
        self.dq = {"sp": [], "pool": []}
        self.dnext = {"sp": 0, "pool": 0}
        self.uid = 0
        self.phase_es = None

    def begin_phase(self):
        self.phase_es = ExitStack()
        self.dnext = {"sp": 0, "pool": 0}

    def end_phase(self):
        self.barrier()
        self.phase_es.close()
        self.phase_es = None

    def sb(self, shape, dtype, name="t"):
        self.uid += 1
        h = self.phase_es.enter_context(self.nc.sbuf_tensor(f"{name}_{self.uid}", list(shape), dtype))
        return Tl(h[tuple(slice(None) for _ in shape)], name)

    def ps(self, shape, dtype, name="p"):
        self.uid += 1
        nb = 512 if dtype == F32 else 1024
        h = self.phase_es.enter_context(self.nc.psum_tensor(f"{name}_{self.uid}", [shape[0], nb], dtype))
        n = 1
        for d in shape[1:]:
            n *= d
        assert n <= nb
        v = h[:, 0:n]
        if len(shape) == 3:
            v = v.rearrange("p (a b) -> p a b", a=shape[1])
        t = Tl(v, name)
        t.psum = True
        return t

    def sbpool(self, n, shape, dtype, name="t"):
        return Pool_([self.sb(shape, dtype, name) for _ in range(n)])

    def pspool(self, n, shape, dtype, name="p"):
        return Pool_([self.ps(shape, dtype, name) for _ in range(n)])

    def _dsem(self, t, q):
        if q not in t.dsem:
            lst = self.dq[q]
            if self.dnext[q] >= len(lst):
                s = self.es.enter_context(self.nc.semaphore(f"d{q}_{len(lst)}"))
                ent = [s, 0]
                lst.append(ent)
                self.dsems.append(ent)
            t.dsem[q] = lst[self.dnext[q]]
            self.dnext[q] += 1
        return t.dsem[q]

    def _wait(self, e, ev):
        key, sem, val = ev
        if e == "pe" and key == "pe":
            return
        k = (e, key)
        if self.waited.get(k, 0) >= val:
            return
        self.eng[e].wait_ge(sem, val)
        self.waited[k] = val

    def _deps(self, e, outs, ins):
        for t in ins:
            if t is None:
                continue
            for ev in t.w.values():
                self._wait(e, ev)
            if t.psum:
                for ev in list(t.r.values()):
                    if ev[0] != e:
                        self._wait(e, ev)
        for t in outs:
            if t is None:
                continue
            for ev in t.w.values():
                self._wait(e, ev)
            for ev in t.r.values():
                self._wait(e, ev)

    def _mark(self, ev, outs, ins):
        for t in ins:
            if t is not None:
                t.r[ev[0]] = ev
        for t in outs:
            if t is not None:
                t.w = {ev[0]: ev}
                t.r = {}

    def op(self, e, inst_fn, outs, ins):
        self._deps(e, outs, ins)
        inst = inst_fn(self.eng[e])
        self.cnt[e] += 1
        inst.then_inc(self.sem[e], 1)
        self._mark((e, self.sem[e], self.cnt[e]), outs, ins)

    def dma(self, q, out_ap, in_ap, out_t=None, in_t=None):
        self._deps(q, [out_t], [in_t])
        owner = out_t if out_t is not None else in_t
        ds = self._dsem(owner, q)
        inst = self.eng[q].dma_start(out=out_ap, in_=in_ap)
        ds[1] += 16
        inst.then_inc(ds[0], 16)
        self._mark((id(ds), ds[0], ds[1]), [out_t], [in_t])

    def barrier(self):
        for e in ("pe", "act", "dve", "pool", "sp"):
            for k in self.sem:
                if k != e and self.cnt[k] > 0:
                    self._wait(e, (k, self.sem[k], self.cnt[k]))
            for ds in self.dsems:
                if ds[1] > 0:
                    self._wait(e, (id(ds), ds[0], ds[1]))

    def mm(self, out_t, out_ap, lhsT_t, lhsT_ap, rhs_t, rhs_ap, start=True, stop=True):
        self.op("pe", lambda en: en.matmul(out_ap, lhsT=lhsT_ap, rhs=rhs_ap, start=start, stop=stop),
                [out_t], [lhsT_t, rhs_t])

    def tr(self, out_t, out_ap, in_t, in_ap, id_t, id_ap):
        self.op("pe", lambda en: en.transpose(out_ap, in_ap, id_ap), [out_t], [in_t, id_t])

    def act(self, out_t, out_ap, in_t, in_ap, func, bias=0.0, scale=1.0, extra_in=(), accum=None, accum_t=None):
        kw = {}
        if accum is not None:
            kw["accum_out"] = accum
        self.op("act", lambda en: en.activation(out=out_ap, in_=in_ap, func=func, bias=bias, scale=scale, **kw),
                [out_t] + ([accum_t] if accum_t is not None else []), [in_t] + list(extra_in))

    def tt(self, e, out_t, out_ap, a_t, a_ap, b_t, b_ap, op):
        self.op(e, lambda en: en.tensor_tensor(out=out_ap, in0=a_ap, in1=b_ap, op=op), [out_t], [a_t, b_t])

    def ts(self, e, out_t, out_ap, a_t, a_ap, s1, op0, s2=None, op1=None, s_t=()):
        if op1 is None:
            f = lambda en: en.tensor_scalar(out=out_ap, in0=a_ap, scalar1=s1, scalar2=None, op0=op0)
        else:
            f = lambda en: en.tensor_scalar(out=out_ap, in0=a_ap, scalar1=s1, scalar2=s2, op0=op0, op1=op1)
        self.op(e, f, [out_t], [a_t] + list(s_t))

    def stt(self, e, out_t, out_ap, a_t, a_ap, sc, b_t, b_ap, op0, op1, s_t=()):
        e = "dve"
        self.op(e, lambda en: en.scalar_tensor_tensor(out=out_ap, in0=a_ap, scalar=sc, in1=b_ap, op0=op0, op1=op1),
                [out_t], [a_t, b_t] + list(s_t))

    def cp(self, e, out_t, out_ap, in_t, in_ap):
        if e == "act":
            self.op(e, lambda en: en.copy(out=out_ap, in_=in_ap), [out_t], [in_t])
        else:
            self.op(e, lambda en: en.tensor_copy(out=out_ap, in_=in_ap), [out_t], [in_t])

    def memset(self, e, t, ap, val):
        self.op(e, lambda en: en.memset(ap, val), [t], [])

    def recip(self, out_t, out_ap, in_t, in_ap):
        self.op("dve", lambda en: en.reciprocal(out=out_ap, in_=in_ap), [out_t], [in_t])


def token_tiles(Tp):
    res = []
    t = 0
    while t < Tp:
        w = min(512, Tp - t)
        res.append((t, w))
        t += w
    return res


class Builder:
    def __init__(self, T, n_layers=4, debug=False):
        self.T = T
        self.Tp = ((T + 127) // 128) * 128
        self.NC = self.Tp // 128
        self.n_layers = n_layers
        self.nc = bass.Bass("TRN2", target_bir_lowering=False)
        self.c = Ctx(self.nc)
        self.debug = debug

    def declare(self):
        nc, Tp = self.nc, self.Tp

        def inp(name, shape, dt=F32):
            return nc.dram_tensor(name, list(shape), dt, kind="ExternalInput").ap()

        def scr(name, shape, dt):
            kind = "ExternalOutput" if (self.debug and name.startswith("s_")) else "Internal"
            return nc.dram_tensor(name, list(shape), dt, kind=kind).ap()

        self.h0 = inp("h0", [D, Tp])
        self.w32 = {
            "up": inp("mlp_w_up", [4, D, DFF]), "down": inp("mlp_w_down", [4, DFF, D]),
            "gin": inp("gdn_w_in", [2, D, GIN]), "gout": inp("gdn_w_out", [2, 4096, D]),
            "kv": inp("w_kv", [D, 4096]), "q": inp("diff_w_q", [2, D, D]), "o": inp("diff_w_o", [2, D, D]),
        }
        self.wbf = {k: scr(k + "_bf", v.shape, BF16) for k, v in self.w32.items()}
        self.gains = inp("gains", [128, 16, 16])
        self.kvg = inp("kvg", [128, 16])
        self.convw = inp("convw", [128, 2, 4, 64])
        self.alog = inp("alog", [128, 2, 32])
        self.dtb = inp("dtb", [128, 2, 32])
        self.onorm = inp("onorm", [128, 2])
        self.lam = inp("lam", [128, 2, 4, 128])
        self.subln = inp("subln", [128, 2, 256])
        self.k_identb = inp("identb", [128, 128], BF16)
        self.k_onesb = inp("onesb", [128, 128], BF16)
        self.k_onesf = inp("onesf", [128, 128])
        self.k_U = inp("Umat", [128, 128])
        self.k_sel = inp("sel", [32, 32, 128])
        self.k_identf = inp("identf", [128, 128])
        self.k_m4 = inp("m4", [128, 8, 4, 128], BF16)
        self.k_negU = inp("negU", [128, 128])
        self.k_M2 = inp("M2", [128, 128])
        self.k_RT = inp("RT", [128, 128])
        self.k_cos = inp("cosT", [128, Tp])
        self.k_sin = inp("sinT", [128, Tp])
        self.k_md = inp("maskd", [128, 4, 512])
        self.hT = nc.dram_tensor("hT", [D, Tp], F32, kind="ExternalOutput").ap()
        self.s_q = scr("s_q", [D, Tp], BF16)
        self.s_k = scr("s_k", [D, Tp], BF16)
        self.s_v = scr("s_v", [4096, Tp], BF16)
        self.s_z = scr("s_z", [4096, Tp], BF16)
        self.s_bg = scr("s_bg", [Tp, 64], F32)
        self.s_og = scr("s_og", [4096, Tp], BF16)
        self.a_k = scr("a_k", [D, Tp], BF16)
        self.a_v = scr("a_v", [Tp, D], BF16)
        self.a_q = scr("a_q", [D, Tp], BF16)
        self.a_on = scr("a_on", [D, Tp], BF16)

    def phase_init(self):
        c = self.c
        c.begin_phase()
        dummy = c.sb([128, 1], F32, "dummy")
        order = []
        for l in range(2):
            order += [("gin", l), ("gout", l), ("up", l), ("down", l)]
        order += [("kv", None)]
        for l in range(2):
            order += [("q", l), ("o", l), ("up", 2 + l), ("down", 2 + l)]
        for name, l in order:
            src = self.w32[name] if l is None else self.w32[name][l]
            dst = self.wbf[name] if l is None else self.wbf[name][l]
            K = src.shape[0]
            for r0 in range(0, K, 128):
                c.dma("pool", dst[r0:r0 + 128, :], src[r0:r0 + 128, :], in_t=dummy)
        for r0 in range(0, D, 128):
            c.dma("sp", self.hT[r0:r0 + 128, :], self.h0[r0:r0 + 128, :], in_t=dummy)
        c.end_phase()

    def load_const(self, ap_dram, shape, dt, name):
        c = self.c
        t = c.sb(shape, dt, name)
        c.dma("sp", t.ap, ap_dram, out_t=t)
        return t

    def rmsnorm_fm(self, x, nkc, w, gain_t, gain_ap_fn, out_hn, sq, ps_pool, rs_pool, onesb, nfeat):
        c = self.c
        ps = ps_pool.next()
        for kc in range(nkc):
            s1 = sq.next()
            c.act(s1, s1[:, :w], x, x[:, kc, :w], AF.Square)
            c.mm(ps, ps[:, :w], onesb, onesb[:, :], s1, s1[:, :w], start=(kc == 0), stop=(kc == nkc - 1))
        rs = rs_pool.next()
        c.act(rs, rs[:, :w], ps, ps[:, :w], AF.Ln, bias=EPS, scale=1.0 / nfeat)
        c.act(rs, rs[:, :w], rs, rs[:, :w], AF.Exp, scale=-0.5)
        if out_hn is not None:
            for kc in range(nkc):
                e = "dve" if kc % 2 == 0 else "pool"
                c.stt(e, out_hn, out_hn[:, kc, :w], x, x[:, kc, :w], gain_ap_fn(kc), rs, rs[:, :w],
                      ALU.mult, ALU.mult, s_t=[gain_t])
        return rs

    def linear(self, W, k0, KC, n0, n1, bc, x_t, x_fn, w, wpool, pspool, epi):
        c = self.c
        pending = None
        for nb in range(n0, n1, bc):
            cols = min(bc, n1 - nb)
            wt = wpool.next()
            for k8 in range(0, KC, 8):
                c.dma("sp", wt[:, k8:k8 + 8, 0:cols],
                      W[k0 + k8 * 128:k0 + (k8 + 8) * 128, nb:nb + cols].rearrange("(kc p) n -> p kc n", p=128), out_t=wt)
            for j in range(cols // 128):
                ps = pspool.next()
                for kc in range(KC):
                    c.mm(ps, ps[:, :w], wt, wt[:, kc, j * 128:(j + 1) * 128], x_t, x_fn(kc),
                         start=(kc == 0), stop=(kc == KC - 1))
                tail = epi((nb - n0) // 128 + j, ps)
                if pending is not None:
                    pending()
                pending = tail
        if pending is not None:
            pending()

    def postnorm_residual(self, y, w, t0, gains, gidx, h32, sq, ps1, rsp, onesb):
        c = self.c
        rs = self.rmsnorm_fm(y, 16, w, None, None, None, sq, ps1, rsp, onesb, D)
        for kc in range(16):
            e = "dve" if kc % 2 == 0 else "pool"
            c.stt(e, y, y[:, kc, :w], y, y[:, kc, :w], gains[:, gidx, kc:kc + 1], rs, rs[:, :w],
                  ALU.mult, ALU.mult, s_t=[gains])
            c.tt(e, h32, h32[:, kc, :w], y, y[:, kc, :w], h32, h32[:, kc, :w], ALU.add)
        c.dma("pool", self.hT[:, t0:t0 + w].rearrange("(kc p) t -> p kc t", p=128), h32[:, :, :w], in_t=h32)

    def phase_g1(self, l):
        c = self.c
        Tp = self.Tp
        c.begin_phase()
        gains = self.load_const(self.gains, [128, 16, 16], F32, "gains")
        onesb = self.load_const(self.k_onesb, [128, 128], BF16, "onesb")
        cw = self.load_const(self.convw[:, l], [128, 4, 64], F32, "cw")
        alog = self.load_const(self.alog[:, l], [128, 32], F32, "alog")
        dtb = self.load_const(self.dtb[:, l], [128, 32], F32, "dtb")
        negea = c.sb([128, 32], F32, "negea")
        c.act(negea, negea[:, :], alog, alog[:, :], AF.Exp)
        c.ts("dve", negea, negea[:, :], negea, negea[:, :], -1.0, ALU.mult)
        W = self.wbf["gin"][l]
        wlast = c.sb([128, 16, 64], BF16, "wlast")
        c.dma("sp", wlast[:, :, :], W[:, 12288:12352].rearrange("(kc p) n -> p kc n", p=128), out_t=wlast)
        halo = c.sb([128, 64, 3], F32, "halo")
        c.memset("pool", halo, halo[:, :, :], 0.0)
        h32p = c.sbpool(1, [128, 16, 512], F32, "h32")
        hn = c.sb([128, 16, 512], BF16, "hn")
        sq = c.sbpool(4, [128, 512], BF16, "sq")
        wpool = c.sbpool(3, [128, 16, 512], BF16, "wblk")
        psp = c.pspool(4, [128, 512], F32, "psl")
        ps1 = c.pspool(2, [128, 512], F32, "ps1")
        psba = c.pspool(1, [128, 64], F32, "psba")
        rsp = c.sbpool(2, [128, 512], F32, "rs")
        xpp = c.sbpool(3, [128, 515], F32, "xp")
        yp = c.sbpool(3, [128, 512], F32, "y")
        sp_ = c.sbpool(8, [128, 512], F32, "s")
        sqp = c.sbpool(8, [128, 512], BF16, "sq1")
        r1p = c.sbpool(5, [128, 512], F32, "r1")
        obp = c.sbpool(6, [128, 512], BF16, "ob")
        bap = c.sbpool(2, [128, 64], F32, "ba")
        tmp32 = c.sbpool(4, [128, 32], F32, "tmp32")

        for (t0, w) in token_tiles(Tp):
            h32 = h32p.next()
            c.dma("sp", h32[:, :, :w], self.hT[:, t0:t0 + w].rearrange("(kc p) t -> p kc t", p=128), out_t=h32)
            self.rmsnorm_fm(h32, 16, w, gains, lambda kc: gains[:, l * 4 + 0, kc:kc + 1], hn, sq, ps1, rsp, onesb, D)

            qk_batch = []

            def epi(oc, ps, t0=t0, w=w, qk_batch=qk_batch):
                if oc < 64:
                    xp = xpp.next()
                    c.cp("pool", xp, xp[:, 0:3], halo, halo[:, oc, :])
                    c.cp("act", xp, xp[:, 3:3 + w], ps, ps[:, :w])
                    c.cp("pool", halo, halo[:, oc, :], xp, xp[:, w:w + 3])
                    y = yp.next()
                    c.ts("dve", y, y[:, :w], xp, xp[:, 0:w], cw[:, 0, oc:oc + 1], ALU.mult, s_t=[cw])
                    for j in range(1, 4):
                        c.stt("dve", y, y[:, :w], xp, xp[:, j:j + w], cw[:, j, oc:oc + 1], y, y[:, :w],
                              ALU.mult, ALU.add, s_t=[cw])
                    if oc >= 32:
                        ob = obp.next()
                        c.act(ob, ob[:, :w], y, y[:, :w], AF.Silu)
                        r = (oc - 32) * 128
                        c.dma("pool", self.s_v[r:r + 128, t0:t0 + w], ob[:, :w], in_t=ob)
                    else:
                        s = sp_.next()
                        c.act(s, s[:, :w], y, y[:, :w], AF.Silu)
                        sq1 = sqp.next()
                        c.tt("pool", sq1, sq1[:, :w], s, s[:, :w], s, s[:, :w], ALU.mult)
                        qk_batch.append((oc, s, sq1))
                        if len(qk_batch) < 4:
                            return None
                        batch = list(qk_batch)
                        del qk_batch[:]

                        def tail(batch=batch, w=w, t0=t0):
                            r1s = []
                            for (oc_, s_, sq_) in batch:
                                p1 = ps1.next()
                                c.mm(p1, p1[:, :w], onesb, onesb[:, :], sq_, sq_[:, :w])
                                r1 = r1p.next()
                                c.act(r1, r1[:, :w], p1, p1[:, :w], AF.Ln, bias=EPS, scale=1.0)
                                r1s.append(r1)
                            for r1 in r1s:
                                c.act(r1, r1[:, :w], r1, r1[:, :w], AF.Exp, scale=-0.5)
                            for (oc_, s_, sq_), r1 in zip(batch, r1s):
                                ob = obp.next()
                                if oc_ < 16:
                                    c.stt("dve", ob, ob[:, :w], s_, s_[:, :w], 128.0 ** -0.5, r1, r1[:, :w], ALU.mult, ALU.mult)
                                    c.dma("pool", self.s_q[oc_ * 128:(oc_ + 1) * 128, t0:t0 + w], ob[:, :w], in_t=ob)
                                else:
                                    c.tt("dve", ob, ob[:, :w], s_, s_[:, :w], r1, r1[:, :w], ALU.mult)
                                    r = (oc_ - 16) * 128
                                    c.dma("pool", self.s_k[r:r + 128, t0:t0 + w], ob[:, :w], in_t=ob)
                        return tail
                else:
                    ob = obp.next()
                    c.act(ob, ob[:, :w], ps, ps[:, :w], AF.Silu)
                    r = (oc - 64) * 128
                    c.dma("pool", self.s_z[r:r + 128, t0:t0 + w], ob[:, :w], in_t=ob)

            self.linear(W, 0, 16, 0, 12288, 512, hn, lambda kc, w=w: hn[:, kc, :w], w, wpool, psp, epi)
            for sbk in range(w // 128):
                pb = psba.next()
                for kc in range(16):
                    c.mm(pb, pb[:, :], hn, hn[:, kc, sbk * 128:(sbk + 1) * 128], wlast, wlast[:, kc, :],
                         start=(kc == 0), stop=(kc == 15))
                ba = bap.next()
                c.act(ba, ba[:, 0:32], pb, pb[:, 0:32], AF.Sigmoid)
                a = tmp32.next()
                c.tt("dve", a, a[:, :], pb, pb[:, 32:64], dtb, dtb[:, :], ALU.add)
                ab = tmp32.next()
                c.act(ab, ab[:, :], a, a[:, :], AF.Abs)
                c.act(ab, ab[:, :], ab, ab[:, :], AF.Exp, scale=-1.0)
                c.act(ab, ab[:, :], ab, ab[:, :], AF.Ln, bias=1.0)
                c.stt("dve", a, a[:, :], a, a[:, :], 0.0, ab, ab[:, :], ALU.max, ALU.add)
                c.tt("dve", ba, ba[:, 32:64], a, a[:, :], negea, negea[:, :], ALU.mult)
                tt0 = t0 + sbk * 128
                c.dma("pool", self.s_bg[tt0:tt0 + 128, :], ba[:, :], in_t=ba)
        c.end_phase()

    def phase_g2(self, l):
        c = self.c
        c.begin_phase()
        identb = self.load_const(self.k_identb, [128, 128], BF16, "identb")
        onesb = self.load_const(self.k_onesb, [128, 128], BF16, "onesb")
        onesf = self.load_const(self.k_onesf, [128, 128], F32, "onesf")
        Um = self.load_const(self.k_U, [128, 128], F32, "Um")
        negU = self.load_const(self.k_negU, [128, 128], F32, "negU")
        M2 = self.load_const(self.k_M2, [128, 128], F32, "M2")
        onorm = self.load_const(self.onorm, [128, 2], F32, "onorm")
        S32h = [c.sb([128, 4, 128], F32, "S32") for _ in range(8)]
        Sbfh = [c.sb([128, 4, 128], BF16, "Sbf") for _ in range(8)]
        for t in S32h + Sbfh:
            c.memset("pool", t, t[:, :, :], 0.0)
        qp = c.sbpool(2, [128, 16, 128], BF16, "qc")
        kp = c.sbpool(2, [128, 16, 128], BF16, "kc")
        vp = c.sbpool(2, [128, 32, 128], BF16, "vc")
        zp = c.sbpool(2, [128, 32, 128], BF16, "zc")
        bgp = c.sbpool(2, [128, 64], F32, "bg")
        gtp = c.sbpool(2, [32, 128], F32, "GT")
        sel = self.load_const(self.k_sel, [32, 32, 128], F32, "sel")
        identf = self.load_const(self.k_identf, [128, 128], F32, "identf")
        m4 = self.load_const(self.k_m4, [128, 8, 4, 128], BF16, "m4")
        small = {n: c.sbpool(2, [128, 32], F32, n) for n in ("G", "eG", "nbeG", "negb", "kdc", "eGl")}
        NSLOT = 3
        slots = []
        for _ in range(NSLOT):
            P = {n: c.sb([128, 4, 128], F32, n) for n in ("e1", "e2", "eGbc", "o32", "rs4")}
            P.update({n: c.sb([128, 4, 128], BF16, n) for n in
                      ("aqk", "qg", "TT", "kdec", "bv", "rb", "vn", "sq4", "og", "Pa", "X", "Ts")})
            P["amn"] = c.sb([128, 6, 4, 128], BF16, "amn")
            slots.append(P)
        psf = c.pspool(5, [128, 4, 128], F32, "psf")
        psb = c.pspool(2, [128, 8, 128], BF16, "psb")
        pss = c.pspool(1, [128, 512], F32, "pss")

        def group_gen(hg, P, t0, qc, kc_, vc, zc, bg, G, GT, nbeG, negb, kdc, eGl):
            hs = [4 * hg + j for j in range(4)]
            qhs = [2 * hg, 2 * hg + 1]
            e1, e2, eGbc = P["e1"], P["e2"], P["eGbc"]
            pG = psf.next()
            for j, h in enumerate(hs):
                c.mm(pG, pG[:, j, :], sel, sel[:, h, :], GT, GT[:, :])
            for j, h in enumerate(hs):
                c.stt("dve", e1, e1[:, j, :], pG, pG[:, j, :], G[:, h:h + 1], negU, negU[:, :],
                      ALU.subtract, ALU.add, s_t=[G])
                c.stt("dve", e2, e2[:, j, :], pG, pG[:, j, :], G[:, h:h + 1], M2, M2[:, :],
                      ALU.subtract, ALU.subtract, s_t=[G])
            c.act(eGbc, eGbc[:, :, :], pG, pG[:, :, :], AF.Exp)
            c.act(e1, e1[:, :, :], e1, e1[:, :, :], AF.Exp)
            c.act(e2, e2[:, :, :], e2, e2[:, :, :], AF.Exp, scale=-1.0)
            DmT, Dms = e1, e2
            yield
            pK = psf.next()
            for j, qh in enumerate(qhs):
                c.mm(pK, pK[:, j, :], kc_, kc_[:, qh, :], kc_, kc_[:, qh, :])
                c.mm(pK, pK[:, 2 + j, :], kc_, kc_[:, qh, :], qc, qc[:, qh, :])
            Pa, aqk, qg = P["Pa"], P["aqk"], P["qg"]
            for j, h in enumerate(hs):
                c.stt("dve", Pa, Pa[:, j, :], pK, pK[:, j // 2, :], negb[:, h:h + 1], Dms, Dms[:, j, :],
                      ALU.mult, ALU.mult, s_t=[negb])
                c.tt("dve", aqk, aqk[:, j, :], pK, pK[:, 2 + j // 2, :], DmT, DmT[:, j, :], ALU.mult)
                c.tt("pool", qg, qg[:, j, :], qc, qc[:, qhs[j // 2], :], eGbc, eGbc[:, j, :], ALU.mult)
            yield
            pT = psb.next()
            for j in range(4):
                c.tr(pT, pT[:, j, :], Pa, Pa[:, j, :], identb, identb[:, :])
            for j, qh in enumerate(qhs):
                c.tr(pT, pT[:, 4 + j, :], kc_, kc_[:, qh, :], identb, identb[:, :])
            pV = psb.next()
            for j, h in enumerate(hs):
                c.tr(pV, pV[:, j, :], vc, vc[:, h, :], identb, identb[:, :])
            TT, kdec, bv, amn = P["TT"], P["kdec"], P["bv"], P["amn"]
            c.tt("dve", TT, TT[:, :, :], pT, pT[:, 0:4, :], m4, m4[:, 1, :, :], ALU.mult)
            c.tt("dve", TT, TT[:, :, :], TT, TT[:, :, :], m4, m4[:, 0, :, :], ALU.add)
            for j, h in enumerate(hs):
                c.ts("dve", kdec, kdec[:, j, :], pT, pT[:, 4 + j // 2, :], kdc[:, h:h + 1], ALU.mult, s_t=[kdc])
                c.ts("dve", bv, bv[:, j, :], pV, pV[:, j, :], bg[:, h:h + 1], ALU.mult, s_t=[bg])
            for sv in range(6):
                c.tt("pool", amn, amn[:, sv, :, :], Pa, Pa[:, :, :], m4, m4[:, 2 + sv, :, :], ALU.mult)
            yield
            X, Ts = P["X"], P["Ts"]
            for sv in range(6):
                pX = psf.next()
                for j in range(4):
                    c.mm(pX, pX[:, j, :], amn, amn[:, sv, j, :], TT, TT[:, j, :])
                c.cp("act", X, X[:, :, :], pX, pX[:, :, :])
                pTr = psb.next()
                for j in range(4):
                    c.tr(pTr, pTr[:, j, :], TT, TT[:, j, :], identb, identb[:, :])
                c.cp("dve", Ts, Ts[:, :, :], pTr, pTr[:, 0:4, :])
                yield
                pD = psf.next()
                for j in range(4):
                    c.mm(pD, pD[:, j, :], Ts, Ts[:, j, :], X, X[:, j, :])
                c.tt("dve", TT, TT[:, :, :], pD, pD[:, :, :], TT, TT[:, :, :], ALU.add)
                yield
            rb, vn = P["rb"], P["vn"]
            pkS = psf.next()
            for j, h in enumerate(hs):
                c.mm(pkS, pkS[:, j, :], kc_, kc_[:, qhs[j // 2], :], Sbfh[hg], Sbfh[hg][:, j, :])
            for j, h in enumerate(hs):
                c.stt("dve", rb, rb[:, j, :], pkS, pkS[:, j, :], nbeG[:, h:h + 1], bv, bv[:, j, :],
                      ALU.mult, ALU.add, s_t=[nbeG])
            yield
            pvn = psf.next()
            for j in range(4):
                c.mm(pvn, pvn[:, j, :], TT, TT[:, j, :], rb, rb[:, j, :])
            c.cp("act", vn, vn[:, :, :], pvn, pvn[:, :, :])
            yield
            o32, sq4, rs4, og = P["o32"], P["sq4"], P["rs4"], P["og"]
            po = psf.next()
            for j, h in enumerate(hs):
                c.mm(po, po[:, j, :], Sbfh[hg], Sbfh[hg][:, j, :], qg, qg[:, j, :], start=True, stop=False)
                c.mm(po, po[:, j, :], vn, vn[:, j, :], aqk, aqk[:, j, :], start=False, stop=True)
            pS = psf.next()
            for j in range(4):
                c.mm(pS, pS[:, j, :], kdec, kdec[:, j, :], vn, vn[:, j, :])
            c.cp("act", o32, o32[:, :, :], po, po[:, :, :])
            for j, h in enumerate(hs):
                c.stt("dve", S32h[hg], S32h[hg][:, j, :], S32h[hg], S32h[hg][:, j, :], eGl[:, h:h + 1], pS, pS[:, j, :],
                      ALU.mult, ALU.add, s_t=[eGl])
            c.cp("pool", Sbfh[hg], Sbfh[hg][:, :, :], S32h[hg], S32h[hg][:, :, :])
            c.tt("pool", sq4, sq4[:, :, :], o32, o32[:, :, :], o32, o32[:, :, :], ALU.mult)
            yield
            pq = psf.next()
            c.mm(pq, pq[:, :, :], onesb, onesb[:, :], sq4, sq4[:, :, :])
            c.act(rs4, rs4[:, :, :], pq, pq[:, :, :], AF.Ln, bias=EPS, scale=1.0 / 128)
            c.act(rs4, rs4[:, :, :], rs4, rs4[:, :, :], AF.Exp, scale=-0.5)
            c.stt("dve", o32, o32[:, :, :], o32, o32[:, :, :], onorm[:, l:l + 1], rs4, rs4[:, :, :],
                  ALU.mult, ALU.mult, s_t=[onorm])
            c.tt("pool", og, og[:, :, :], o32, o32[:, :, :], zc, zc[:, 4 * hg:4 * hg + 4, :], ALU.mult)
            c.dma("pool", self.s_og[hg * 512:(hg + 1) * 512, t0:t0 + 128].rearrange("(h p) t -> p h t", p=128),
                  og[:, :, :], in_t=og)

        for ci in range(self.NC):
            t0 = ci * 128
            qc, kc_, vc, zc, bg = qp.next(), kp.next(), vp.next(), zp.next(), bgp.next()
            for hh in range(0, 16, 8):
                c.dma("sp", qc[:, hh:hh + 8, :],
                      self.s_q[hh * 128:(hh + 8) * 128, t0:t0 + 128].rearrange("(h p) t -> p h t", p=128), out_t=qc)
                c.dma("sp", kc_[:, hh:hh + 8, :],
                      self.s_k[hh * 128:(hh + 8) * 128, t0:t0 + 128].rearrange("(h p) t -> p h t", p=128), out_t=kc_)
            for hh in range(0, 32, 8):
                c.dma("sp", vc[:, hh:hh + 8, :],
                      self.s_v[hh * 128:(hh + 8) * 128, t0:t0 + 128].rearrange("(h p) t -> p h t", p=128), out_t=vc)
                c.dma("sp", zc[:, hh:hh + 8, :],
                      self.s_z[hh * 128:(hh + 8) * 128, t0:t0 + 128].rearrange("(h p) t -> p h t", p=128), out_t=zc)
            c.dma("sp", bg[:, :], self.s_bg[t0:t0 + 128, :], out_t=bg)
            p = pss.next()
            c.mm(p, p[:, 0:32], Um, Um[:, :], bg, bg[:, 32:64])
            c.mm(p, p[:, 32:64], onesf, onesf[:, :], bg, bg[:, 32:64])
            G = small["G"].next(); eG = small["eG"].next(); nbeG = small["nbeG"].next()
            negb = small["negb"].next(); kdc = small["kdc"].next(); eGl = small["eGl"].next()
            c.cp("dve", G, G[:, :], p, p[:, 0:32])
            c.act(eG, eG[:, :], p, p[:, 0:32], AF.Exp)
            c.stt("dve", nbeG, nbeG[:, :], eG, eG[:, :], -1.0, bg, bg[:, 0:32], ALU.mult, ALU.mult)
            c.ts("dve", negb, negb[:, :], bg, bg[:, 0:32], -1.0, ALU.mult)
            c.tt("dve", kdc, kdc[:, :], p, p[:, 32:64], G, G[:, :], ALU.subtract)
            c.act(kdc, kdc[:, :], kdc, kdc[:, :], AF.Exp)
            c.act(eGl, eGl[:, :], p, p[:, 32:64], AF.Exp)
            c.tr(p, p[0:32, 128:256], G, G[:, :], identf, identf[:, :])
            GT = gtp.next()
            c.cp("act", GT, GT[:, :], p, p[0:32, 128:256])
            pending = list(range(8))
            active = []
            for sl in range(NSLOT):
                hg = pending.pop(0)
                active.append(group_gen(hg, slots[sl], t0, qc, kc_, vc, zc, bg, G, GT, nbeG, negb, kdc, eGl))
            while any(a is not None for a in active):
                for sl in range(NSLOT):
                    g = active[sl]
                    if g is None:
                        continue
                    try:
                        next(g)
                    except StopIteration:
                        if pending:
                            hg = pending.pop(0)
                            active[sl] = group_gen(hg, slots[sl], t0, qc, kc_, vc, zc, bg, G, GT, nbeG, negb, kdc, eGl)
                        else:
                            active[sl] = None
        c.end_phase()

    def phase_outproj(self, l, src, KC, W):
        c = self.c
        c.begin_phase()
        gains = self.load_const(self.gains, [128, 16, 16], F32, "gains")
        onesb = self.load_const(self.k_onesb, [128, 128], BF16, "onesb")
        xin = c.sbpool(1, [128, KC, 512], BF16, "xin")
        h32p = c.sbpool(1, [128, 16, 512], F32, "h32")
        mix = c.sb([128, 16, 512], F32, "mix")
        sq = c.sbpool(4, [128, 512], BF16, "sq")
        bc = 512 if KC == 16 else 256
        wpool = c.sbpool(3, [128, KC, bc], BF16, "wblk")
        psp = c.pspool(4, [128, 512], F32, "psl")
        ps1 = c.pspool(2, [128, 512], F32, "ps1")
        rsp = c.sbpool(2, [128, 512], F32, "rs")
        for (t0, w) in token_tiles(self.Tp):
            x = xin.next()
            for k8 in range(0, KC, 8):
                c.dma("sp", x[:, k8:k8 + 8, :w],
                      src[k8 * 128:(k8 + 8) * 128, t0:t0 + w].rearrange("(kc p) t -> p kc t", p=128), out_t=x)
            h32 = h32p.next()
            c.dma("sp", h32[:, :, :w], self.hT[:, t0:t0 + w].rearrange("(kc p) t -> p kc t", p=128), out_t=h32)

            def epi(oc, ps, w=w):
                c.cp("act", mix, mix[:, oc, :w], ps, ps[:, :w])

            self.linear(W, 0, KC, 0, D, bc, x, lambda kc, w=w, x=x: x[:, kc, :w], w, wpool, psp, epi)
            self.postnorm_residual(mix, w, t0, gains, l * 4 + 1, h32, sq, ps1, rsp, onesb)
        c.end_phase()

    def phase_mlp(self, l):
        c = self.c
        c.begin_phase()
        gains = self.load_const(self.gains, [128, 16, 16], F32, "gains")
        onesb = self.load_const(self.k_onesb, [128, 128], BF16, "onesb")
        h32p = c.sbpool(1, [128, 16, 512], F32, "h32")
        hn = c.sb([128, 16, 512], BF16, "hn")
        sq = c.sbpool(4, [128, 512], BF16, "sq")
        actb = c.sb([128, 16, 512], BF16, "actb")
        ff = c.sb([128, 16, 512], F32, "ff")
        wpl = c.sbpool(3, [128, 16, 512], BF16, "wblk")
        psp = c.pspool(4, [128, 512], F32, "psl")
        ps1 = c.pspool(2, [128, 512], F32, "ps1")
        rsp = c.sbpool(2, [128, 512], F32, "rs")
        rl = c.sbpool(3, [128, 512], F32, "rl")
        Wu, Wd = self.wbf["up"][l], self.wbf["down"][l]
        for (t0, w) in token_tiles(self.Tp):
            h32 = h32p.next()
            c.dma("sp", h32[:, :, :w], self.hT[:, t0:t0 + w].rearrange("(kc p) t -> p kc t", p=128), out_t=h32)
            self.rmsnorm_fm(h32, 16, w, gains, lambda kc: gains[:, l * 4 + 2, kc:kc + 1], hn, sq, ps1, rsp, onesb, D)
            for qd in range(4):
                def epi_up(oc, ps, w=w):
                    r = rl.next()
                    c.act(r, r[:, :w], ps, ps[:, :w], AF.Relu)
                    c.tt("pool", actb, actb[:, oc, :w], r, r[:, :w], r, r[:, :w], ALU.mult)

                self.linear(Wu, 0, 16, qd * 2048, qd * 2048 + 2048, 512, hn, lambda kc, w=w: hn[:, kc, :w],
                            w, wpl, psp, epi_up)

                def epi_dn(oc, ps, w=w, qd=qd):
                    if qd == 0:
                        c.cp("act", ff, ff[:, oc, :w], ps, ps[:, :w])
                    else:
                        c.tt("dve", ff, ff[:, oc, :w], ps, ps[:, :w], ff, ff[:, oc, :w], ALU.add)

                self.linear(Wd, qd * 2048, 16, 0, D, 512, actb, lambda kc, w=w: actb[:, kc, :w],
                            w, wpl, psp, epi_dn)
            self.postnorm_residual(ff, w, t0, gains, l * 4 + 3, h32, sq, ps1, rsp, onesb)
        c.end_phase()

    def rope_epi(self, ps, w, t0, cosT, sinT, RT, xsp, t1p, obp, ps1, dst_rows, scale):
        c = self.c
        xs = xsp.next()
        c.cp("act", xs, xs[:, :w], ps, ps[:, :w])
        return lambda: self.rope_tail(xs, w, t0, cosT, sinT, RT, t1p, obp, ps1, dst_rows, scale)

    def rope_tail(self, xs, w, t0, cosT, sinT, RT, t1p, obp, ps1, dst_rows, scale):
        c = self.c
        pr = ps1.next()
        c.mm(pr, pr[:, :w], RT, RT[:, :], xs, xs[:, :w])
        t1 = t1p.next()
        c.tt("dve", t1, t1[:, :w], pr, pr[:, :w], sinT, sinT[:, t0:t0 + w], ALU.mult)
        c.tt("pool", xs, xs[:, :w], xs, xs[:, :w], cosT, cosT[:, t0:t0 + w], ALU.mult)
        ob = obp.next()
        c.tt("dve", t1, t1[:, :w], t1, t1[:, :w], xs, xs[:, :w], ALU.add)
        c.ts("dve", ob, ob[:, :w], t1, t1[:, :w], scale, ALU.mult)
        c.dma("pool", dst_rows[:, t0:t0 + w], ob[:, :w], in_t=ob)

    def phase_qkproj(self, l, mode):
        c = self.c
        c.begin_phase()
        gains = self.load_const(self.gains, [128, 16, 16], F32, "gains")
        kvg = self.load_const(self.kvg, [128, 16], F32, "kvg")
        onesb = self.load_const(self.k_onesb, [128, 128], BF16, "onesb")
        RT = self.load_const(self.k_RT, [128, 128], F32, "RT")
        cosT = self.load_const(self.k_cos, [128, self.Tp], F32, "cosT")
        sinT = self.load_const(self.k_sin, [128, self.Tp], F32, "sinT")
        h32p = c.sbpool(1, [128, 16, 512], F32, "h32")
        hn = c.sb([128, 16, 512], BF16, "hn")
        sq = c.sbpool(4, [128, 512], BF16, "sq")
        wpool = c.sbpool(3, [128, 16, 512], BF16, "wblk")
        psp = c.pspool(4, [128, 512], F32, "psl")
        ps1 = c.pspool(2, [128, 512], F32, "ps1")
        rsp = c.sbpool(2, [128, 512], F32, "rs")
        xsp = c.sbpool(3, [128, 512], F32, "xs")
        t1p = c.sbpool(3, [128, 512], F32, "t1")
        obp = c.sbpool(4, [128, 512], BF16, "ob")
        if mode == "kv":
            W, dst, gt, gfn, scale = self.wbf["kv"], self.a_k, kvg, (lambda kc: kvg[:, kc:kc + 1]), 1.0
        else:
            W, dst, gt, gfn, scale = self.wbf["q"][l - 2], self.a_q, gains, (lambda kc: gains[:, l * 4, kc:kc + 1]), 128.0 ** -0.5
        for (t0, w) in token_tiles(self.Tp):
            h32 = h32p.next()
            c.dma("sp", h32[:, :, :w], self.hT[:, t0:t0 + w].rearrange("(kc p) t -> p kc t", p=128), out_t=h32)
            self.rmsnorm_fm(h32, 16, w, gt, gfn, hn, sq, ps1, rsp, onesb, D)

            def epi(oc, ps, w=w, t0=t0):
                return self.rope_epi(ps, w, t0, cosT, sinT, RT, xsp, t1p, obp, ps1, dst[oc * 128:(oc + 1) * 128, :], scale)

            self.linear(W, 0, 16, 0, D, 512, hn, lambda kc, w=w: hn[:, kc, :w], w, wpool, psp, epi)
            if mode == "kv":
                for nb in range(4):
                    wt = wpool.next()
                    c.dma("sp", wt[:, :, :], W[:, D + nb * 512:D + (nb + 1) * 512].rearrange("(kc p) n -> p kc n", p=128),
                          out_t=wt)
                    for sbk in range(w // 128):
                        ps = psp.next()
                        for kc in range(16):
                            c.mm(ps, ps[:, :], hn, hn[:, kc, sbk * 128:(sbk + 1) * 128], wt, wt[:, kc, :],
                                 start=(kc == 0), stop=(kc == 15))
                        ob = obp.next()
                        c.cp("act", ob, ob[:, :], ps, ps[:, :])
                        tt0 = t0 + sbk * 128
                        c.dma("pool", self.a_v[tt0:tt0 + 128, nb * 512:(nb + 1) * 512], ob[:, :], in_t=ob)
        c.end_phase()

    def phase_attn(self, l):
        c = self.c
        Tp, NB, T = self.Tp, self.NC, self.T
        j_ = l - 2
        lambda_init = 0.8 - 0.6 * math.exp(-0.3 * l)
        c.begin_phase()
        identb = self.load_const(self.k_identb, [128, 128], BF16, "identb")
        onesb = self.load_const(self.k_onesb, [128, 128], BF16, "onesb")
        md = self.load_const(self.k_md, [128, 4, 512], F32, "md")
        lamt = self.load_const(self.lam[:, j_], [128, 4, 128], F32, "lam")
        subg = self.load_const(self.subln[:, j_], [128, 256], F32, "subg")
        c.ts("dve", subg, subg[:, :], subg, subg[:, :], 1.0 - lambda_init, ALU.mult)
        lp = c.sb([128, 2, 128], F32, "lp")
        c.tt("dve", lp, lp[:, 0, :], lamt, lamt[:, 0, :], lamt, lamt[:, 1, :], ALU.mult)
        c.tt("dve", lp, lp[:, 1, :], lamt, lamt[:, 2, :], lamt, lamt[:, 3, :], ALU.mult)
        ls = c.sb([128, 2], F32, "ls")
        c.op("dve", lambda en: en.tensor_reduce(out=ls[:, :], in_=lp[:, :, :], axis=AX.X, op=ALU.add), [ls], [lp])
        c.act(ls, ls[:, :], ls, ls[:, :], AF.Exp)
        neglam = c.sb([128, 1], F32, "neglam")
        c.tt("dve", neglam, neglam[:, :], ls, ls[:, 1:2], ls, ls[:, 0:1], ALU.subtract)
        c.ts("dve", neglam, neglam[:, :], neglam, neglam[:, :], -lambda_init, ALU.add)

        ktp = c.sbpool(2, [128, 2, Tp], BF16, "kt")
        vxp = c.sbpool(2, [128, NB, 257], BF16, "vx")
        for t in vxp.tiles:
            c.memset("pool", t, t[:, :, 256:257], 1.0)
        qtp = c.sbpool(2, [128, 2, 512], BF16, "qt")
        sqp = c.sbpool(2, [128, 512], BF16, "sqq")
        km2 = c.sb([128, 2], F32, "km2")
        kmt = c.sbpool(2, [128, 1], F32, "kmt")
        negBp = c.sbpool(3, [128, 512], F32, "negB")
        tmpp = c.sbpool(4, [128, 512], F32, "tmp")
        ptp = c.sbpool(4, [128, 512], BF16, "pt")
        on0p = c.sbpool(2, [128, 4, 256], F32, "on0")
        on1p = c.sbpool(2, [128, 256], F32, "on1")
        junk = c.sbpool(2, [128, 256], F32, "junk")
        colp = c.sbpool(4, [128, 1], F32, "col")
        osp = c.sbpool(2, [128, 256], BF16, "osn")
        ontp = c.sbpool(2, [128, 2, 128], BF16, "ont")
        pss = c.pspool(2, [128, 512], F32, "pss")
        pso = c.pspool(4, [128, 257], F32, "pso")
        psq = c.pspool(1, [128, 512], F32, "psq")
        pst = c.pspool(1, [128, 2, 128], BF16, "pst")

        for h in range(8):
            kt = ktp.next(); vx = vxp.next()
            c.dma("sp", kt[:, :, :], self.a_k[h * 256:(h + 1) * 256, :].rearrange("(m p) t -> p m t", p=128), out_t=kt)
            for n8 in range(0, NB, 8):
                n9 = min(NB, n8 + 8)
                c.dma("sp", vx[:, n8:n9, 0:256],
                      self.a_v[n8 * 128:n9 * 128, h * 256:(h + 1) * 256].rearrange("(nb p) f -> p nb f", p=128), out_t=vx)
            for m in range(2):
                first = True
                for (t0, w) in token_tiles(Tp):
                    s2 = sqp.next()
                    c.tt("pool", s2, s2[:, :w], kt, kt[:, m, t0:t0 + w], kt, kt[:, m, t0:t0 + w], ALU.mult)
                    pq = psq.next()
                    c.mm(pq, pq[:, :w], onesb, onesb[:, :], s2, s2[:, :w])
                    if first:
                        c.op("dve", lambda en, pq=pq, w=w, m=m: en.tensor_reduce(out=km2[:, m:m + 1], in_=pq[:, :w], axis=AX.X, op=ALU.max),
                             [km2], [pq])
                        first = False
                    else:
                        k1 = kmt.next()
                        c.op("dve", lambda en, pq=pq, w=w, k1=k1: en.tensor_reduce(out=k1[:, :], in_=pq[:, :w], axis=AX.X, op=ALU.max),
                             [k1], [pq])
                        c.tt("dve", km2, km2[:, m:m + 1], km2, km2[:, m:m + 1], k1, k1[:, :], ALU.max)
            c.ts("dve", km2, km2[:, :], km2, km2[:, :], 1.05, ALU.mult)
            for qi, (t0, w) in enumerate(token_tiles(Tp)):
                if t0 >= T:
                    continue
                nqs = w // 128
                qt = qtp.next()
                c.dma("sp", qt[:, :, :w], self.a_q[h * 256:(h + 1) * 256, t0:t0 + w].rearrange("(m p) t -> p m t", p=128),
                      out_t=qt)
                on0 = on0p.next()
                for m in range(2):
                    s2 = sqp.next()
                    c.tt("pool", s2, s2[:, :w], qt, qt[:, m, :w], qt, qt[:, m, :w], ALU.mult)
                    pq = psq.next()
                    c.mm(pq, pq[:, :w], onesb, onesb[:, :], s2, s2[:, :w])
                    negB = negBp.next()
                    c.act(negB, negB[:, :w], pq, pq[:, :w], AF.Sqrt, scale=km2[:, m:m + 1], extra_in=[km2])
                    c.ts("dve", negB, negB[:, :w], negB, negB[:, :w], -1.0, ALU.mult)
                    accs = [pso.next() for _ in range(nqs)]
                    kb_last = t0 // 128 + nqs - 1
                    def emit_pv(kb, pt, t0=t0, nqs=nqs, accs=accs):
                        d = kb - t0 // 128
                        for qs in range(nqs):
                            if d > qs:
                                continue
                            c.mm(accs[qs], accs[qs][:, :], pt, pt[:, qs * 128:(qs + 1) * 128], vx, vx[:, kb, :],
                                 start=(kb == 0), stop=(d == qs))

                    inflight = []
                    for kb in range(kb_last + 1):
                        d = kb - t0 // 128
                        ps = pss.next()
                        c.mm(ps, ps[:, :w], kt, kt[:, m, kb * 128:(kb + 1) * 128], qt, qt[:, m, :w])
                        tmp = tmpp.next()
                        c.tt("dve", tmp, tmp[:, :w], ps, ps[:, :w], negB, negB[:, :w], ALU.add)
                        if d >= 0:
                            c.tt("pool", tmp, tmp[:, :w], tmp, tmp[:, :w], md, md[:, d, :w], ALU.add)
                        pt = ptp.next()
                        c.act(pt, pt[:, :w], tmp, tmp[:, :w], AF.Exp)
                        inflight.append((kb, pt))
                        if len(inflight) > 2:
                            emit_pv(*inflight.pop(0))
                    while inflight:
                        emit_pv(*inflight.pop(0))
                    for qs in range(nqs):
                        a = accs[qs]
                        rl_ = colp.next()
                        c.recip(rl_, rl_[:, :], a, a[:, 256:257])
                        if m == 0:
                            c.ts("dve", on0, on0[:, qs, :], a, a[:, 0:256], rl_[:, 0:1], ALU.mult, s_t=[rl_])
                        else:
                            on1 = on1p.next()
                            c.ts("dve", on1, on1[:, :], a, a[:, 0:256], rl_[:, 0:1], ALU.mult, s_t=[rl_])
                            c.stt("dve", on1, on1[:, :], on1, on1[:, :], neglam[:, 0:1], on0, on0[:, qs, :],
                                  ALU.mult, ALU.add, s_t=[neglam])
                            jk = junk.next(); ssq = colp.next()
                            c.memset("dve", ssq, ssq[:, :], 0.0)
                            c.act(jk, jk[:, :], on1, on1[:, :], AF.Square, accum=ssq[:, :], accum_t=ssq)
                            c.act(ssq, ssq[:, :], ssq, ssq[:, :], AF.Sqrt, bias=EPS, scale=1.0 / 256)
                            c.recip(ssq, ssq[:, :], ssq, ssq[:, :])
                            osn = osp.next()
                            c.stt("dve", osn, osn[:, :], on1, on1[:, :], ssq[:, 0:1], subg, subg[:, :],
                                  ALU.mult, ALU.mult, s_t=[ssq])
                            ptr = pst.next()
                            for e2 in range(2):
                                c.tr(ptr, ptr[:, e2, :], osn, osn[:, e2 * 128:(e2 + 1) * 128], identb, identb[:, :])
                            ont = ontp.next()
                            c.cp("act", ont, ont[:, :, :], ptr, ptr[:, :, :])
                            tq = t0 + qs * 128
                            c.dma("pool", self.a_on[h * 256:(h + 1) * 256, tq:tq + 128].rearrange("(e p) t -> p e t", p=128),
                                  ont[:, :, :], in_t=ont)
        c.end_phase()

    def phase_final(self):
        c = self.c
        c.barrier()

    def build(self, stop=None):
        self.declare()
        self.phase_init()
        for l in range(self.n_layers):
            if l < 2:
                self.phase_g1(l)
                if stop == "g1":
                    break
                self.phase_g2(l)
                if stop == "g2":
                    break
                self.phase_outproj(l, self.s_og, 32, self.wbf["gout"][l])
                if stop == "g3":
                    break
            else:
                if l == 2:
                    self.phase_qkproj(l, "kv")
                self.phase_qkproj(l, "q")
                self.phase_attn(l)
                self.phase_outproj(l, self.a_on, 16, self.wbf["o"][l - 2])
            self.phase_mlp(l)
        self.phase_final()
        return self.nc


def host_consts(Tp):
    bf = ml_dtypes.bfloat16
    p = np.arange(128)[:, None]
    f = np.arange(128)[None, :]
    cst = {}
    cst["identb"] = np.eye(128, dtype=np.float32).astype(bf)
    cst["onesb"] = np.ones((128, 128), np.float32).astype(bf)
    cst["onesf"] = np.ones((128, 128), np.float32)
    cst["identf"] = np.eye(128, dtype=np.float32)
    sel = np.zeros((32, 32, 128), np.float32)
    for hh in range(32):
        sel[hh, hh, :] = 1.0
    cst["sel"] = sel
    ii = np.arange(128)[:, None]; jj = np.arange(128)[None, :]
    m4 = np.zeros((128, 8, 4, 128), np.float32)
    m4[:, 0] = np.eye(128, dtype=np.float32)[:, None, :]
    def M(sv):
        n = 2 ** sv
        return ((ii // (2 * n) == jj // (2 * n)) & (ii % (2 * n) >= n) & (jj % (2 * n) < n)).astype(np.float32)
    m4[:, 1] = M(0).T[:, None, :]
    for sv in range(1, 7):
        m4[:, 1 + sv] = M(sv)[:, None, :]
    cst["m4"] = m4.astype(bf)
    cst["Umat"] = (p <= f).astype(np.float32)
    cst["negU"] = np.where(f >= p, 0.0, NEG).astype(np.float32)
    cst["M2"] = np.where(p > f, 0.0, NEG).astype(np.float32)
    R = np.zeros((128, 128), np.float32)
    for m in range(16):
        R[m, m + 16] = -1.0
        R[m + 16, m] = 1.0
    cst["RT"] = np.ascontiguousarray(R.T)
    pos = np.arange(Tp, dtype=np.float32)
    inv = (500000.0 ** (-np.arange(0, 32, 2, dtype=np.float32) / 32)).astype(np.float32)
    ang = pos[None, :] * inv[:, None]
    cosT = np.ones((128, Tp), np.float32)
    sinT = np.zeros((128, Tp), np.float32)
    cosT[0:16] = np.cos(ang); cosT[16:32] = np.cos(ang)
    sinT[0:16] = np.sin(ang); sinT[16:32] = np.sin(ang)
    cst["cosT"], cst["sinT"] = cosT, sinT
    md = np.zeros((128, 4, 512), np.float32)
    ff = np.arange(512)[None, :]
    for d in range(4):
        md[:, d, :] = np.where(d * 128 + p <= ff, 0.0, NEG)
    cst["maskd"] = md
    return cst


def rep(a, n=128):
    return np.ascontiguousarray(np.broadcast_to(a[None], (n,) + a.shape)).astype(np.float32)


def host_params(inp):
    ng = np.asarray(inp["norm_gains"], np.float32)
    d = {}
    d["gains"] = np.ascontiguousarray(ng.reshape(16, 16, 128).transpose(2, 0, 1))
    d["kvg"] = np.ascontiguousarray(np.asarray(inp["kv_norm"], np.float32).reshape(16, 128).T)
    cw = np.asarray(inp["gdn_conv_w"], np.float32)
    d["convw"] = np.ascontiguousarray(cw.reshape(2, 4, 64, 128).transpose(3, 0, 1, 2))
    d["alog"] = rep(np.asarray(inp["gdn_a_log"], np.float32))
    d["dtb"] = rep(np.asarray(inp["gdn_dt_bias"], np.float32))
    d["onorm"] = np.ascontiguousarray(np.asarray(inp["gdn_o_norm"], np.float32).T)
    d["lam"] = rep(np.asarray(inp["diff_lambda"], np.float32))
    d["subln"] = rep(np.asarray(inp["diff_subln"], np.float32))
    return d


_CACHE = {}


def kernel(**inputs):
    x = np.asarray(inputs["x"], np.float32)
    B, S, _ = x.shape
    T = NMETA + S
    Tp = ((T + 127) // 128) * 128
    key = (T,)
    if key not in _CACHE:
        _CACHE[key] = Builder(T).build()
    nc = _CACHE[key]
    cst = host_consts(Tp)
    prm = host_params(inputs)
    meta = np.asarray(inputs["meta_tokens"], np.float32)
    shared = dict(cst)
    shared.update(prm)
    for k in ("mlp_w_up", "mlp_w_down", "gdn_w_in", "gdn_w_out", "w_kv", "diff_w_q", "diff_w_o"):
        shared[k] = np.ascontiguousarray(np.asarray(inputs[k], np.float32))
    in_maps = []
    for core in range(8):
        b = core % B
        h0 = np.zeros((D, Tp), np.float32)
        h0[:, :NMETA] = meta.T
        h0[:, NMETA:T] = x[b].T
        m = dict(shared)
        m["h0"] = h0
        in_maps.append(m)
    res = run_bass_kernel_spmd(nc, in_maps, core_ids=list(range(8)))
    out = np.empty((B, S, D), np.float32)
    for b in range(B):
        hT = res.results[b]["hT"]
        out[b] = hT[:, NMETA:T].T
    return out
```

```python
import math, os
from contextlib import ExitStack
import numpy as np
import ml_dtypes
import concourse.bass as bass
import concourse.mybir as mybir
from concourse.bass_utils import run_bass_kernel_spmd

F32, BF16 = mybir.dt.float32, mybir.dt.bfloat16
AF = mybir.ActivationFunctionType
ALU = mybir.AluOpType
AX = mybir.AxisListType

D = 2048
DFF = 8192
NMETA = 16
GIN = 12352
EPS = 1e-6
NEG = -30000.0


class Tl:
    __slots__ = ("ap", "w", "r", "dsem", "name", "psum")

    def __init__(self, ap, name=""):
        self.ap = ap
        self.w = {}
        self.r = {}
        self.dsem = {}
        self.name = name
        self.psum = False

    def __getitem__(self, k):
        return self.ap[k]


class Pool_:
    def __init__(self, tiles):
        self.tiles = tiles
        self.i = 0

    def next(self):
        t = self.tiles[self.i % len(self.tiles)]
        self.i += 1
        return t


class Ctx:
    def __init__(self, nc):
        self.nc = nc
        self.es = ExitStack()
        self.eng = {"pe": nc.tensor, "act": nc.scalar, "dve": nc.vector, "pool": nc.gpsimd, "sp": nc.sync}
        self.sem = {k: self.es.enter_context(nc.semaphore("s_" + k)) for k in ("pe", "act", "dve", "pool")}
        self.cnt = {k: 0 for k in self.sem}
        self.waited = {}
        self.dsems = []
        self.dq = {"sp": [], "pool": []}
        self.dnext = {"sp": 0, "pool": 0}
        self.uid = 0
        self.phase_es = None

    def begin_phase(self):
        self.phase_es = ExitStack()
        self.dnext = {"sp": 0, "pool": 0}

    def end_phase(self):
        self.barrier()
        self.phase_es.close()
        self.phase_es = None

    def sb(self, shape, dtype, name="t"):
        self.uid += 1
        h = self.phase_es.enter_context(self.nc.sbuf_tensor(f"{name}_{self.uid}", list(shape), dtype))
        return Tl(h[tuple(slice(None) for _ in shape)], name)

    def ps(self, shape, dtype, name="p"):
        self.uid += 1
        nb = 512 if dtype == F32 else 1024
        h = self.phase_es.enter_context(self.nc.psum_tensor(f"{name}_{self.uid}", [shape[0], nb], dtype))
        n = 1
        for d in shape[1:]:
            n *= d
        assert n <= nb
        v = h[:, 0:n]
        if len(shape) == 3:
            v = v.rearrange("p (a b) -> p a b", a=shape[1])
        t = Tl(v, name)
        t.psum = True
        return t

    def sbpool(self, n, shape, dtype, name="t"):
        return Pool_([self.sb(shape, dtype, name) for _ in range(n)])

    def pspool(self, n, shape, dtype, name="p"):
        return Pool_([self.ps(shape, dtype, name) for _ in range(n)])

    def _dsem(self, t, q):
        if q not in t.dsem:
            lst = self.dq[q]
            if self.dnext[q] >= len(lst):
                s = self.es.enter_context(self.nc.semaphore(f"d{q}_{len(lst)}"))
                ent = [s, 0]
                lst.append(ent)
                self.dsems.append(ent)
            t.dsem[q] = lst[self.dnext[q]]
            self.dnext[q] += 1
        return t.dsem[q]

    def _wait(self, e, ev):
        key, sem, val = ev
        if e == "pe" and key == "pe":
            return
        k = (e, key)
        if self.waited.get(k, 0) >= val:
            return
        self.eng[e].wait_ge(sem, val)
        self.waited[k] = val

    def _deps(self, e, outs, ins):
        for t in ins:
            if t is None:
                continue
            for ev in t.w.values():
                self._wait(e, ev)
            if t.psum:
                for ev in list(t.r.values()):
                    if ev[0] != e:
                        self._wait(e, ev)
        for t in outs:
            if t is None:
                continue
            for ev in t.w.values():
                self._wait(e, ev)
            for ev in t.r.values():
                self._wait(e, ev)

    def _mark(self, ev, outs, ins):
        for t in ins:
            if t is not None:
                t.r[ev[0]] = ev
        for t in outs:
            if t is not None:
                t.w = {ev[0]: ev}
                t.r = {}

    def op(self, e, inst_fn, outs, ins):
        self._deps(e, outs, ins)
        inst = inst_fn(self.eng[e])
        self.cnt[e] += 1
        inst.then_inc(self.sem[e], 1)
        self._mark((e, self.sem[e], self.cnt[e]), outs, ins)

    def dma(self, q, out_ap, in_ap, out_t=None, in_t=None):
        self._deps(q, [out_t], [in_t])
        owner = out_t if out_t is not None else in_t
        ds = self._dsem(owner, q)
        inst = self.eng[q].dma_start(out=out_ap, in_=in_ap)
        ds[1] += 16
        inst.then_inc(ds[0], 16)
        self._mark((id(ds), ds[0], ds[1]), [out_t], [in_t])

    def barrier(self):
        for e in ("pe", "act", "dve", "pool", "sp"):
            for k in self.sem:
                if k != e and self.cnt[k] > 0:
                    self._wait(e, (k, self.sem[k], self.cnt[k]))
            for ds in self.dsems:
                if ds[1] > 0:
                    self._wait(e, (id(ds), ds[0], ds[1]))

    def mm(self, out_t, out_ap, lhsT_t, lhsT_ap, rhs_t, rhs_ap, start=True, stop=True):
        self.op("pe", lambda en: en.matmul(out_ap, lhsT=lhsT_ap, rhs=rhs_ap, start=start, stop=stop),
                [out_t], [lhsT_t, rhs_t])

    def tr(self, out_t, out_ap, in_t, in_ap, id_t, id_ap):
        self.op("pe", lambda en: en.transpose(out_ap, in_ap, id_ap), [out_t], [in_t, id_t])

    def act(self, out_t, out_ap, in_t, in_ap, func, bias=0.0, scale=1.0, extra_in=(), accum=None, accum_t=None):
        kw = {}
        if accum is not None:
            kw["accum_out"] = accum
        self.op("act", lambda en: en.activation(out=out_ap, in_=in_ap, func=func, bias=bias, scale=scale, **kw),
                [out_t] + ([accum_t] if accum_t is not None else []), [in_t] + list(extra_in))

    def tt(self, e, out_t, out_ap, a_t, a_ap, b_t, b_ap, op):
        self.op(e, lambda en: en.tensor_tensor(out=out_ap, in0=a_ap, in1=b_ap, op=op), [out_t], [a_t, b_t])

    def ts(self, e, out_t, out_ap, a_t, a_ap, s1, op0, s2=None, op1=None, s_t=()):
        if op1 is None:
            f = lambda en: en.tensor_scalar(out=out_ap, in0=a_ap, scalar1=s1, scalar2=None, op0=op0)
        else:
            f = lambda en: en.tensor_scalar(out=out_ap, in0=a_ap, scalar1=s1, scalar2=s2, op0=op0, op1=op1)
        self.op(e, f, [out_t], [a_t] + list(s_t))

    def stt(self, e, out_t, out_ap, a_t, a_ap, sc, b_t, b_ap, op0, op1, s_t=()):
        e = "dve"
        self.op(e, lambda en: en.scalar_tensor_tensor(out=out_ap, in0=a_ap, scalar=sc, in1=b_ap, op0=op0, op1=op1),
                [out_t], [a_t, b_t] + list(s_t))

    def cp(self, e, out_t, out_ap, in_t, in_ap):
        if e == "act":
            self.op(e, lambda en: en.copy(out=out_ap, in_=in_ap), [out_t], [in_t])
        else:
            self.op(e, lambda en: en.tensor_copy(out=out_ap, in_=in_ap), [out_t], [in_t])

    def memset(self, e, t, ap, val):
        self.op(e, lambda en: en.memset(ap, val), [t], [])

    def recip(self, out_t, out_ap, in_t, in_ap):
        self.op("dve", lambda en: en.reciprocal(out=out_ap, in_=in_ap), [out_t], [in_t])


def token_tiles(Tp):
    res = []
    t = 0
    while t < Tp:
        w = min(512, Tp - t)
        res.append((t, w))
        t += w
    return res


class Builder:
    def __init__(self, T, n_layers=4, debug=False):
        self.T = T
        self.Tp = ((T + 127) // 128) * 128
        self.NC = self.Tp // 128
        self.n_layers = n_layers
        self.nc = bass.Bass("TRN2", target_bir_lowering=False)
        self.c = Ctx(self.nc)
        self.debug = debug

    def declare(self):
        nc, Tp = self.nc, self.Tp

        def inp(name, shape, dt=F32):
            return nc.dram_tensor(name, list(shape), dt, kind="ExternalInput").ap()

        def scr(name, shape, dt):
            kind = "ExternalOutput" if (self.debug and name.startswith("s_")) else "Internal"
            return nc.dram_tensor(name, list(shape), dt, kind=kind).ap()

        self.h0 = inp("h0", [D, Tp])
        self.w32 = {
            "up": inp("mlp_w_up", [4, D, DFF]), "down": inp("mlp_w_down", [4, DFF, D]),
            "gin": inp("gdn_w_in", [2, D, GIN]), "gout": inp("gdn_w_out", [2, 4096, D]),
            "kv": inp("w_kv", [D, 4096]), "q": inp("diff_w_q", [2, D, D]), "o": inp("diff_w_o", [2, D, D]),
        }
        self.wbf = {k: scr(k + "_bf", v.shape, BF16) for k, v in self.w32.items()}
        self.gains = inp("gains", [128, 16, 16])
        self.kvg = inp("kvg", [128, 16])
        self.convw = inp("convw", [128, 2, 4, 64])
        self.alog = inp("alog", [128, 2, 32])
        self.dtb = inp("dtb", [128, 2, 32])
        self.onorm = inp("onorm", [128, 2])
        self.lam = inp("lam", [128, 2, 4, 128])
        self.subln = inp("subln", [128, 2, 256])
        self.k_identb = inp("identb", [128, 128], BF16)
        self.k_onesb = inp("onesb", [128, 128], BF16)
        self.k_onesf = inp("onesf", [128, 128])
        self.k_U = inp("Umat", [128, 128])
        self.k_sel = inp("sel", [32, 32, 128])
        self.k_identf = inp("identf", [128, 128])
        self.k_m4 = inp("m4", [128, 8, 4, 128], BF16)
        self.k_negU = inp("negU", [128, 128])
        self.k_M2 = inp("M2", [128, 128])
        self.k_RT = inp("RT", [128, 128])
        self.k_cos = inp("cosT", [128, Tp])
        self.k_sin = inp("sinT", [128, Tp])
        self.k_md = inp("maskd", [128, 4, 512])
        self.hT = nc.dram_tensor("hT", [D, Tp], F32, kind="ExternalOutput").ap()
        self.s_q = scr("s_q", [D, Tp], BF16)
        self.s_k = scr("s_k", [D, Tp], BF16)
        self.s_v = scr("s_v", [4096, Tp], BF16)
        self.s_z = scr("s_z", [4096, Tp], BF16)
        self.s_bg = scr("s_bg", [Tp, 64], F32)
        self.s_og = scr("s_og", [4096, Tp], BF16)
        self.a_k = scr("a_k", [D, Tp], BF16)
        self.a_v = scr("a_v", [Tp, D], BF16)
        self.a_q = scr("a_q", [D, Tp], BF16)
        self.a_on = scr("a_on", [D, Tp], BF16)

    def phase_init(self):
        c = self.c
        c.begin_phase()
        dummy = c.sb([128, 1], F32, "dummy")
        order = []
        for l in range(2):
            order += [("gin", l), ("gout", l), ("up", l), ("down", l)]
        order += [("kv", None)]
        for l in range(2):
            order += [("q", l), ("o", l), ("up", 2 + l), ("down", 2 + l)]
        for name, l in order:
            src = self.w32[name] if l is None else self.w32[name][l]
            dst = self.wbf[name] if l is None else self.wbf[name][l]
            K = src.shape[0]
            for r0 in range(0, K, 128):
                c.dma("pool", dst[r0:r0 + 128, :], src[r0:r0 + 128, :], in_t=dummy)
        for r0 in range(0, D, 128):
            c.dma("sp", self.hT[r0:r0 + 128, :], self.h0[r0:r0 + 128, :], in_t=dummy)
        c.end_phase()

    def load_const(self, ap_dram, shape, dt, name):
        c = self.c
        t = c.sb(shape, dt, name)
        c.dma("sp", t.ap, ap_dram, out_t=t)
        return t

    def rmsnorm_fm(self, x, nkc, w, gain_t, gain_ap_fn, out_hn, sq, ps_pool, rs_pool, onesb, nfeat):
        c = self.c
        ps = ps_pool.next()
        for kc in range(nkc):
            s1 = sq.next()
            c.act(s1, s1[:, :w], x, x[:, kc, :w], AF.Square)
            c.mm(ps, ps[:, :w], onesb, onesb[:, :], s1, s1[:, :w], start=(kc == 0), stop=(kc == nkc - 1))
        rs = rs_pool.next()
        c.act(rs, rs[:, :w], ps, ps[:, :w], AF.Ln, bias=EPS, scale=1.0 / nfeat)
        c.act(rs, rs[:, :w], rs, rs[:, :w], AF.Exp, scale=-0.5)
        if out_hn is not None:
            for kc in range(nkc):
                e = "dve" if kc % 2 == 0 else "pool"
                c.stt(e, out_hn, out_hn[:, kc, :w], x, x[:, kc, :w], gain_ap_fn(kc), rs, rs[:, :w],
                      ALU.mult, ALU.mult, s_t=[gain_t])
        return rs

    def linear(self, W, k0, KC, n0, n1, bc, x_t, x_fn, w, wpool, pspool, epi):
        c = self.c
        pending = None
        for nb in range(n0, n1, bc):
            cols = min(bc, n1 - nb)
            wt = wpool.next()
            for k8 in range(0, KC, 8):
                c.dma("sp", wt[:, k8:k8 + 8, 0:cols],
                      W[k0 + k8 * 128:k0 + (k8 + 8) * 128, nb:nb + cols].rearrange("(kc p) n -> p kc n", p=128), out_t=wt)
            for j in range(cols // 128):
                ps = pspool.next()
                for kc in range(KC):
                    c.mm(ps, ps[:, :w], wt, wt[:, kc, j * 128:(j + 1) * 128], x_t, x_fn(kc),
                         start=(kc == 0), stop=(kc == KC - 1))
                tail = epi((nb - n0) // 128 + j, ps)
                if pending is not None:
                    pending()
                pending = tail
        if pending is not None:
            pending()

    def postnorm_residual(self, y, w, t0, gains, gidx, h32, sq, ps1, rsp, onesb):
        c = self.c
        rs = self.rmsnorm_fm(y, 16, w, None, None, None, sq, ps1, rsp, onesb, D)
        for kc in range(16):
            e = "dve" if kc % 2 == 0 else "pool"
            c.stt(e, y, y[:, kc, :w], y, y[:, kc, :w], gains[:, gidx, kc:kc + 1], rs, rs[:, :w],
                  ALU.mult, ALU.mult, s_t=[gains])
            c.tt(e, h32, h32[:, kc, :w], y, y[:, kc, :w], h32, h32[:, kc, :w], ALU.add)
        c.dma("pool", self.hT[:, t0:t0 + w].rearrange("(kc p) t -> p kc t", p=128), h32[:, :, :w], in_t=h32)

    def phase_g1(self, l):
        c = self.c
        Tp = self.Tp
        c.begin_phase()
        gains = self.load_const(self.gains, [128, 16, 16], F32, "gains")
        onesb = self.load_const(self.k_onesb, [128, 128], BF16, "onesb")
        cw = self.load_const(self.convw[:, l], [128, 4, 64], F32, "cw")
        alog = self.load_const(self.alog[:, l], [128, 32], F32, "alog")
        dtb = self.load_const(self.dtb[:, l], [128, 32], F32, "dtb")
        negea = c.sb([128, 32], F32, "negea")
        c.act(negea, negea[:, :], alog, alog[:, :], AF.Exp)
        c.ts("dve", negea, negea[:, :], negea, negea[:, :], -1.0, ALU.mult)
        W = self.wbf["gin"][l]
        wlast = c.sb([128, 16, 64], BF16, "wlast")
        c.dma("sp", wlast[:, :, :], W[:, 12288:12352].rearrange("(kc p) n -> p kc n", p=128), out_t=wlast)
        halo = c.sb([128, 64, 3], F32, "halo")
        c.memset("pool", halo, halo[:, :, :], 0.0)
        h32p = c.sbpool(2, [128, 16, 512], F32, "h32")
        hn = c.sb([128, 16, 512], BF16, "hn")
        sq = c.sbpool(4, [128, 512], BF16, "sq")
        wpool = c.sbpool(3, [128, 16, 512], BF16, "wblk")
        psp = c.pspool(4, [128, 512], F32, "psl")
        ps1 = c.pspool(2, [128, 512], F32, "ps1")
        psba = c.pspool(1, [128, 64], F32, "psba")
        rsp = c.sbpool(2, [128, 512], F32, "rs")
        xpp = c.sbpool(3, [128, 515], F32, "xp")
        yp = c.sbpool(3, [128, 512], F32, "y")
        sp_ = c.sbpool(8, [128, 512], F32, "s")
        sqp = c.sbpool(8, [128, 512], BF16, "sq1")
        r1p = c.sbpool(5, [128, 512], F32, "r1")
        obp = c.sbpool(6, [128, 512], BF16, "ob")
        bap = c.sbpool(2, [128, 64], F32, "ba")
        tmp32 = c.sbpool(4, [128, 32], F32, "tmp32")

        for (t0, w) in token_tiles(Tp):
            h32 = h32p.next()
            c.dma("sp", h32[:, :, :w], self.hT[:, t0:t0 + w].rearrange("(kc p) t -> p kc t", p=128), out_t=h32)
            self.rmsnorm_fm(h32, 16, w, gains, lambda kc: gains[:, l * 4 + 0, kc:kc + 1], hn, sq, ps1, rsp, onesb, D)

            qk_batch = []

            def epi(oc, ps, t0=t0, w=w, qk_batch=qk_batch):
                if oc < 64:
                    xp = xpp.next()
                    c.cp("pool", xp, xp[:, 0:3], halo, halo[:, oc, :])
                    c.cp("act", xp, xp[:, 3:3 + w], ps, ps[:, :w])
                    c.cp("pool", halo, halo[:, oc, :], xp, xp[:, w:w + 3])
                    y = yp.next()
                    c.ts("dve", y, y[:, :w], xp, xp[:, 0:w], cw[:, 0, oc:oc + 1], ALU.mult, s_t=[cw])
                    for j in range(1, 4):
                        c.stt("dve", y, y[:, :w], xp, xp[:, j:j + w], cw[:, j, oc:oc + 1], y, y[:, :w],
                              ALU.mult, ALU.add, s_t=[cw])
                    if oc >= 32:
                        ob = obp.next()
                        c.act(ob, ob[:, :w], y, y[:, :w], AF.Silu)
                        r = (oc - 32) * 128
                        c.dma("pool", self.s_v[r:r + 128, t0:t0 + w], ob[:, :w], in_t=ob)
                    else:
                        s = sp_.next()
                        c.act(s, s[:, :w], y, y[:, :w], AF.Silu)
                        sq1 = sqp.next()
                        c.tt("pool", sq1, sq1[:, :w], s, s[:, :w], s, s[:, :w], ALU.mult)
                        qk_batch.append((oc, s, sq1))
                        if len(qk_batch) < 4:
                            return None
                        batch = list(qk_batch)
                        del qk_batch[:]

                        def tail(batch=batch, w=w, t0=t0):
                            r1s = []
                            for (oc_, s_, sq_) in batch:
                                p1 = ps1.next()
                                c.mm(p1, p1[:, :w], onesb, onesb[:, :], sq_, sq_[:, :w])
                                r1 = r1p.next()
                                c.act(r1, r1[:, :w], p1, p1[:, :w], AF.Ln, bias=EPS, scale=1.0)
                                r1s.append(r1)
                            for r1 in r1s:
                                c.act(r1, r1[:, :w], r1, r1[:, :w], AF.Exp, scale=-0.5)
                            for (oc_, s_, sq_), r1 in zip(batch, r1s):
                                ob = obp.next()
                                if oc_ < 16:
                                    c.stt("dve", ob, ob[:, :w], s_, s_[:, :w], 128.0 ** -0.5, r1, r1[:, :w], ALU.mult, ALU.mult)
                                    c.dma("pool", self.s_q[oc_ * 128:(oc_ + 1) * 128, t0:t0 + w], ob[:, :w], in_t=ob)
                                else:
                                    c.tt("dve", ob, ob[:, :w], s_, s_[:, :w], r1, r1[:, :w], ALU.mult)
                                    r = (oc_ - 16) * 128
                                    c.dma("pool", self.s_k[r:r + 128, t0:t0 + w], ob[:, :w], in_t=ob)
                        return tail
                else:
                    ob = obp.next()
                    c.act(ob, ob[:, :w], ps, ps[:, :w], AF.Silu)
                    r = (oc - 64) * 128
                    c.dma("pool", self.s_z[r:r + 128, t0:t0 + w], ob[:, :w], in_t=ob)

            self.linear(W, 0, 16, 0, 12288, 512, hn, lambda kc, w=w: hn[:, kc, :w], w, wpool, psp, epi)
            for sbk in range(w // 128):
                pb = psba.next()
                for kc in range(16):
                    c.mm(pb, pb[:, :], hn, hn[:, kc, sbk * 128:(sbk + 1) * 128], wlast, wlast[:, kc, :],
                         start=(kc == 0), stop=(kc == 15))
                ba = bap.next()
                c.act(ba, ba[:, 0:32], pb, pb[:, 0:32], AF.Sigmoid)
                a = tmp32.next()
                c.tt("dve", a, a[:, :], pb, pb[:, 32:64], dtb, dtb[:, :], ALU.add)
                ab = tmp32.next()
                c.act(ab, ab[:, :], a, a[:, :], AF.Abs)
                c.act(ab, ab[:, :], ab, ab[:, :], AF.Exp, scale=-1.0)
                c.act(ab, ab[:, :], ab, ab[:, :], AF.Ln, bias=1.0)
                c.stt("dve", a, a[:, :], a, a[:, :], 0.0, ab, ab[:, :], ALU.max, ALU.add)
                c.tt("dve", ba, ba[:, 32:64], a, a[:, :], negea, negea[:, :], ALU.mult)
                tt0 = t0 + sbk * 128
                c.dma("pool", self.s_bg[tt0:tt0 + 128, :], ba[:, :], in_t=ba)
        c.end_phase()

    def phase_g2(self, l):
        c = self.c
        c.begin_phase()
        identb = self.load_const(self.k_identb, [128, 128], BF16, "identb")
        onesb = self.load_const(self.k_onesb, [128, 128], BF16, "onesb")
        onesf = self.load_const(self.k_onesf, [128, 128], F32, "onesf")
        Um = self.load_const(self.k_U, [128, 128], F32, "Um")
        negU = self.load_const(self.k_negU, [128, 128], F32, "negU")
        M2 = self.load_const(self.k_M2, [128, 128], F32, "M2")
        onorm = self.load_const(self.onorm, [128, 2], F32, "onorm")
        S32h = [c.sb([128, 4, 128], F32, "S32") for _ in range(8)]
        Sbfh = [c.sb([128, 4, 128], BF16, "Sbf") for _ in range(8)]
        for t in S32h + Sbfh:
            c.memset("pool", t, t[:, :, :], 0.0)
        qp = c.sbpool(2, [128, 16, 128], BF16, "qc")
        kp = c.sbpool(2, [128, 16, 128], BF16, "kc")
        vp = c.sbpool(2, [128, 32, 128], BF16, "vc")
        zp = c.sbpool(2, [128, 32, 128], BF16, "zc")
        bgp = c.sbpool(2, [128, 64], F32, "bg")
        gtp = c.sbpool(2, [32, 128], F32, "GT")
        sel = self.load_const(self.k_sel, [32, 32, 128], F32, "sel")
        identf = self.load_const(self.k_identf, [128, 128], F32, "identf")
        m4 = self.load_const(self.k_m4, [128, 8, 4, 128], BF16, "m4")
        small = {n: c.sbpool(2, [128, 32], F32, n) for n in ("G", "eG", "nbeG", "negb", "kdc", "eGl")}
        NSLOT = 3
        slots = []
        for _ in range(NSLOT):
            P = {n: c.sb([128, 4, 128], F32, n) for n in ("e1", "e2", "eGbc", "o32", "rs4")}
            P.update({n: c.sb([128, 4, 128], BF16, n) for n in
                      ("aqk", "qg", "TT", "kdec", "bv", "rb", "vn", "sq4", "og", "Pa", "X", "Ts")})
            P["amn"] = c.sb([128, 6, 4, 128], BF16, "amn")
            slots.append(P)
        psf = c.pspool(5, [128, 4, 128], F32, "psf")
        psb = c.pspool(2, [128, 8, 128], BF16, "psb")
        pss = c.pspool(1, [128, 512], F32, "pss")

        def group_gen(hg, P, t0, qc, kc_, vc, zc, bg, G, GT, nbeG, negb, kdc, eGl):
            hs = [4 * hg + j for j in range(4)]
            qhs = [2 * hg, 2 * hg + 1]
            e1, e2, eGbc = P["e1"], P["e2"], P["eGbc"]
            pG = psf.next()
            for j, h in enumerate(hs):
                c.mm(pG, pG[:, j, :], sel, sel[:, h, :], GT, GT[:, :])
            for j, h in enumerate(hs):
                c.stt("dve", e1, e1[:, j, :], pG, pG[:, j, :], G[:, h:h + 1], negU, negU[:, :],
                      ALU.subtract, ALU.add, s_t=[G])
                c.stt("dve", e2, e2[:, j, :], pG, pG[:, j, :], G[:, h:h + 1], M2, M2[:, :],
                      ALU.subtract, ALU.subtract, s_t=[G])
            c.act(eGbc, eGbc[:, :, :], pG, pG[:, :, :], AF.Exp)
            c.act(e1, e1[:, :, :], e1, e1[:, :, :], AF.Exp)
            c.act(e2, e2[:, :, :], e2, e2[:, :, :], AF.Exp, scale=-1.0)
            DmT, Dms = e1, e2
            yield
            pK = psf.next()
            for j, qh in enumerate(qhs):
                c.mm(pK, pK[:, j, :], kc_, kc_[:, qh, :], kc_, kc_[:, qh, :])
                c.mm(pK, pK[:, 2 + j, :], kc_, kc_[:, qh, :], qc, qc[:, qh, :])
            Pa, aqk, qg = P["Pa"], P["aqk"], P["qg"]
            for j, h in enumerate(hs):
                c.stt("dve", Pa, Pa[:, j, :], pK, pK[:, j // 2, :], negb[:, h:h + 1], Dms, Dms[:, j, :],
                      ALU.mult, ALU.mult, s_t=[negb])
                c.tt("dve", aqk, aqk[:, j, :], pK, pK[:, 2 + j // 2, :], DmT, DmT[:, j, :], ALU.mult)
                c.tt("pool", qg, qg[:, j, :], qc, qc[:, qhs[j // 2], :], eGbc, eGbc[:, j, :], ALU.mult)
            yield
            pT = psb.next()
            for j in range(4):
                c.tr(pT, pT[:, j, :], Pa, Pa[:, j, :], identb, identb[:, :])
            for j, qh in enumerate(qhs):
                c.tr(pT, pT[:, 4 + j, :], kc_, kc_[:, qh, :], identb, identb[:, :])
            pV = psb.next()
            for j, h in enumerate(hs):
                c.tr(pV, pV[:, j, :], vc, vc[:, h, :], identb, identb[:, :])
            TT, kdec, bv, amn = P["TT"], P["kdec"], P["bv"], P["amn"]
            c.tt("dve", TT, TT[:, :, :], pT, pT[:, 0:4, :], m4, m4[:, 1, :, :], ALU.mult)
            c.tt("dve", TT, TT[:, :, :], TT, TT[:, :, :], m4, m4[:, 0, :, :], ALU.add)
            for j, h in enumerate(hs):
                c.ts("dve", kdec, kdec[:, j, :], pT, pT[:, 4 + j // 2, :], kdc[:, h:h + 1], ALU.mult, s_t=[kdc])
                c.ts("dve", bv, bv[:, j, :], pV, pV[:, j, :], bg[:, h:h + 1], ALU.mult, s_t=[bg])
            for sv in range(6):
                c.tt("pool", amn, amn[:, sv, :, :], Pa, Pa[:, :, :], m4, m4[:, 2 + sv, :, :], ALU.mult)
            yield
            X, Ts = P["X"], P["Ts"]
            for sv in range(6):
                pX = psf.next()
                for j in range(4):
                    c.mm(pX, pX[:, j, :], amn, amn[:, sv, j, :], TT, TT[:, j, :])
                c.cp("act", X, X[:, :, :], pX, pX[:, :, :])
                pTr = psb.next()
                for j in range(4):
                    c.tr(pTr, pTr[:, j, :], TT, TT[:, j, :], identb, identb[:, :])
                c.cp("dve", Ts, Ts[:, :, :], pTr, pTr[:, 0:4, :])
                yield
                pD = psf.next()
                for j in range(4):
                    c.mm(pD, pD[:, j, :], Ts, Ts[:, j, :], X, X[:, j, :])
                c.tt("dve", TT, TT[:, :, :], pD, pD[:, :, :], TT, TT[:, :, :], ALU.add)
                yield
            rb, vn = P["rb"], P["vn"]
            pkS = psf.next()
            for j, h in enumerate(hs):
                c.mm(pkS, pkS[:, j, :], kc_, kc_[:, qhs[j // 2], :], Sbfh[hg], Sbfh[hg][:, j, :])
            for j, h in enumerate(hs):
                c.stt("dve", rb, rb[:, j, :], pkS, pkS[:, j, :], nbeG[:, h:h + 1], bv, bv[:, j, :],
                      ALU.mult, ALU.add, s_t=[nbeG])
            yield
            pvn = psf.next()
            for j in range(4):
                c.mm(pvn, pvn[:, j, :], TT, TT[:, j, :], rb, rb[:, j, :])
            c.cp("act", vn, vn[:, :, :], pvn, pvn[:, :, :])
            yield
            o32, sq4, rs4, og = P["o32"], P["sq4"], P["rs4"], P["og"]
            po = psf.next()
            for j, h in enumerate(hs):
                c.mm(po, po[:, j, :], Sbfh[hg], Sbfh[hg][:, j, :], qg, qg[:, j, :], start=True, stop=False)
                c.mm(po, po[:, j, :], vn, vn[:, j, :], aqk, aqk[:, j, :], start=False, stop=True)
            pS = psf.next()
            for j in range(4):
                c.mm(pS, pS[:, j, :], kdec, kdec[:, j, :], vn, vn[:, j, :])
            c.cp("act", o32, o32[:, :, :], po, po[:, :, :])
            for j, h in enumerate(hs):
                c.stt("dve", S32h[hg], S32h[hg][:, j, :], S32h[hg], S32h[hg][:, j, :], eGl[:, h:h + 1], pS, pS[:, j, :],
                      ALU.mult, ALU.add, s_t=[eGl])
            c.cp("pool", Sbfh[hg], Sbfh[hg][:, :, :], S32h[hg], S32h[hg][:, :, :])
            c.tt("pool", sq4, sq4[:, :, :], o32, o32[:, :, :], o32, o32[:, :, :], ALU.mult)
            yield
            pq = psf.next()
            c.mm(pq, pq[:, :, :], onesb, onesb[:, :], sq4, sq4[:, :, :])
            c.act(rs4, rs4[:, :, :], pq, pq[:, :, :], AF.Ln, bias=EPS, scale=1.0 / 128)
            c.act(rs4, rs4[:, :, :], rs4, rs4[:, :, :], AF.Exp, scale=-0.5)
            c.stt("dve", o32, o32[:, :, :], o32, o32[:, :, :], onorm[:, l:l + 1], rs4, rs4[:, :, :],
                  ALU.mult, ALU.mult, s_t=[onorm])
            c.tt("pool", og, og[:, :, :], o32, o32[:, :, :], zc, zc[:, 4 * hg:4 * hg + 4, :], ALU.mult)
            c.dma("pool", self.s_og[hg * 512:(hg + 1) * 512, t0:t0 + 128].rearrange("(h p) t -> p h t", p=128),
                  og[:, :, :], in_t=og)

        for ci in range(self.NC):
            t0 = ci * 128
            qc, kc_, vc, zc, bg = qp.next(), kp.next(), vp.next(), zp.next(), bgp.next()
            for hh in range(0, 16, 8):
                c.dma("sp", qc[:, hh:hh + 8, :],
                      self.s_q[hh * 128:(hh + 8) * 128, t0:t0 + 128].rearrange("(h p) t -> p h t", p=128), out_t=qc)
                c.dma("sp", kc_[:, hh:hh + 8, :],
                      self.s_k[hh * 128:(hh + 8) * 128, t0:t0 + 128].rearrange("(h p) t -> p h t", p=128), out_t=kc_)
            for hh in range(0, 32, 8):
                c.dma("sp", vc[:, hh:hh + 8, :],
                      self.s_v[hh * 128:(hh + 8) * 128, t0:t0 + 128].rearrange("(h p) t -> p h t", p=128), out_t=vc)
                c.dma("sp", zc[:, hh:hh + 8, :],
                      self.s_z[hh * 128:(hh + 8) * 128, t0:t0 + 128].rearrange("(h p) t -> p h t", p=128), out_t=zc)
            c.dma("sp", bg[:, :], self.s_bg[t0:t0 + 128, :], out_t=bg)
            p = pss.next()
            c.mm(p, p[:, 0:32], Um, Um[:, :], bg, bg[:, 32:64])
            c.mm(p, p[:, 32:64], onesf, onesf[:, :], bg, bg[:, 32:64])
            G = small["G"].next(); eG = small["eG"].next(); nbeG = small["nbeG"].next()
            negb = small["negb"].next(); kdc = small["kdc"].next(); eGl = small["eGl"].next()
            c.cp("dve", G, G[:, :], p, p[:, 0:32])
            c.act(eG, eG[:, :], p, p[:, 0:32], AF.Exp)
            c.stt("dve", nbeG, nbeG[:, :], eG, eG[:, :], -1.0, bg, bg[:, 0:32], ALU.mult, ALU.mult)
            c.ts("dve", negb, negb[:, :], bg, bg[:, 0:32], -1.0, ALU.mult)
            c.tt("dve", kdc, kdc[:, :], p, p[:, 32:64], G, G[:, :], ALU.subtract)
            c.act(kdc, kdc[:, :], kdc, kdc[:, :], AF.Exp)
            c.act(eGl, eGl[:, :], p, p[:, 32:64], AF.Exp)
            c.tr(p, p[0:32, 128:256], G, G[:, :], identf, identf[:, :])
            GT = gtp.next()
            c.cp("act", GT, GT[:, :], p, p[0:32, 128:256])
            pending = list(range(8))
            active = []
            for sl in range(NSLOT):
                hg = pending.pop(0)
                active.append(group_gen(hg, slots[sl], t0, qc, kc_, vc, zc, bg, G, GT, nbeG, negb, kdc, eGl))
            while any(a is not None for a in active):
                for sl in range(NSLOT):
                    g = active[sl]
                    if g is None:
                        continue
                    try:
                        next(g)
                    except StopIteration:
                        if pending:
                            hg = pending.pop(0)
                            active[sl] = group_gen(hg, slots[sl], t0, qc, kc_, vc, zc, bg, G, GT, nbeG, negb, kdc, eGl)
                        else:
                            active[sl] = None
        c.end_phase()

    def phase_outproj(self, l, src, KC, W):
        c = self.c
        c.begin_phase()
        gains = self.load_const(self.gains, [128, 16, 16], F32, "gains")
        onesb = self.load_const(self.k_onesb, [128, 128], BF16, "onesb")
        xin = c.sbpool(1, [128, KC, 512], BF16, "xin")
        h32p = c.sbpool(2, [128, 16, 512], F32, "h32")
        mix = c.sb([128, 16, 512], F32, "mix")
        sq = c.sbpool(4, [128, 512], BF16, "sq")
        bc = 512 if KC == 16 else 256
        wpool = c.sbpool(3, [128, KC, bc], BF16, "wblk")
        psp = c.pspool(4, [128, 512], F32, "psl")
        ps1 = c.pspool(2, [128, 512], F32, "ps1")
        rsp = c.sbpool(2, [128, 512], F32, "rs")
        for (t0, w) in token_tiles(self.Tp):
            x = xin.next()
            for k8 in range(0, KC, 8):
                c.dma("sp", x[:, k8:k8 + 8, :w],
                      src[k8 * 128:(k8 + 8) * 128, t0:t0 + w].rearrange("(kc p) t -> p kc t", p=128), out_t=x)
            h32 = h32p.next()
            c.dma("sp", h32[:, :, :w], self.hT[:, t0:t0 + w].rearrange("(kc p) t -> p kc t", p=128), out_t=h32)

            def epi(oc, ps, w=w):
                c.cp("act", mix, mix[:, oc, :w], ps, ps[:, :w])

            self.linear(W, 0, KC, 0, D, bc, x, lambda kc, w=w, x=x: x[:, kc, :w], w, wpool, psp, epi)
            self.postnorm_residual(mix, w, t0, gains, l * 4 + 1, h32, sq, ps1, rsp, onesb)
        c.end_phase()

    def phase_mlp(self, l):
        c = self.c
        c.begin_phase()
        gains = self.load_const(self.gains, [128, 16, 16], F32, "gains")
        onesb = self.load_const(self.k_onesb, [128, 128], BF16, "onesb")
        h32p = c.sbpool(1, [128, 16, 512], F32, "h32")
        hn = c.sb([128, 16, 512], BF16, "hn")
        sq = c.sbpool(4, [128, 512], BF16, "sq")
        actb = c.sb([128, 16, 512], BF16, "actb")
        ff = c.sb([128, 16, 512], F32, "ff")
        wpl = c.sbpool(3, [128, 16, 512], BF16, "wblk")
        psp = c.pspool(4, [128, 512], F32, "psl")
        ps1 = c.pspool(2, [128, 512], F32, "ps1")
        rsp = c.sbpool(2, [128, 512], F32, "rs")
        rl = c.sbpool(3, [128, 512], F32, "rl")
        Wu, Wd = self.wbf["up"][l], self.wbf["down"][l]
        for (t0, w) in token_tiles(self.Tp):
            h32 = h32p.next()
            c.dma("sp", h32[:, :, :w], self.hT[:, t0:t0 + w].rearrange("(kc p) t -> p kc t", p=128), out_t=h32)
            self.rmsnorm_fm(h32, 16, w, gains, lambda kc: gains[:, l * 4 + 2, kc:kc + 1], hn, sq, ps1, rsp, onesb, D)
            for qd in range(4):
                def epi_up(oc, ps, w=w):
                    r = rl.next()
                    c.act(r, r[:, :w], ps, ps[:, :w], AF.Relu)
                    c.tt("pool", actb, actb[:, oc, :w], r, r[:, :w], r, r[:, :w], ALU.mult)

                self.linear(Wu, 0, 16, qd * 2048, qd * 2048 + 2048, 512, hn, lambda kc, w=w: hn[:, kc, :w],
                            w, wpl, psp, epi_up)

                def epi_dn(oc, ps, w=w, qd=qd):
                    if qd == 0:
                        c.cp("act", ff, ff[:, oc, :w], ps, ps[:, :w])
                    else:
                        c.tt("dve", ff, ff[:, oc, :w], ps, ps[:, :w], ff, ff[:, oc, :w], ALU.add)

                self.linear(Wd, qd * 2048, 16, 0, D, 512, actb, lambda kc, w=w: actb[:, kc, :w],
                            w, wpl, psp, epi_dn)
            self.postnorm_residual(ff, w, t0, gains, l * 4 + 3, h32, sq, ps1, rsp, onesb)
        c.end_phase()

    def rope_epi(self, ps, w, t0, cosT, sinT, RT, xsp, t1p, obp, ps1, dst_rows, scale):
        c = self.c
        xs = xsp.next()
        c.cp("act", xs, xs[:, :w], ps, ps[:, :w])
        return lambda: self.rope_tail(xs, w, t0, cosT, sinT, RT, t1p, obp, ps1, dst_rows, scale)

    def rope_tail(self, xs, w, t0, cosT, sinT, RT, t1p, obp, ps1, dst_rows, scale):
        c = self.c
        pr = ps1.next()
        c.mm(pr, pr[:, :w], RT, RT[:, :], xs, xs[:, :w])
        t1 = t1p.next()
        c.tt("dve", t1, t1[:, :w], pr, pr[:, :w], sinT, sinT[:, t0:t0 + w], ALU.mult)
        c.tt("pool", xs, xs[:, :w], xs, xs[:, :w], cosT, cosT[:, t0:t0 + w], ALU.mult)
        ob = obp.next()
        c.tt("dve", t1, t1[:, :w], t1, t1[:, :w], xs, xs[:, :w], ALU.add)
        c.ts("dve", ob, ob[:, :w], t1, t1[:, :w], scale, ALU.mult)
        c.dma("pool", dst_rows[:, t0:t0 + w], ob[:, :w], in_t=ob)

    def phase_qkproj(self, l, mode):
        c = self.c
        c.begin_phase()
        gains = self.load_const(self.gains, [128, 16, 16], F32, "gains")
        kvg = self.load_const(self.kvg, [128, 16], F32, "kvg")
        onesb = self.load_const(self.k_onesb, [128, 128], BF16, "onesb")
        RT = self.load_const(self.k_RT, [128, 128], F32, "RT")
        cosT = self.load_const(self.k_cos, [128, self.Tp], F32, "cosT")
        sinT = self.load_const(self.k_sin, [128, self.Tp], F32, "sinT")
        h32p = c.sbpool(2, [128, 16, 512], F32, "h32")
        hn = c.sb([128, 16, 512], BF16, "hn")
        sq = c.sbpool(4, [128, 512], BF16, "sq")
        wpool = c.sbpool(3, [128, 16, 512], BF16, "wblk")
        psp = c.pspool(4, [128, 512], F32, "psl")
        ps1 = c.pspool(2, [128, 512], F32, "ps1")
        rsp = c.sbpool(2, [128, 512], F32, "rs")
        xsp = c.sbpool(3, [128, 512], F32, "xs")
        t1p = c.sbpool(3, [128, 512], F32, "t1")
        obp = c.sbpool(4, [128, 512], BF16, "ob")
        if mode == "kv":
            W, dst, gt, gfn, scale = self.wbf["kv"], self.a_k, kvg, (lambda kc: kvg[:, kc:kc + 1]), 1.0
        else:
            W, dst, gt, gfn, scale = self.wbf["q"][l - 2], self.a_q, gains, (lambda kc: gains[:, l * 4, kc:kc + 1]), 128.0 ** -0.5
        for (t0, w) in token_tiles(self.Tp):
            h32 = h32p.next()
            c.dma("sp", h32[:, :, :w], self.hT[:, t0:t0 + w].rearrange("(kc p) t -> p kc t", p=128), out_t=h32)
            self.rmsnorm_fm(h32, 16, w, gt, gfn, hn, sq, ps1, rsp, onesb, D)

            def epi(oc, ps, w=w, t0=t0):
                return self.rope_epi(ps, w, t0, cosT, sinT, RT, xsp, t1p, obp, ps1, dst[oc * 128:(oc + 1) * 128, :], scale)

            self.linear(W, 0, 16, 0, D, 512, hn, lambda kc, w=w: hn[:, kc, :w], w, wpool, psp, epi)
            if mode == "kv":
                for nb in range(4):
                    wt = wpool.next()
                    c.dma("sp", wt[:, :, :], W[:, D + nb * 512:D + (nb + 1) * 512].rearrange("(kc p) n -> p kc n", p=128),
                          out_t=wt)
                    for sbk in range(w // 128):
                        ps = psp.next()
                        for kc in range(16):
                            c.mm(ps, ps[:, :], hn, hn[:, kc, sbk * 128:(sbk + 1) * 128], wt, wt[:, kc, :],
                                 start=(kc == 0), stop=(kc == 15))
                        ob = obp.next()
                        c.cp("act", ob, ob[:, :], ps, ps[:, :])
                        tt0 = t0 + sbk * 128
                        c.dma("pool", self.a_v[tt0:tt0 + 128, nb * 512:(nb + 1) * 512], ob[:, :], in_t=ob)
        c.end_phase()

    def phase_attn(self, l):
        c = self.c
        Tp, NB, T = self.Tp, self.NC, self.T
        j_ = l - 2
        lambda_init = 0.8 - 0.6 * math.exp(-0.3 * l)
        c.begin_phase()
        identb = self.load_const(self.k_identb, [128, 128], BF16, "identb")
        onesb = self.load_const(self.k_onesb, [128, 128], BF16, "onesb")
        md = self.load_const(self.k_md, [128, 4, 512], F32, "md")
        lamt = self.load_const(self.lam[:, j_], [128, 4, 128], F32, "lam")
        subg = self.load_const(self.subln[:, j_], [128, 256], F32, "subg")
        c.ts("dve", subg, subg[:, :], subg, subg[:, :], 1.0 - lambda_init, ALU.mult)
        lp = c.sb([128, 2, 128], F32, "lp")
        c.tt("dve", lp, lp[:, 0, :], lamt, lamt[:, 0, :], lamt, lamt[:, 1, :], ALU.mult)
        c.tt("dve", lp, lp[:, 1, :], lamt, lamt[:, 2, :], lamt, lamt[:, 3, :], ALU.mult)
        ls = c.sb([128, 2], F32, "ls")
        c.op("dve", lambda en: en.tensor_reduce(out=ls[:, :], in_=lp[:, :, :], axis=AX.X, op=ALU.add), [ls], [lp])
        c.act(ls, ls[:, :], ls, ls[:, :], AF.Exp)
        neglam = c.sb([128, 1], F32, "neglam")
        c.tt("dve", neglam, neglam[:, :], ls, ls[:, 1:2], ls, ls[:, 0:1], ALU.subtract)
        c.ts("dve", neglam, neglam[:, :], neglam, neglam[:, :], -lambda_init, ALU.add)

        ktp = c.sbpool(2, [128, 2, Tp], BF16, "kt")
        vxp = c.sbpool(2, [128, NB, 257], BF16, "vx")
        for t in vxp.tiles:
            c.memset("pool", t, t[:, :, 256:257], 1.0)
        qtp = c.sbpool(2, [128, 2, 512], BF16, "qt")
        sqp = c.sbpool(2, [128, 512], BF16, "sqq")
        km2 = c.sb([128, 2], F32, "km2")
        kmt = c.sbpool(2, [128, 1], F32, "kmt")
        negBp = c.sbpool(3, [128, 512], F32, "negB")
        tmpp = c.sbpool(4, [128, 512], F32, "tmp")
        ptp = c.sbpool(4, [128, 512], BF16, "pt")
        on0p = c.sbpool(2, [128, 4, 256], F32, "on0")
        on1p = c.sbpool(2, [128, 256], F32, "on1")
        junk = c.sbpool(2, [128, 256], F32, "junk")
        colp = c.sbpool(4, [128, 1], F32, "col")
        osp = c.sbpool(2, [128, 256], BF16, "osn")
        ontp = c.sbpool(2, [128, 2, 128], BF16, "ont")
        pss = c.pspool(2, [128, 512], F32, "pss")
        pso = c.pspool(4, [128, 257], F32, "pso")
        psq = c.pspool(1, [128, 512], F32, "psq")
        pst = c.pspool(1, [128, 2, 128], BF16, "pst")

        for h in range(8):
            kt = ktp.next(); vx = vxp.next()
            c.dma("sp", kt[:, :, :], self.a_k[h * 256:(h + 1) * 256, :].rearrange("(m p) t -> p m t", p=128), out_t=kt)
            for n8 in range(0, NB, 8):
                n9 = min(NB, n8 + 8)
                c.dma("sp", vx[:, n8:n9, 0:256],
                      self.a_v[n8 * 128:n9 * 128, h * 256:(h + 1) * 256].rearrange("(nb p) f -> p nb f", p=128), out_t=vx)
            for m in range(2):
                first = True
                for (t0, w) in token_tiles(Tp):
                    s2 = sqp.next()
                    c.tt("pool", s2, s2[:, :w], kt, kt[:, m, t0:t0 + w], kt, kt[:, m, t0:t0 + w], ALU.mult)
                    pq = psq.next()
                    c.mm(pq, pq[:, :w], onesb, onesb[:, :], s2, s2[:, :w])
                    if first:
                        c.op("dve", lambda en, pq=pq, w=w, m=m: en.tensor_reduce(out=km2[:, m:m + 1], in_=pq[:, :w], axis=AX.X, op=ALU.max),
                             [km2], [pq])
                        first = False
                    else:
                        k1 = kmt.next()
                        c.op("dve", lambda en, pq=pq, w=w, k1=k1: en.tensor_reduce(out=k1[:, :], in_=pq[:, :w], axis=AX.X, op=ALU.max),
                             [k1], [pq])
                        c.tt("dve", km2, km2[:, m:m + 1], km2, km2[:, m:m + 1], k1, k1[:, :], ALU.max)
            c.ts("dve", km2, km2[:, :], km2, km2[:, :], 1.05, ALU.mult)
            for qi, (t0, w) in enumerate(token_tiles(Tp)):
                if t0 >= T:
                    continue
                nqs = w // 128
                qt = qtp.next()
                c.dma("sp", qt[:, :, :w], self.a_q[h * 256:(h + 1) * 256, t0:t0 + w].rearrange("(m p) t -> p m t", p=128),
                      out_t=qt)
                on0 = on0p.next()
                for m in range(2):
                    s2 = sqp.next()
                    c.tt("pool", s2, s2[:, :w], qt, qt[:, m, :w], qt, qt[:, m, :w], ALU.mult)
                    pq = psq.next()
                    c.mm(pq, pq[:, :w], onesb, onesb[:, :], s2, s2[:, :w])
                    negB = negBp.next()
                    c.act(negB, negB[:, :w], pq, pq[:, :w], AF.Sqrt, scale=km2[:, m:m + 1], extra_in=[km2])
                    c.ts("dve", negB, negB[:, :w], negB, negB[:, :w], -1.0, ALU.mult)
                    accs = [pso.next() for _ in range(nqs)]
                    kb_last = t0 // 128 + nqs - 1
                    def emit_pv(kb, pt, t0=t0, nqs=nqs, accs=accs):
                        d = kb - t0 // 128
                        for qs in range(nqs):
                            if d > qs:
                                continue
                            c.mm(accs[qs], accs[qs][:, :], pt, pt[:, qs * 128:(qs + 1) * 128], vx, vx[:, kb, :],
                                 start=(kb == 0), stop=(d == qs))

                    inflight = []
                    for kb in range(kb_last + 1):
                        d = kb - t0 // 128
                        ps = pss.next()
                        c.mm(ps, ps[:, :w], kt, kt[:, m, kb * 128:(kb + 1) * 128], qt, qt[:, m, :w])
                        tmp = tmpp.next()
                        c.tt("dve", tmp, tmp[:, :w], ps, ps[:, :w], negB, negB[:, :w], ALU.add)
                        if d >= 0:
                            c.tt("pool", tmp, tmp[:, :w], tmp, tmp[:, :w], md, md[:, d, :w], ALU.add)
                        pt = ptp.next()
                        c.act(pt, pt[:, :w], tmp, tmp[:, :w], AF.Exp)
                        inflight.append((kb, pt))
                        if len(inflight) > 2:
                            emit_pv(*inflight.pop(0))
                    while inflight:
                        emit_pv(*inflight.pop(0))
                    for qs in range(nqs):
                        a = accs[qs]
                        rl_ = colp.next()
                        c.recip(rl_, rl_[:, :], a, a[:, 256:257])
                        if m == 0:
                            c.ts("dve", on0, on0[:, qs, :], a, a[:, 0:256], rl_[:, 0:1], ALU.mult, s_t=[rl_])
                        else:
                            on1 = on1p.next()
                            c.ts("dve", on1, on1[:, :], a, a[:, 0:256], rl_[:, 0:1], ALU.mult, s_t=[rl_])
                            c.stt("dve", on1, on1[:, :], on1, on1[:, :], neglam[:, 0:1], on0, on0[:, qs, :],
                                  ALU.mult, ALU.add, s_t=[neglam])
                            jk = junk.next(); ssq = colp.next()
                            c.memset("dve", ssq, ssq[:, :], 0.0)
                            c.act(jk, jk[:, :], on1, on1[:, :], AF.Square, accum=ssq[:, :], accum_t=ssq)
                            c.act(ssq, ssq[:, :], ssq, ssq[:, :], AF.Sqrt, bias=EPS, scale=1.0 / 256)
                            c.recip(ssq, ssq[:, :], ssq, ssq[:, :])
                            osn = osp.next()
                            c.stt("dve", osn, osn[:, :], on1, on1[:, :], ssq[:, 0:1], subg, subg[:, :],
                                  ALU.mult, ALU.mult, s_t=[ssq])
                            ptr = pst.next()
                            for e2 in range(2):
                                c.tr(ptr, ptr[:, e2, :], osn, osn[:, e2 * 128:(e2 + 1) * 128], identb, identb[:, :])
                            ont = ontp.next()
                            c.cp("act", ont, ont[:, :, :], ptr, ptr[:, :, :])
                            tq = t0 + qs * 128
                            c.dma("pool", self.a_on[h * 256:(h + 1) * 256, tq:tq + 128].rearrange("(e p) t -> p e t", p=128),
                                  ont[:, :, :], in_t=ont)
        c.end_phase()

    def phase_final(self):
        c = self.c
        c.barrier()

    def build(self, stop=None):
        self.declare()
        self.phase_init()
        for l in range(self.n_layers):
            if l < 2:
                self.phase_g1(l)
                if stop == "g1":
                    break
                self.phase_g2(l)
                if stop == "g2":
                    break
                self.phase_outproj(l, self.s_og, 32, self.wbf["gout"][l])
                if stop == "g3":
                    break
            else:
                if l == 2:
                    self.phase_qkproj(l, "kv")
                self.phase_qkproj(l, "q")
                self.phase_attn(l)
                self.phase_outproj(l, self.a_on, 16, self.wbf["o"][l - 2])
            self.phase_mlp(l)
        self.phase_final()
        return self.nc


def host_consts(Tp):
    bf = ml_dtypes.bfloat16
    p = np.arange(128)[:, None]
    f = np.arange(128)[None, :]
    cst = {}
    cst["identb"] = np.eye(128, dtype=np.float32).astype(bf)
    cst["onesb"] = np.ones((128, 128), np.float32).astype(bf)
    cst["onesf"] = np.ones((128, 128), np.float32)
    cst["identf"] = np.eye(128, dtype=np.float32)
    sel = np.zeros((32, 32, 128), np.float32)
    for hh in range(32):
        sel[hh, hh, :] = 1.0
    cst["sel"] = sel
    ii = np.arange(128)[:, None]; jj = np.arange(128)[None, :]
    m4 = np.zeros((128, 8, 4, 128), np.float32)
    m4[:, 0] = np.eye(128, dtype=np.float32)[:, None, :]
    def M(sv):
        n = 2 ** sv
        return ((ii // (2 * n) == jj // (2 * n)) & (ii % (2 * n) >= n) & (jj % (2 * n) < n)).astype(np.float32)
    m4[:, 1] = M(0).T[:, None, :]
    for sv in range(1, 7):
        m4[:, 1 + sv] = M(sv)[:, None, :]
    cst["m4"] = m4.astype(bf)
    cst["Umat"] = (p <= f).astype(np.float32)
    cst["negU"] = np.where(f >= p, 0.0, NEG).astype(np.float32)
    cst["M2"] = np.where(p > f, 0.0, NEG).astype(np.float32)
    R = np.zeros((128, 128), np.float32)
    for m in range(16):
        R[m, m + 16] = -1.0
        R[m + 16, m] = 1.0
    cst["RT"] = np.ascontiguousarray(R.T)
    pos = np.arange(Tp, dtype=np.float32)
    inv = (500000.0 ** (-np.arange(0, 32, 2, dtype=np.float32) / 32)).astype(np.float32)
    ang = pos[None, :] * inv[:, None]
    cosT = np.ones((128, Tp), np.float32)
    sinT = np.zeros((128, Tp), np.float32)
    cosT[0:16] = np.cos(ang); cosT[16:32] = np.cos(ang)
    sinT[0:16] = np.sin(ang); sinT[16:32] = np.sin(ang)
    cst["cosT"], cst["sinT"] = cosT, sinT
    md = np.zeros((128, 4, 512), np.float32)
    ff = np.arange(512)[None, :]
    for d in range(4):
        md[:, d, :] = np.where(d * 128 + p <= ff, 0.0, NEG)
    cst["maskd"] = md
    return cst


def rep(a, n=128):
    return np.ascontiguousarray(np.broadcast_to(a[None], (n,) + a.shape)).astype(np.float32)


def host_params(inp):
    ng = np.asarray(inp["norm_gains"], np.float32)
    d = {}
    d["gains"] = np.ascontiguousarray(ng.reshape(16, 16, 128).transpose(2, 0, 1))
    d["kvg"] = np.ascontiguousarray(np.asarray(inp["kv_norm"], np.float32).reshape(16, 128).T)
    cw = np.asarray(inp["gdn_conv_w"], np.float32)
    d["convw"] = np.ascontiguousarray(cw.reshape(2, 4, 64, 128).transpose(3, 0, 1, 2))
    d["alog"] = rep(np.asarray(inp["gdn_a_log"], np.float32))
    d["dtb"] = rep(np.asarray(inp["gdn_dt_bias"], np.float32))
    d["onorm"] = np.ascontiguousarray(np.asarray(inp["gdn_o_norm"], np.float32).T)
    d["lam"] = rep(np.asarray(inp["diff_lambda"], np.float32))
    d["subln"] = rep(np.asarray(inp["diff_subln"], np.float32))
    return d


_CACHE = {}


def kernel(**inputs):
    x = np.asarray(inputs["x"], np.float32)
    B, S, _ = x.shape
    T = NMETA + S
    Tp = ((T + 127) // 128) * 128
    key = (T,)
    if key not in _CACHE:
        _CACHE[key] = Builder(T).build()
    nc = _CACHE[key]
    cst = host_consts(Tp)
    prm = host_params(inputs)
    meta = np.asarray(inputs["meta_tokens"], np.float32)
    shared = dict(cst)
    shared.update(prm)
    for k in ("mlp_w_up", "mlp_w_down", "gdn_w_in", "gdn_w_out", "w_kv", "diff_w_q", "diff_w_o"):
        shared[k] = np.ascontiguousarray(np.asarray(inputs[k], np.float32))
    in_maps = []
    for core in range(8):
        b = core % B
        h0 = np.zeros((D, Tp), np.float32)
        h0[:, :NMETA] = meta.T
        h0[:, NMETA:T] = x[b].T
        m = dict(shared)
        m["h0"] = h0
        in_maps.append(m)
    res = run_bass_kernel_spmd(nc, in_maps, core_ids=list(range(8)))
    out = np.empty((B, S, D), np.float32)
    for b in range(B):
        hT = res.results[b]["hT"]
        out[b] = hT[:, NMETA:T].T
    return out
```
